# Optimizing a Trainium2 kernel written in Bass

```python
import math
import jax
import jax.numpy as jnp
from jax import lax
import numpy as np

D_MODEL = 2048
BATCH = 2
SEQ = 8192
DEPTH = 1
DEC_BATCH = 32
DEC_SEQ = 4
PAST_LEN = 16384
PAGE_SIZE = 128

D_MIX = D_MODEL
D_ATTN = D_MIX // 2
D_SSM = D_MIX - D_ATTN
HEAD_DIM = 64
N_HEADS = D_ATTN // HEAD_DIM
SSM_CH = 16
SSM_GROUPS = D_SSM // SSM_CH
SSM_STATE = 64
DILATED = ((128, 1), (512, 4), (2048, 16))
W_MAX = 2048
BLK = 128
NUM_BUCKETS = 32
MAX_DISTANCE = W_MAX
ATTN_SCALE = HEAD_DIM ** -0.5
LN_EPS = 1e-5
DN_ALPHA = (2 * DEPTH) ** 0.25
DN_BETA = (8 * DEPTH) ** -0.25
N_PROJ = 4 * D_ATTN + 2 * D_SSM

kernel_name = 'hymba_longnet_s5_deepnorm_step'


def t5_bucket(dist):
    exact = NUM_BUCKETS // 2
    d_f = jnp.maximum(dist, 1).astype(jnp.float32)
    large = exact + (jnp.log(d_f / exact) / math.log(MAX_DISTANCE / exact)
                     * (NUM_BUCKETS - exact)).astype(jnp.int32)
    large = jnp.minimum(large, NUM_BUCKETS - 1)
    return jnp.where(dist < exact, dist, large)


def pattern_bias(rel_bias, dil, kper):
    dist = jnp.arange(kper + 1, dtype=jnp.int32) * dil
    return rel_bias[t5_bucket(dist)].astype(jnp.float32)


def band_dilated_attention(q, k, v, bias_k, dil, kper):
    b, l, h, e = q.shape
    span = BLK * dil
    lp = -(-l // span) * span
    nb = lp // span
    padw = ((0, 0), (0, lp - l), (0, 0), (0, 0))

    def blocks(t):
        return jnp.pad(t, padw).reshape(b, nb, BLK, dil, h, e)

    def with_prev(t):
        prev = jnp.pad(t, ((0, 0), (1, 0), (0, 0), (0, 0), (0, 0), (0, 0)))[:, :-1]
        return jnp.concatenate([prev, t], axis=2)

    qb = blocks(q)
    kb = with_prev(blocks(k))
    vb = with_prev(blocks(v))
    s = jnp.einsum('bnqrhe,bnkrhe->bnrhqk', qb, kb,
                   preferred_element_type=jnp.float32) * ATTN_SCALE
    rel = jnp.arange(BLK)[:, None] + BLK - jnp.arange(2 * BLK)[None, :]
    band = (rel >= 0) & (rel <= kper)
    in_cur = jnp.arange(2 * BLK) >= BLK
    valid = band[None] & ((jnp.arange(nb) > 0)[:, None, None] | in_cur[None, None, :])
    bias = jnp.transpose(bias_k[jnp.clip(rel, 0, kper)], (2, 0, 1))
    s = jnp.where(valid[None, :, None, None], s + bias, -jnp.inf)
    lse = jax.nn.logsumexp(s, axis=-1)
    p = jnp.exp(s - lse[..., None])
    o = jnp.einsum('bnrhqk,bnkrhe->bnqrhe', p, vb.astype(jnp.float32))
    o = o.reshape(b, lp, h, e)[:, :l]
    lse = jnp.transpose(lse, (0, 1, 4, 2, 3)).reshape(b, lp, h)[:, :l]
    return o, lse


def cached_dilated_attention(q, k_all, v_all, bias_k, dil, kper, buf):
    s_len = q.shape[1]
    idx = buf + jnp.arange(s_len)[:, None] - jnp.arange(kper + 1)[None, :] * dil
    valid = idx >= 0
    idx = jnp.maximum(idx, 0)
    kg = jnp.take(k_all, idx, axis=1)
    vg = jnp.take(v_all, idx, axis=1)
    s = jnp.einsum('bshe,bskhe->bhsk', q, kg, preferred_element_type=jnp.float32) * ATTN_SCALE
    s = jnp.where(valid[None, None], s + bias_k.T[None, :, None, :], -jnp.inf)
    lse = jax.nn.logsumexp(s, axis=-1)
    p = jnp.exp(s - lse[..., None])
    o = jnp.einsum('bhsk,bskhe->bshe', p, vg.astype(jnp.float32))
    return o, jnp.transpose(lse, (0, 2, 1))


def merge_by_denominator(outs, lses):
    w = jax.nn.softmax(jnp.stack(lses), axis=0)
    return jnp.sum(w[..., None] * jnp.stack(outs), axis=0)


def prompt_attention(q, k, v, rel_bias):
    outs, lses = [], []
    for win, dil in DILATED:
        kper = win // dil
        o, l = band_dilated_attention(q, k, v, pattern_bias(rel_bias, dil, kper), dil, kper)
        outs.append(o)
        lses.append(l)
    return merge_by_denominator(outs, lses)


def sample_attention(q, k, v, cache_k, cache_v, rel_bias):
    buf = cache_k.shape[1]
    k_all = jnp.concatenate([cache_k, k.astype(cache_k.dtype)], axis=1)
    v_all = jnp.concatenate([cache_v, v.astype(cache_v.dtype)], axis=1)
    outs, lses = [], []
    for win, dil in DILATED:
        kper = win // dil
        o, l = cached_dilated_attention(q, k_all, v_all, pattern_bias(rel_bias, dil, kper), dil, kper, buf)
        outs.append(o)
        lses.append(l)
    return merge_by_denominator(outs, lses)


def s5_ssm(u, x0_re, x0_im, lam_re, lam_im, log_dt, b_re, b_im, c_re, c_im, d_skip):
    bsz, l, _ = u.shape
    f32 = jnp.float32
    u = u.astype(f32).reshape(bsz, l, SSM_GROUPS, SSM_CH)
    lr = jnp.minimum(lam_re.astype(f32), -1e-4)
    li = lam_im.astype(f32)
    dt = jnp.exp(log_dt.astype(f32))[:, None]
    mag = jnp.exp(lr * dt)
    ab_re, ab_im = mag * jnp.cos(li * dt), mag * jnp.sin(li * dt)
    den = lr * lr + li * li
    inv_re, inv_im = lr / den, -li / den
    n_re, n_im = ab_re - 1.0, ab_im
    cf_re = n_re * inv_re - n_im * inv_im
    cf_im = n_re * inv_im + n_im * inv_re
    br, bi = b_re.astype(f32), b_im.astype(f32)
    bb_re = cf_re[..., None] * br - cf_im[..., None] * bi
    bb_im = cf_re[..., None] * bi + cf_im[..., None] * br
    bu_re = jnp.einsum('blgc,gnc->lbgn', u, bb_re)
    bu_im = jnp.einsum('blgc,gnc->lbgn', u, bb_im)
    bu_re = bu_re.at[0].add(ab_re * x0_re - ab_im * x0_im)
    bu_im = bu_im.at[0].add(ab_re * x0_im + ab_im * x0_re)
    a_re = jnp.broadcast_to(ab_re, (l, 1, SSM_GROUPS, SSM_STATE))
    a_im = jnp.broadcast_to(ab_im, (l, 1, SSM_GROUPS, SSM_STATE))

    def combine(e1, e2):
        a1r, a1i, b1r, b1i = e1
        a2r, a2i, b2r, b2i = e2
        return (a2r * a1r - a2i * a1i, a2r * a1i + a2i * a1r,
                a2r * b1r - a2i * b1i + b2r, a2r * b1i + a2i * b1r + b2i)

    _, _, xr, xi = lax.associative_scan(combine, (a_re, a_im, bu_re, bu_im), axis=0)
    y = (jnp.einsum('lbgn,gcn->blgc', xr, c_re.astype(f32))
         - jnp.einsum('lbgn,gcn->blgc', xi, c_im.astype(f32)))
    y = y + d_skip.astype(f32).reshape(SSM_GROUPS, SSM_CH) * u
    return y.reshape(bsz, l, D_SSM), xr[-1], xi[-1]


def layer_norm(x, g, b):
    xf = x.astype(jnp.float32)
    mu = jnp.mean(xf, axis=-1, keepdims=True)
    var = jnp.mean(jnp.square(xf - mu), axis=-1, keepdims=True)
    return ((xf - mu) * lax.rsqrt(var + LN_EPS) * g.astype(jnp.float32)
            + b.astype(jnp.float32)).astype(x.dtype)


def mixer_layer(x, attend, x0_re, x0_im, w_in, w_out, b_out, lam_re, lam_im, log_dt,
                b_re, b_im, c_re, c_im, d_skip, w_glu, b_glu, ln_g, ln_b):
    bsz, l, _ = x.shape
    h = jnp.einsum('bld,df->blf', x, w_in)
    q, k, v, g_attn, u, g_ssm = jnp.split(
        h, [D_ATTN, 2 * D_ATTN, 3 * D_ATTN, 4 * D_ATTN, 4 * D_ATTN + D_SSM], axis=-1)
    heads = lambda t: t.reshape(bsz, l, N_HEADS, HEAD_DIM)
    k_h, v_h = heads(k), heads(v)
    attn = attend(heads(q), k_h, v_h).reshape(bsz, l, D_ATTN).astype(x.dtype)
    y, s_re, s_im = s5_ssm(u, x0_re, x0_im, lam_re, lam_im, log_dt, b_re, b_im, c_re, c_im, d_skip)
    z = jax.nn.gelu(y)
    z = (z * jax.nn.sigmoid(z @ w_glu.astype(jnp.float32) + b_glu.astype(jnp.float32))).astype(x.dtype)
    branches = jnp.concatenate([attn * jax.nn.silu(g_attn), z * jax.nn.silu(g_ssm)], axis=-1)
    mix = jnp.einsum('blf,fd->bld', branches, w_out) + b_out
    out = layer_norm(DN_ALPHA * x + mix, ln_g, ln_b)
    return out, k_h, v_h, s_re, s_im


def setup_inputs(seed: int = 0) -> dict:
    key = jax.random.key(seed)
    ks = jax.random.split(key, 24)
    nrm = lambda k, shape, s: jax.random.normal(k, shape, jnp.float32) * s
    cache_len = min(W_MAX, PAST_LEN)
    g, n = SSM_GROUPS, SSM_STATE
    lam_im0 = jnp.pi * jnp.arange(n, dtype=jnp.float32)
    return {
        'x_prompt': nrm(ks[0], (BATCH, SEQ, D_MODEL), 1.0),
        'x_sample': nrm(ks[1], (DEC_BATCH, DEC_SEQ, D_MODEL), 1.0),
        'cache_k': nrm(ks[2], (DEPTH, DEC_BATCH, cache_len, N_HEADS, HEAD_DIM), 1.0),
        'cache_v': nrm(ks[3], (DEPTH, DEC_BATCH, cache_len, N_HEADS, HEAD_DIM), 1.0),
        'state_ssm_re': nrm(ks[4], (DEPTH, DEC_BATCH, g, n), 0.1),
        'state_ssm_im': nrm(ks[5], (DEPTH, DEC_BATCH, g, n), 0.1),
        'w_in': nrm(ks[6], (DEPTH, D_MODEL, N_PROJ), D_MODEL ** -0.5),
        'w_out': nrm(ks[7], (DEPTH, D_MIX, D_MODEL), D_MIX ** -0.5 * DN_BETA),
        'b_out': nrm(ks[8], (DEPTH, D_MODEL), 0.01),
        'rel_bias': nrm(ks[9], (NUM_BUCKETS, N_HEADS), 0.5),
        'lam_re': -0.5 + nrm(ks[10], (DEPTH, g, n), 0.01),
        'lam_im': lam_im0 + nrm(ks[11], (DEPTH, g, n), 0.01),
        'log_dt': jax.random.uniform(ks[12], (DEPTH, g), jnp.float32,
                                     minval=math.log(1e-3), maxval=math.log(1e-1)),
        'b_re': nrm(ks[13], (DEPTH, g, n, SSM_CH), (2 * SSM_CH) ** -0.5),
        'b_im': nrm(ks[14], (DEPTH, g, n, SSM_CH), (2 * SSM_CH) ** -0.5),
        'c_re': nrm(ks[15], (DEPTH, g, SSM_CH, n), n ** -0.5),
        'c_im': nrm(ks[16], (DEPTH, g, SSM_CH, n), n ** -0.5),
        'd_skip': nrm(ks[17], (DEPTH, D_SSM), 1.0),
        'w_glu': nrm(ks[18], (DEPTH, D_SSM, D_SSM), D_SSM ** -0.5),
        'b_glu': nrm(ks[19], (DEPTH, D_SSM), 0.01),
        'ln_g': 1.0 + nrm(ks[20], (DEPTH, D_MODEL), 0.01),
        'ln_b': nrm(ks[21], (DEPTH, D_MODEL), 0.01),
    }


def reference(x_prompt, x_sample, cache_k, cache_v, state_ssm_re, state_ssm_im, w_in, w_out, b_out,
              rel_bias, lam_re, lam_im, log_dt, b_re, b_im, c_re, c_im, d_skip, w_glu, b_glu, ln_g, ln_b):
    xp, xs = x_prompt, x_sample
    keep = min(W_MAX, x_prompt.shape[1])
    kp_l, vp_l, srp_l, sip_l = [], [], [], []
    ks_l, vs_l, srs_l, sis_l = [], [], [], []
    for layer in range(DEPTH):
        params = (w_in[layer], w_out[layer], b_out[layer], lam_re[layer], lam_im[layer], log_dt[layer],
                  b_re[layer], b_im[layer], c_re[layer], c_im[layer], d_skip[layer], w_glu[layer],
                  b_glu[layer], ln_g[layer], ln_b[layer])
        zeros = jnp.zeros((xp.shape[0], SSM_GROUPS, SSM_STATE), jnp.float32)
        xp, kp, vp, srp, sip = mixer_layer(
            xp, lambda q, k, v: prompt_attention(q, k, v, rel_bias), zeros, zeros, *params)
        ck, cv = cache_k[layer], cache_v[layer]
        xs, kn, vn, srs, sis = mixer_layer(
            xs, lambda q, k, v, ck=ck, cv=cv: sample_attention(q, k, v, ck, cv, rel_bias),
            state_ssm_re[layer].astype(jnp.float32), state_ssm_im[layer].astype(jnp.float32), *params)
        kp_l.append(kp[:, -keep:])
        vp_l.append(vp[:, -keep:])
        srp_l.append(srp)
        sip_l.append(sip)
        ks_l.append(kn)
        vs_l.append(vn)
        srs_l.append(srs)
        sis_l.append(sis)
    return (xp, xs, jnp.stack(kp_l), jnp.stack(vp_l), jnp.stack(srp_l), jnp.stack(sip_l),
            jnp.stack(ks_l), jnp.stack(vs_l), jnp.stack(srs_l), jnp.stack(sis_l))
```

```python
import math
import numpy as np
import ml_dtypes
import concourse.bass as bass
import concourse.mybir as mybir
from concourse.bass_utils import run_bass_kernel_spmd

F32 = mybir.dt.float32
BF16 = mybir.dt.bfloat16
AF = mybir.ActivationFunctionType
ALU = mybir.AluOpType

D = 2048
TC = 2048
NPRE = 3
NS = 16
NPROJ = 6144
NH = 16
HD = 64
CL = 2048
SCALE = HD ** -0.5
DN_ALPHA = 2.0 ** 0.25
LN_EPS = 1e-5
import os
DEBUG = bool(int(os.environ.get("KDEBUG", "0")))
DBGSET = [x for x in os.environ.get("KDBGSET", "").split(",") if x]


class R:
    __slots__ = ("name", "writer", "readers")

    def __init__(self, name):
        self.name = name
        self.writer = None
        self.readers = []


class Op:
    __slots__ = ("eng", "fn", "deps", "signal", "is_dma", "sem", "val", "prev_same_sem")

    def __init__(self, eng, fn, is_dma):
        self.eng = eng
        self.fn = fn
        self.deps = []
        self.signal = False
        self.is_dma = is_dma
        self.sem = None
        self.val = 0
        self.prev_same_sem = None


class Sched:
    ENGS = ("pe", "act", "dve", "pool", "sp")

    def __init__(self):
        self.ops = {e: [] for e in self.ENGS}
        self.all_dma = []

    def op(self, eng, fn, reads=(), writes=(), dma=False):
        o = Op(eng, fn, dma)
        deps = []
        for r in reads:
            if r.writer is not None:
                deps.append(r.writer)
        for w in writes:
            if w.writer is not None:
                deps.append(w.writer)
            deps.extend(w.readers)
        seen = set()
        for d in deps:
            if d is o or id(d) in seen:
                continue
            seen.add(id(d))
            if d.eng == "pe" and eng == "pe" and not d.is_dma and not dma:
                continue
            d.signal = True
            o.deps.append(d)
        for r in reads:
            if not dma:
                r.readers = [x for x in r.readers if x.is_dma or x.eng != eng]
            r.readers.append(o)
        for w in writes:
            w.writer = o
            w.readers = []
        if dma:
            o.signal = True
            self.all_dma.append(o)
        self.ops[eng].append(o)
        return o

    def emit(self, nc, block, eng_sems, dma_sems):
        NDS = {e: len(dma_sems[e]) for e in dma_sems}
        for e in self.ENGS:
            cnt = 0
            dcnt = 0
            last_on_sem = {}
            for o in self.ops[e]:
                if o.is_dma:
                    k = dcnt % NDS[e]
                    dcnt += 1
                    o.sem = dma_sems[e][k]
                    o.prev_same_sem = last_on_sem.get(k)
                    o.val = (o.prev_same_sem.val if o.prev_same_sem is not None else 0) + 16
                    last_on_sem[k] = o
                elif o.signal:
                    cnt += 1
                    o.sem = eng_sems[e]
                    o.val = cnt
        handles = {"pe": nc.tensor, "act": nc.scalar, "dve": nc.vector, "pool": nc.gpsimd, "sp": nc.sync}
        final_dma = {}
        for o in self.all_dma:
            final_dma[id(o.sem)] = (o.sem, max(o.val, final_dma.get(id(o.sem), (None, 0))[1]))

        def run(e):
            h = handles[e]
            waited = {}

            def wait(sem, val):
                if waited.get(id(sem), 0) >= val:
                    return
                waited[id(sem)] = val
                h.wait_ge(sem, val)

            for o in self.ops[e]:
                if o.is_dma and o.prev_same_sem is not None:
                    wait(o.prev_same_sem.sem, o.prev_same_sem.val)
                for d in o.deps:
                    wait(d.sem, d.val)
                inst = o.fn(h)
                if o.signal:
                    inst.then_inc(o.sem, 16 if o.is_dma else 1)
            if e == "sp":
                for sem, val in final_dma.values():
                    wait(sem, val)

        block.sync(lambda _e: run("sp"))
        block.tensor(lambda _e: run("pe"))
        block.scalar(lambda _e: run("act"))
        block.vector(lambda _e: run("dve"))
        block.gpsimd(lambda _e: run("pool"))


PAGE = 512
SB_BYTES = 206 * 1024


class Buf:
    def __init__(self, mem, off, nbytes, req=None):
        self.mem = mem
        self.off = off
        self.nbytes = nbytes
        self.req = nbytes if req is None else req
        self.res = mem.pages[off // PAGE:(off + nbytes + PAGE - 1) // PAGE]

    def ap(self, dtype, np_=128):
        return self.mem.big[0:np_, self.off:self.off + self.req].bitcast(dtype)

    def sub(self, b0, b1):
        return Buf(self.mem, self.off + b0, b1 - b0)


class Mem:
    def __init__(self, big):
        self.big = big
        self.pages = [R("pg%d" % i) for i in range(SB_BYTES // PAGE)]
        self.top = 0
        self.peak = 0

    def alloc(self, nbytes):
        nb = (nbytes + PAGE - 1) // PAGE * PAGE
        assert self.top + nb <= SB_BYTES, ("SBUF overflow", self.top, nb)
        b = Buf(self, self.top, nb, nbytes)
        self.top += nb
        self.peak = max(self.peak, self.top)
        return b

    def mark(self):
        return self.top

    def release(self, m):
        self.top = m


def res_of(items):
    out = []
    for it in items:
        if isinstance(it, R):
            out.append(it)
        elif isinstance(it, Buf):
            out.extend(it.res)
        else:
            out.extend(res_of(it))
    return out


class Prog:
    def __init__(self):
        self.nc = bass.Bass("TRN2", target_bir_lowering=False)
        self.S = Sched()
        self.din = {}
        self.dout = {}
        self.rdram = {}

    def inp(self, name, shape, dtype=F32):
        t = self.nc.dram_tensor(name, list(shape), dtype, kind="ExternalInput").ap()
        self.din[name] = t
        self.rdram[name] = R(name)
        return t

    def outp(self, name, shape, dtype=F32):
        t = self.nc.dram_tensor(name, list(shape), dtype, kind="ExternalOutput").ap()
        self.dout[name] = t
        self.rdram[name] = R(name)
        return t

    def scratch(self, name, shape, dtype):
        t = self.nc.dram_tensor(name, list(shape), dtype, kind="Internal").ap()
        self.rdram[name] = R(name)
        return t

    def dbg(self, name, ap, shape, dtype, reads):
        if not DEBUG:
            return
        if DBGSET and name not in DBGSET:
            return
        t = self.nc.dram_tensor("dbg_" + name, list(shape), dtype, kind="ExternalOutput").ap()
        self.rdram["dbg_" + name] = R("dbg_" + name)
        self.dma("sp", t, ap, reads, [self.rdram["dbg_" + name]], slow=True)

    def dma(self, eng, out, in_, reads, writes, slow=False):
        def fn(h):
            if slow:
                return h.dma_start(out=out, in_=in_, allow_slow_non_contiguous=True)
            return h.dma_start(out=out, in_=in_)
        return self.S.op(eng, fn, res_of(reads), res_of(writes), dma=True)

    def mm(self, items, reads, writes):
        def fn(h):
            inst = None
            for (o, l, r, st, sp) in items:
                inst = h.matmul(o, lhsT=l, rhs=r, start=st, stop=sp)
            return inst
        return self.S.op("pe", fn, res_of(reads), res_of(writes))

    def tr(self, items, reads, writes):
        def fn(h):
            inst = None
            for (o, i, idn) in items:
                inst = h.transpose(out=o, in_=i, identity=idn)
            return inst
        return self.S.op("pe", fn, res_of(reads), res_of(writes))

    def act(self, out, in_, func, reads, writes, scale=None, bias=None, eng="act"):
        def fn(h):
            kw = {}
            if scale is not None:
                kw["scale"] = scale
            if bias is not None:
                kw["bias"] = bias
            return h.activation(out=out, in_=in_, func=func, **kw)
        return self.S.op(eng, fn, res_of(reads), res_of(writes))

    def copy(self, eng, out, in_, reads, writes):
        if eng == "act":
            return self.act(out, in_, AF.Copy, reads, writes)

        def fn(h):
            return h.tensor_copy(out=out, in_=in_)
        return self.S.op(eng, fn, res_of(reads), res_of(writes))

    def tt(self, eng, out, in0, in1, op, reads, writes):
        def fn(h):
            return h.tensor_tensor(out=out, in0=in0, in1=in1, op=op)
        return self.S.op(eng, fn, res_of(reads), res_of(writes))

    def ts(self, eng, out, in0, s1, s2, op0, op1, reads, writes):
        def fn(h):
            if op1 is None:
                return h.tensor_scalar(out=out, in0=in0, scalar1=s1, scalar2=None, op0=op0)
            return h.tensor_scalar(out=out, in0=in0, scalar1=s1, scalar2=s2, op0=op0, op1=op1)
        return self.S.op(eng, fn, res_of(reads), res_of(writes))

    def stt(self, out, in0, scalar, in1, op0, op1, reads, writes):
        def fn(h):
            return h.scalar_tensor_tensor(out=out, in0=in0, scalar=scalar, in1=in1, op0=op0, op1=op1)
        return self.S.op("dve", fn, res_of(reads), res_of(writes))

    def memset(self, eng, ap, val, writes):
        def fn(h):
            return h.memset(ap, val)
        return self.S.op(eng, fn, [], res_of(writes))

    def recip(self, out, in_, reads, writes):
        def fn(h):
            return h.reciprocal(out=out, in_=in_)
        return self.S.op("dve", fn, res_of(reads), res_of(writes))


U8 = mybir.dt.uint8


class PSBank:
    def __init__(self, ap, r):
        self.ap = ap
        self.r = r


def build_program(stage=99, npre_tiles=None):
    from contextlib import ExitStack
    P = Prog()
    nc = P.nc
    rd = P.rdram
    xo = P.inp("xo", [TC, D]); xh = P.inp("xh", [TC, D]); xp = P.inp("xp", [NPRE * TC, D]); xs = P.inp("xs", [NS, D])
    w_in = P.inp("w_in", [D, NPROJ]); w_out = P.inp("w_out", [D, D]); w_glu = P.inp("w_glu", [1024, 1024])
    ident = P.inp("ident", [128, 128])
    k_o = P.outp("k_o", [TC, 1024]); v_o = P.outp("v_o", [TC, 1024])
    k_s = P.outp("k_s", [NS, 1024]); v_s = P.outp("v_s", [NS, 1024])
    y_o = P.outp("y_o", [TC, D]); y_s = P.outp("y_s", [NS, D])
    sre_o = P.outp("sre_o", [64, 64]); sim_o = P.outp("sim_o", [64, 64])
    sre_s = P.outp("sre_s", [4, 64, 64]); sim_s = P.outp("sim_s", [4, 64, 64])
    lam_re = P.inp("lam_re", [64, 64]); lam_im = P.inp("lam_im", [64, 64]); log_dt = P.inp("log_dt", [1, 64])
    b_re = P.inp("b_re", [64, 64, 16]); b_im = P.inp("b_im", [64, 64, 16])
    c_re = P.inp("c_re", [64, 16, 64]); c_im = P.inp("c_im", [64, 16, 64])
    d_skip = P.inp("d_skip", [1024, 1]); b_glu = P.inp("b_glu", [1024, 1])
    st_re = P.inp("st_re", [4, 64, 64]); st_im = P.inp("st_im", [4, 64, 64])
    cmask = P.inp("cmask", [128, 512]); ctri = P.inp("ctri", [128, 128]); ctris = P.inp("ctris", [16, 16])
    UT = P.scratch("UT", [1024, TC], BF16); GST = P.scratch("GST", [1024, TC], BF16)
    BR = P.scratch("BR", [2048, TC], BF16)
    GV = P.scratch("GV", [48, 384], F32)
    BRSD = P.scratch("BRSD", [1024, NS], F32)
    ck = P.inp("ck", [4, CL, 1024]); cv = P.inp("cv", [4, CL, 1024]); ccnt = P.inp("ccnt", [32, 32 * 128])
    rel_bias = P.inp("rel_bias", [32, 16]); coh = P.inp("coh", [32, 3 * 384]); cj = P.inp("cj", [128, 128]); hv = P.inp("hv", [128, 1])
    b_out = P.inp("b_out", [1, D]); ln_g = P.inp("ln_g", [1, D]); ln_b = P.inp("ln_b", [1, D])
    QT = P.scratch("QT", [1024, TC], BF16); GT = P.scratch("GT", [1024, TC], BF16)
    KT = P.scratch("KT", [1024, 2 * TC], BF16); VS = P.scratch("VS", [2 * TC, 1024], BF16)

    es = ExitStack()
    with es:
        big = es.enter_context(nc.sbuf_tensor("big", [128, SB_BYTES], U8))
        banks = []
        for i in range(8):
            t = es.enter_context(nc.psum_tensor("ps%d" % i, [128, 512], F32))
            banks.append(PSBank(t[:, :], R("ps%d" % i)))
        eng_sems = {e: es.enter_context(nc.semaphore("sem_" + e)) for e in Sched.ENGS}
        dma_sems = {e: [es.enter_context(nc.semaphore("dsem_%s%d" % (e, i))) for i in range(n)]
                    for e, n in (("sp", 24), ("act", 8), ("pool", 16), ("dve", 2), ("pe", 2))}
        M = Mem(big)
        psi = [0]

        def nextps():
            b = banks[psi[0] % 8]
            psi[0] += 1
            return b

        identb = M.alloc(128 * 4)
        identF = identb.ap(F32)
        P.dma("sp", identF, ident[:, :], [rd["ident"]], [identb])
        HS = M.alloc(48 * NS * 4)
        HS3 = HS.ap(F32).rearrange("p (f t) -> p f t", t=NS)
        XTs = M.alloc(16 * NS * 2)
        XTs3 = XTs.ap(BF16).rearrange("p (k t) -> p k t", t=NS)
        VSs = M.alloc(1024 * 2)
        BRS = M.alloc(16 * NS * 2)

        def sl_ap0(t, off=0):
            return bass.AP(t.tensor, off, [[1, 128], [128, 32]])
        kin = M.alloc(3 * 128); kina = kin.ap(F32)
        bin_ = [M.alloc(32 * 16 * 4), M.alloc(32 * 16 * 4)]
        x0in = [M.alloc(32 * 4 * 4), M.alloc(32 * 4 * 4)]
        dskb = M.alloc(32); bglb = M.alloc(32)

        def early_prefetch():
            P.dma("act", kina[:, 0:32], sl_ap0(lam_re), [rd["lam_re"]], [kin], slow=True)
            P.dma("act", kina[:, 32:64], sl_ap0(lam_im), [rd["lam_im"]], [kin], slow=True)
            for g2 in range(2):
                P.dma("act", kina[g2 * 64:(g2 + 1) * 64, 64:96], bass.AP(log_dt.tensor, g2, [[0, 64], [2, 32]]), [rd["log_dt"]], [kin], slow=True)
            for i, (nm, t_) in enumerate((("b_re", b_re), ("b_im", b_im))):
                P.dma("act", bin_[i].ap(F32).rearrange("p (a c) -> p a c", c=16), bass.AP(t_.tensor, 0, [[16, 128], [2048, 32], [1, 16]]),
                      [rd[nm]], [bin_[i]], slow=True)
            for ri, (nm, t_) in enumerate((("st_re", st_re), ("st_im", st_im))):
                for b in range(4):
                    P.dma("act", x0in[ri].ap(F32).rearrange("p (a b) -> p a b", b=4)[:, :, b], sl_ap0(t_, b * 4096), [rd[nm]], [x0in[ri]], slow=True)
            P.dma("act", dskb.ap(F32), d_skip.rearrange("(o p) one -> p (o one)", p=128), [rd["d_skip"]], [dskb], slow=True)
            P.dma("act", bglb.ap(F32), b_glu.rearrange("(o p) one -> p (o one)", p=128), [rd["b_glu"]], [bglb], slow=True)

        def load_xT(src, src_r, ntiles, XT, xst):
            XT4 = XT.ap(BF16).rearrange("p (t k c) -> p t k c", k=16, c=128)
            for tt in range(ntiles):
                xb = xst[tt % 2]
                xa = xb.ap(F32)
                P.dma("sp", xa, src[tt * 128:(tt + 1) * 128, :], [src_r], [xb])
                for q in range(4):
                    ps = nextps()
                    P.tr([(ps.ap[:, j * 128:(j + 1) * 128], xa[:, (4 * q + j) * 128:(4 * q + j + 1) * 128], identF)
                          for j in range(4)], [xb, identb], [ps.r])
                    P.copy("act" if q % 2 == 0 else "dve", XT4[:, tt, 4 * q:4 * q + 4, :],
                           ps.ap.rearrange("p (a b) -> p a b", a=4), [ps.r], [XT.sub(tt * 4096 + q * 1024, tt * 4096 + (q + 1) * 1024)])
            return XT4

        def load_wtile(wb, c0, ncols):
            w3 = wb.ap(BF16).rearrange("p (k f) -> p k f", k=16)
            P.dma("pool", w3, w_in[:, c0:c0 + ncols].rearrange("(k p) f -> p k f", p=128), [rd["w_in"]], [wb])
            return w3

        def mm_fm(ps, w3, wb, XT4, XT, tb):
            P.mm([(ps.ap[:, :], w3[:, kt, :], XT4[:, 4 * tb:4 * tb + 4, kt, :], kt == 0, kt == 15) for kt in range(16)],
                 [XT.sub(tb * 16384, (tb + 1) * 16384), wb], [ps.r])

        def mm_fm_sample(ps, w3, wb):
            P.mm([(ps.ap[:, 0:NS], w3[:, kt, :], XTs3[:, kt, :], kt == 0, kt == 15) for kt in range(16)],
                 [XTs, wb], [ps.r])

        m0 = M.mark()
        xsb = M.alloc(D * 4)
        xsa = xsb.ap(F32)
        P.dma("sp", xsa[0:NS, :], xs[:, :], [rd["xs"]], [xsb])
        ps = nextps()
        P.tr([(ps.ap[:, kt * NS:(kt + 1) * NS], xsa[0:NS, kt * 128:(kt + 1) * 128], identF[0:NS, 0:NS]) for kt in range(16)],
             [xsb, identb], [ps.r])
        P.copy("dve", XTs3[:, :, :], ps.ap[:, 0:16 * NS].rearrange("p (k t) -> p k t", t=NS), [ps.r], [XTs])
        M.release(m0)

        mB1 = M.mark()
        XT = M.alloc(16 * TC * 2)
        xst = [M.alloc(D * 4), M.alloc(D * 4)]
        wt = [M.alloc(16 * 128 * 2), M.alloc(16 * 128 * 2)]
        wblk = [M.alloc(16 * 512 * 2), M.alloc(16 * 512 * 2)]
        stg = [M.alloc(TC * 2), M.alloc(TC * 2)]
        ost = [M.alloc(512 * 4), M.alloc(512 * 4)]
        vbs = [M.alloc(512 * 2), M.alloc(512 * 2)]
        wi = [0]
        evq = [0]

        def ev_eng():
            evq[0] += 1
            return "act" if evq[0] % 2 == 0 else "dve"

        for grp in ("halo", "own"):
            src, src_r = (xh, rd["xh"]) if grp == "halo" else (xo, rd["xo"])
            XT4 = load_xT(src, src_r, 16, XT, xst)
            tok0 = 0 if grp == "halo" else TC
            fts = [("k", 8 + i) for i in range(8)]
            if grp == "own":
                fts = ([("q", i) for i in range(8)] + fts + [("g", 24 + i) for i in range(8)]
                       + [("u", 32 + i) for i in range(8)] + [("gs", 40 + i) for i in range(8)])
            for (kind, ft) in fts:
                wb = wt[wi[0] % 2]
                w3 = load_wtile(wb, ft * 128, 128)
                sb = stg[wi[0] % 2]
                wi[0] += 1
                sa = sb.ap(BF16)
                for tb in range(4):
                    ps = nextps()
                    mm_fm(ps, w3, wb, XT4, XT, tb)
                    P.copy(ev_eng(), sa[:, tb * 512:(tb + 1) * 512], ps.ap[:, :], [ps.r], [sb.sub(tb * 1024, (tb + 1) * 1024)])
                if kind == "q":
                    P.dma("sp", QT[ft * 128:(ft + 1) * 128, :], sa, [sb], [rd["QT"]])
                elif kind == "k":
                    r0 = (ft - 8) * 128
                    P.dma("sp", KT[r0:r0 + 128, tok0:tok0 + TC], sa, [sb], [rd["KT"]])
                elif kind == "g":
                    r0 = (ft - 24) * 128
                    P.dma("sp", GT[r0:r0 + 128, :], sa, [sb], [rd["GT"]])
                elif kind == "u":
                    r0 = (ft - 32) * 128
                    P.dma("sp", UT[r0:r0 + 128, :], sa, [sb], [rd["UT"]])
                else:
                    r0 = (ft - 40) * 128
                    P.dma("sp", GST[r0:r0 + 128, :], sa, [sb], [rd["GST"]])
                if grp == "own":
                    ps = nextps()
                    mm_fm_sample(ps, w3, wb)
                    P.copy(ev_eng(), HS3[:, ft, :], ps.ap[:, 0:NS], [ps.r], [HS])
            blks = [("v", 2048 + 512 * i) for i in range(2)]
            if grp == "own":
                blks = [("k", 1024 + 512 * i) for i in range(2)] + blks
            for bi, (kind, c0) in enumerate(blks):
                wb = wblk[bi % 2]
                w3 = load_wtile(wb, c0, 512)
                fcol = (c0 % 1024)
                for tt in range(16 + (1 if grp == "own" else 0)):
                    ps = nextps()
                    if tt < 16:
                        P.mm([(ps.ap[:, :], XT4[:, tt, kt, :], w3[:, kt, :], kt == 0, kt == 15) for kt in range(16)],
                             [XT.sub(tt * 4096, (tt + 1) * 4096), wb], [ps.r])
                        if grp == "own":
                            ob = ost[tt % 2]
                            P.copy("act", ob.ap(F32), ps.ap[:, :], [ps.r], [ob])
                            dst = k_o if kind == "k" else v_o
                            P.dma("sp", dst[tt * 128:(tt + 1) * 128, fcol:fcol + 512], ob.ap(F32), [ob], [rd[dst.tensor.name]])
                            if kind == "v":
                                vb = vbs[tt % 2]
                                P.copy("pool", vb.ap(BF16), ob.ap(F32), [ob], [vb])
                                P.dma("sp", VS[TC + tt * 128:TC + (tt + 1) * 128, fcol:fcol + 512], vb.ap(BF16), [vb], [rd["VS"]])
                        else:
                            vb = vbs[tt % 2]
                            P.copy(ev_eng(), vb.ap(BF16), ps.ap[:, :], [ps.r], [vb])
                            P.dma("sp", VS[tt * 128:(tt + 1) * 128, fcol:fcol + 512], vb.ap(BF16), [vb], [rd["VS"]])
                    else:
                        P.mm([(ps.ap[0:NS, :], XTs3[:, kt, :], w3[:, kt, :], kt == 0, kt == 15) for kt in range(16)],
                             [XTs, wb], [ps.r])
                        ob = ost[tt % 2]
                        P.copy("act", ob.ap(F32)[0:NS, :], ps.ap[0:NS, :], [ps.r], [ob])
                        dst = k_s if kind == "k" else v_s
                        P.dma("sp", dst[:, fcol:fcol + 512], ob.ap(F32)[0:NS, :], [ob], [rd[dst.tensor.name]])
                        if kind == "v":
                            P.copy("pool", VSs.ap(BF16)[0:NS, fcol:fcol + 512], ob.ap(F32)[0:NS, :], [ob], [VSs])
            if grp == "halo":
                early_prefetch()
        M.release(mB1)

        if stage >= 2:
            if npre_tiles is not None:
                pass
            _E = dict(locals())
            if npre_tiles is not None:
                _E["npre_tiles"] = npre_tiles
            mS = M.mark()
            ssm_phase(_E)
            M.release(mS)
        if stage >= 3:
            _E = dict(locals())
            m3 = M.mark()
            attn_tables(_E)
            m4 = M.mark()
            attn_phase(_E)
            M.release(m4)
            if not os.environ.get("KSKIP_SA"):
                sample_attn_phase(_E)
            M.release(m3)
            out_phase(_E)

        block = es.enter_context(nc.Block())
        P.S.emit(nc, block, eng_sems, dma_sems)
    print("ops:", {e: len(v) for e, v in P.S.ops.items()}, "sbuf peak", M.peak)
    return nc


def host_inputs(inputs):
    xpr = inputs["x_prompt"]
    maps = []
    zeros_chunk = np.zeros((TC, D), np.float32)
    for core in range(8):
        b, c = divmod(core, 4)
        xo = np.ascontiguousarray(xpr[b, c * TC:(c + 1) * TC])
        xh = np.ascontiguousarray(xpr[b, (c - 1) * TC:c * TC]) if c > 0 else zeros_chunk
        pre = []
        for j in range(NPRE):
            cc = c - NPRE + j
            pre.append(xpr[b, cc * TC:(cc + 1) * TC] if cc >= 0 else zeros_chunk)
        xp = np.ascontiguousarray(np.concatenate(pre, axis=0))
        xs = np.ascontiguousarray(inputs["x_sample"][core * 4:(core + 1) * 4].reshape(NS, D))
        m = {
            "xo": xo, "xh": xh, "xp": xp, "xs": xs,
            "w_in": np.ascontiguousarray(inputs["w_in"][0]),
            "w_out": np.ascontiguousarray(inputs["w_out"][0]),
            "w_glu": np.ascontiguousarray(inputs["w_glu"][0]),
            "ident": np.eye(128, dtype=np.float32),
            "lam_re": np.ascontiguousarray(inputs["lam_re"][0]), "lam_im": np.ascontiguousarray(inputs["lam_im"][0]),
            "log_dt": np.ascontiguousarray(inputs["log_dt"][0][None, :]),
            "b_re": np.ascontiguousarray(inputs["b_re"][0]), "b_im": np.ascontiguousarray(inputs["b_im"][0]),
            "c_re": np.ascontiguousarray(inputs["c_re"][0]), "c_im": np.ascontiguousarray(inputs["c_im"][0]),
            "d_skip": np.ascontiguousarray(inputs["d_skip"][0][:, None]), "b_glu": np.ascontiguousarray(inputs["b_glu"][0][:, None]),
            "st_re": np.ascontiguousarray(inputs["state_ssm_re"][0, core * 4:(core + 1) * 4]),
            "st_im": np.ascontiguousarray(inputs["state_ssm_im"][0, core * 4:(core + 1) * 4]),
            "cmask": CMASK, "ctri": CTRI, "ctris": CTRIS,
            "rel_bias": np.ascontiguousarray(inputs["rel_bias"]), "coh": COH, "cj": CJ,
            "hv": np.full((128, 1), 1.0 if c > 0 else 0.0, np.float32),
            "ck": np.ascontiguousarray(inputs["cache_k"][0, core * 4:(core + 1) * 4].reshape(4, CL, 1024)),
            "cv": np.ascontiguousarray(inputs["cache_v"][0, core * 4:(core + 1) * 4].reshape(4, CL, 1024)),
            "ccnt": CCNT,
            "b_out": np.ascontiguousarray(inputs["b_out"][0][None, :]),
            "ln_g": np.ascontiguousarray(inputs["ln_g"][0][None, :]), "ln_b": np.ascontiguousarray(inputs["ln_b"][0][None, :]),
        }
        maps.append(m)
    return maps


def _consts():
    cm = np.zeros((128, 4, 128), np.float32)
    for p in range(128):
        g2 = p // 64
        for q in range(4):
            cm[p, q, q * 32 + g2 * 16:q * 32 + g2 * 16 + 16] = 1.0
    tri = np.triu(np.ones((128, 128), np.float32))
    tris = np.zeros((16, 16), np.float32)
    for b in range(4):
        tris[b * 4:(b + 1) * 4, b * 4:(b + 1) * 4] = np.triu(np.ones((4, 4), np.float32))
    return cm.reshape(128, 512), tri, tris


def _t5_bucket(dist):
    dist = np.asarray(dist, np.int64)
    d_f = np.maximum(dist, 1).astype(np.float32)
    large = 16 + (np.log(d_f / np.float32(16.0)) / np.float32(math.log(2048 / 16)) * np.float32(16.0)).astype(np.int32)
    large = np.minimum(large, 31)
    return np.where(dist < 16, dist, large)


def _attn_consts():
    oh = np.zeros((32, 3, 384), np.float32)
    for p, d in enumerate((1, 4, 16)):
        for j in range(129):
            oh[_t5_bucket(j * d), p, 127 + j] = 1.0
    cj = np.ascontiguousarray(np.eye(128, dtype=np.float32)[::-1])
    return oh.reshape(32, 3 * 384), cj


def _sample_consts():
    cnt = np.zeros((32, 8, 4, 128), np.float32)
    for tile in range(8):
        for k in range(128):
            if tile < 4:
                idx = 1536 + 128 * tile + k
            elif tile < 7:
                sr, mm = divmod(k, 32)
                idx = 16 * (32 * (tile - 4) + mm) + sr
            else:
                if k >= 4:
                    continue
                idx = CL + k
            for s in range(4):
                dd = CL + s - idx
                if dd < 0:
                    continue
                mult = (1 if dd <= 128 else 0) + (1 if (dd % 4 == 0 and dd <= 512) else 0) + (1 if (dd % 16 == 0 and dd <= 2048) else 0)
                if mult:
                    cnt[int(_t5_bucket(dd)), tile, s, k] += mult
    return cnt.reshape(32, 32 * 128)


CMASK, CTRI, CTRIS = _consts()
COH, CJ = _attn_consts()
CCNT = _sample_consts()
_NC_CACHE = {}


def kernel(**inputs):
    inputs = {k: np.asarray(v) for k, v in inputs.items()}
    if "nc" not in _NC_CACHE:
        _NC_CACHE["nc"] = build_program()
    nc = _NC_CACHE["nc"]
    maps = host_inputs(inputs)
    res = run_bass_kernel_spmd(nc, maps, core_ids=list(range(8)))
    rs = res.results
    B = 2
    y_prompt = np.stack([np.concatenate([rs[b * 4 + c]["y_o"] for c in range(4)], axis=0) for b in range(B)])
    y_sample = np.concatenate([rs[i]["y_s"].reshape(4, 4, D) for i in range(8)], axis=0)
    k_prompt = np.stack([rs[b * 4 + 3]["k_o"].reshape(TC, NH, HD) for b in range(B)])[None]
    v_prompt = np.stack([rs[b * 4 + 3]["v_o"].reshape(TC, NH, HD) for b in range(B)])[None]
    sre_p = np.stack([rs[b * 4 + 3]["sre_o"] for b in range(B)])[None]
    sim_p = np.stack([rs[b * 4 + 3]["sim_o"] for b in range(B)])[None]
    k_sample = np.concatenate([rs[i]["k_s"].reshape(4, 4, NH, HD) for i in range(8)], axis=0)[None]
    v_sample = np.concatenate([rs[i]["v_s"].reshape(4, 4, NH, HD) for i in range(8)], axis=0)[None]
    sre_s = np.concatenate([rs[i]["sre_s"] for i in range(8)], axis=0)[None]
    sim_s = np.concatenate([rs[i]["sim_s"] for i in range(8)], axis=0)[None]
    outs = (y_prompt, y_sample, k_prompt, v_prompt, sre_p, sim_p, k_sample, v_sample, sre_s, sim_s)
    return tuple(np.ascontiguousarray(o, dtype=np.float32) for o in outs)


def ssm_phase(E):
    P, M, rd, banks = E["P"], E["M"], E["rd"], E["banks"]
    identF, identb = E["identF"], E["identb"]
    HS3, HS = E["HS3"], E["HS"]
    xp, w_in, w_glu = E["xp"], E["w_in"], E["w_glu"]
    UT, GST, BR = E["UT"], E["GST"], E["BR"]
    BRS = E["BRS"]
    PI = math.pi
    MUL, ADD, SUB = ALU.mult, ALU.add, ALU.subtract
    rot = [0]

    def nextps():
        b = banks[rot[0] % 5]
        rot[0] += 1
        return b
    ZS, YP0, YP1 = banks[7], banks[5], banks[6]

    Mtab = M.alloc(32 * 256 * 2); Mt3 = Mtab.ap(BF16).rearrange("p (a c) -> p a c", c=256)
    Brhs = M.alloc(32 * 256 * 2); Brhs3 = Brhs.ap(BF16).rearrange("p (a c) -> p a c", c=256)
    onesb = M.alloc(4); ones = onesb.ap(BF16)[:, 0:1]; nones = onesb.ap(BF16)[:, 1:2]
    P.memset("dve", onesb.ap(BF16)[:, 0:1], 1.0, [onesb])
    P.memset("dve", onesb.ap(BF16)[:, 1:2], -1.0, [onesb])
    PTr = M.alloc(32 * 128 * 2); PTi = M.alloc(32 * 128 * 2)
    Clr = M.alloc(32 * 128 * 2); Cli = M.alloc(32 * 128 * 2)
    Clr3 = Clr.ap(BF16).rearrange("p (a c) -> p a c", c=128); Cli3 = Cli.ap(BF16).rearrange("p (a c) -> p a c", c=128)
    wglb = M.alloc(8 * 1024 * 2); wgl3 = wglb.ap(BF16).rearrange("p (i j) -> p i j", j=1024)
    P.dma("pool", wgl3, w_glu.rearrange("(i p) j -> p i j", p=128), [rd["w_glu"]], [wglb])
    trib = M.alloc(256); tri = trib.ap(BF16)
    P.dma("pool", tri, E["ctri"][:, :], [rd["ctri"]], [trib])
    ntrib = M.alloc(256); ntri = ntrib.ap(BF16)
    P.ts("dve", ntri, tri, -1.0, None, ALU.mult, None, [trib], [ntrib])
    kp = M.alloc(26 * 128); kpa = kp.ap(F32)
    KEEP = ["abr", "abi", "A128r", "A128i", "k1", "k2", "k3", "k4", "f1", "f2", "fr", "fi", "alr", "ali", "q1", "q2", "q3", "air", "aii",
            "junkr", "junki", "p127r", "p127i"]
    cbuf = M.alloc(64 * 4); c2 = cbuf.ap(F32)
    P.memset("dve", c2, 0.0, [cbuf])
    zcb = M.alloc(64 * 4); zc2 = zcb.ap(F32)

    mprep = M.mark()
    pp = M.alloc(72 * 128)
    ppa = pp.ap(F32)
    names = {}

    def slot(n):
        if n in KEEP:
            i = KEEP.index(n)
            return kpa[:, i * 32:(i + 1) * 32], kp
        if n not in names:
            names[n] = len(names)
            assert len(names) <= 72
        i = names[n]
        return ppa[:, i * 32:(i + 1) * 32], pp

    def s_(n):
        return slot(n)[0]

    def sb_(*ns):
        return [slot(n)[1] for n in ns]

    def tt(o, a, b, op):
        P.tt("dve", s_(o), s_(a), s_(b), op, sb_(a, b), sb_(o))

    def tsc(o, a, c1, op0, c2_=None, op1=None):
        P.ts("dve", s_(o), s_(a), c1, c2_, op0, op1, sb_(a), sb_(o))

    def sl_ap(t, off=0):
        return bass.AP(t.tensor, off, [[1, 128], [128, 32]])

    kin = E["kin"]
    P.copy("dve", s_("lamre"), kin.ap(F32)[:, 0:32], [kin], [pp])
    P.copy("dve", s_("lamim"), kin.ap(F32)[:, 32:64], [kin], [pp])
    P.copy("dve", s_("logdt"), kin.ap(F32)[:, 64:96], [kin], [pp])
    tsc("lr", "lamre", -1e-4, ALU.min)
    tsc("x16", "logdt", 1.0 / 16.0, MUL)
    P.memset("dve", s_("pe"), 1.0, [pp])
    for k in range(10, 0, -1):
        tt("pe", "pe", "x16", MUL)
        tsc("pe", "pe", 1.0 / k, MUL, 1.0, ADD)
    for _ in range(4):
        tt("pe", "pe", "pe", MUL)
    tt("a", "lr", "pe", MUL)
    tt("th", "lamim", "pe", MUL)
    P.act(s_("mag"), s_("a"), AF.Exp, [pp], [pp])
    P.act(s_("magi"), s_("a"), AF.Exp, [pp], [pp], scale=-1.0)
    tsc("thc", "th", PI / 2, ADD)
    for nm in ("th", "thc"):
        for _ in range(4):
            tsc("m", nm, PI, ALU.is_gt)
            P.stt(s_(nm), s_("m"), -2.0 * PI, s_(nm), MUL, ADD, [pp], [pp])
    P.act(s_("sn"), s_("th"), AF.Sin, [pp], [pp])
    P.act(s_("cs"), s_("thc"), AF.Sin, [pp], [pp])
    tt("abr", "mag", "cs", MUL); tt("abi", "mag", "sn", MUL)
    tt("air", "magi", "cs", MUL)
    P.stt(s_("aii"), s_("magi"), -1.0, s_("sn"), MUL, MUL, [pp], [kp])
    tt("t1", "lr", "lr", MUL); tt("t2", "lamim", "lamim", MUL); tt("den", "t1", "t2", ADD)
    P.recip(s_("rden"), s_("den"), [pp], [pp])
    tt("invr", "lr", "rden", MUL)
    P.stt(s_("invi"), s_("lamim"), -1.0, s_("rden"), MUL, MUL, [pp], [pp])
    tsc("nr", "abr", -1.0, ADD)
    tt("t1", "nr", "invr", MUL); tt("t2", "abi", "invi", MUL); tt("cfr", "t1", "t2", SUB)
    tt("t1", "nr", "invi", MUL); tt("t2", "abi", "invr", MUL); tt("cfi", "t1", "t2", ADD)
    tsc("A128r", "abr", 1.0, MUL); tsc("A128i", "abi", 1.0, MUL)
    for _ in range(7):
        tt("q1", "A128r", "A128r", MUL); tt("q2", "A128i", "A128i", MUL); tt("q3", "A128r", "A128i", MUL)
        tt("A128r", "q1", "q2", SUB); tsc("A128i", "q3", 2.0, MUL)

    for nm in ("pe", "abr", "abi", "cfr", "cfi", "A128r", "A128i", "air", "aii"):
        P.dbg(nm, s_(nm), [128, 32], F32, sb_(nm))

    evq = [0]

    def ev_eng():
        evq[0] += 1
        return "act" if evq[0] % 2 == 0 else "dve"

    def to_time_major(dst3, dstbuf, src3, srcbuf, coff, NT):
        for pr0 in range(0, 32, 4):
            ps = nextps()
            P.tr([(ps.ap[0:NT, j * 128:(j + 1) * 128], src3[:, pr0 + j, 0:NT], identF) for j in range(4)],
                 [srcbuf, identb], [ps.r])
            P.copy(ev_eng(), dst3[0:NT, pr0:pr0 + 4, coff:coff + 128],
                   ps.ap[0:NT, :].rearrange("p (a b) -> p a b", a=4), [ps.r], [dstbuf])

    m1 = M.mark()
    cmb = M.alloc(512 * 4); cm3 = cmb.ap(F32).rearrange("p (q c) -> p q c", c=128)
    P.dma("sp", cmb.ap(F32), E["cmask"][:, :], [rd["cmask"]], [cmb])
    bb = list(E["bin_"]) + [M.alloc(32 * 16 * 4) for _ in range(4)]
    b3 = [b.ap(F32).rearrange("p (a c) -> p a c", c=16) for b in bb]
    cfr_b = s_("cfr").unsqueeze(2).to_broadcast([128, 32, 16]); cfi_b = s_("cfi").unsqueeze(2).to_broadcast([128, 32, 16])
    P.tt("dve", b3[2], b3[0], cfr_b, MUL, [bb[0], pp], [bb[2]]); P.tt("dve", b3[3], b3[1], cfi_b, MUL, [bb[1], pp], [bb[3]])
    P.tt("dve", b3[4], b3[2], b3[3], SUB, [bb[2], bb[3]], [bb[4]])
    P.tt("dve", b3[2], b3[1], cfr_b, MUL, [bb[1], pp], [bb[2]]); P.tt("dve", b3[3], b3[0], cfi_b, MUL, [bb[0], pp], [bb[3]])
    P.tt("dve", b3[5], b3[2], b3[3], ADD, [bb[2], bb[3]], [bb[5]])
    YW = [M.alloc(32 * 128 * 4), M.alloc(32 * 128 * 4)]
    YW3 = [y.ap(F32).rearrange("p (a c) -> p a c", c=128) for y in YW]
    for ri in range(2):
        for o in range(8):
            outv = YW3[ri][:, 4 * o:4 * o + 4, :].rearrange("p q (r c) -> p q r c", c=16)
            in0 = b3[4 + ri][:, 4 * o:4 * o + 4, :].unsqueeze(2).to_broadcast([128, 4, 8, 16])
            in1 = cm3.rearrange("p q (r c) -> p q r c", c=16)
            P.tt("dve", outv, in0, in1, MUL, [bb[4 + ri], cmb], [YW[ri]])
    to_time_major(Brhs3, Brhs, YW3[0], YW[0], 0, 128)
    to_time_major(Brhs3, Brhs, YW3[1], YW[1], 128, 128)
    P.dbg("bbr", b3[4], [128, 32, 16], F32, [bb[4]])
    P.dbg("yw0", YW3[0], [128, 32, 128], F32, [YW[0]])
    P.dbg("brhs", Brhs3, [128, 32, 256], BF16, [Brhs])
    M.release(m1)

    def power_tables(tabs):
        for T in tabs:
            T["Tr3"] = T["Tr"].ap(F32).rearrange("p (a t) -> p a t", t=128); T["Ti3"] = T["Ti"].ap(F32).rearrange("p (a t) -> p a t", t=128)
            T["tA3"] = T["tA"].ap(F32).rearrange("p (a t) -> p a t", t=64); T["tB3"] = T["tB"].ap(F32).rearrange("p (a t) -> p a t", t=64)
            T["sc"] = [M.alloc(128) for _ in range(5)]
            P.memset("dve", T["Tr3"][:, :, 0:1], 1.0, [T["Tr"]]); P.memset("dve", T["Ti3"][:, :, 0:1], 0.0, [T["Ti"]])
            P.ts("dve", T["sc"][0].ap(F32), s_(T["br"]), 1.0, None, MUL, None, [kp], [T["sc"][0]])
            P.ts("dve", T["sc"][1].ap(F32), s_(T["bi"]), 1.0, None, MUL, None, [kp], [T["sc"][1]])
        L = 1
        while L < 128:
            for T in tabs:
                alr, ali, q1, q2, q3 = T["sc"]
                Tr, Ti, tA, tB = T["Tr"], T["Ti"], T["tA"], T["tB"]
                br = alr.ap(F32).unsqueeze(2).to_broadcast([128, 32, L]); bi = ali.ap(F32).unsqueeze(2).to_broadcast([128, 32, L])
                sr = T["Tr3"][:, :, 0:L]; si = T["Ti3"][:, :, 0:L]
                t1 = T["tA3"][:, :, 0:L]; t2 = T["tB3"][:, :, 0:L]
                P.tt("dve", t1, sr, br, MUL, [Tr, alr], [tA]); P.tt("dve", t2, si, bi, MUL, [Ti, ali], [tB])
                P.tt("dve", T["Tr3"][:, :, L:2 * L], t1, t2, SUB, [tA, tB], [Tr])
                P.tt("dve", t1, sr, bi, MUL, [Tr, ali], [tA]); P.tt("dve", t2, si, br, MUL, [Ti, alr], [tB])
                P.tt("dve", T["Ti3"][:, :, L:2 * L], t1, t2, ADD, [tA, tB], [Ti])
                P.tt("dve", q1.ap(F32), alr.ap(F32), alr.ap(F32), MUL, [alr], [q1])
                P.tt("dve", q2.ap(F32), ali.ap(F32), ali.ap(F32), MUL, [ali], [q2])
                P.tt("dve", q3.ap(F32), alr.ap(F32), ali.ap(F32), MUL, [alr, ali], [q3])
                P.tt("dve", alr.ap(F32), q1.ap(F32), q2.ap(F32), SUB, [q1, q2], [alr])
                P.ts("dve", ali.ap(F32), q3.ap(F32), 2.0, None, MUL, None, [q3], [ali])
            L *= 2
    M.release(mprep)
    m1 = M.mark()
    TM = dict(Tr=M.alloc(32 * 128 * 4), Ti=M.alloc(32 * 128 * 4), tA=M.alloc(32 * 64 * 4), tB=M.alloc(32 * 64 * 4), br="air", bi="aii")
    TP = dict(Tr=M.alloc(32 * 128 * 4), Ti=M.alloc(32 * 128 * 4), tA=M.alloc(32 * 64 * 4), tB=M.alloc(32 * 64 * 4), br="abr", bi="abi")
    power_tables([TM, TP])
    MIr, MIi, MIr3, MIi3 = TM["Tr"], TM["Ti"], TM["Tr3"], TM["Ti3"]
    to_time_major(Mt3, Mtab, MIr3, MIr, 0, 128)
    to_time_major(Mt3, Mtab, MIi3, MIi, 128, 128)
    PTrf, PTif, PTrf3, PTif3 = TP["Tr"], TP["Ti"], TP["Tr3"], TP["Ti3"]
    PTr3 = PTr.ap(BF16).rearrange("p (a t) -> p a t", t=128); PTi3 = PTi.ap(BF16).rearrange("p (a t) -> p a t", t=128)
    P.copy("act", PTr3, PTrf3, [PTrf], [PTr]); P.copy("act", PTi3, PTif3, [PTif], [PTi])
    P.copy("dve", s_("p127r"), PTrf3[:, :, 127], [PTrf], [kp]); P.copy("dve", s_("p127i"), PTif3[:, :, 127], [PTif], [kp])
    P.dbg("mir", MIr3, [128, 32, 128], F32, [MIr])
    P.dbg("mtab", Mt3, [128, 32, 256], BF16, [Mtab])
    M.release(m1)
    cmb = M.alloc(512 * 4); cm3 = cmb.ap(F32).rearrange("p (q c) -> p q c", c=128)
    P.dma("sp", cmb.ap(F32), E["cmask"][:, :], [rd["cmask"]], [cmb])
    cn2 = [M.alloc(8 * 128 * 4), M.alloc(8 * 128 * 4)]
    ct2 = [M.alloc(8 * 128 * 4), M.alloc(8 * 128 * 4)]
    for ri, nm in enumerate(("c_re", "c_im")):
        c3 = cn2[ri].ap(F32).rearrange("p (o c) -> p o c", c=128)
        srcv = E[nm].rearrange("(o g) c n -> (g c) o n", o=8)
        for dup in range(2):
            P.dma("sp", c3[:, :, dup * 64:(dup + 1) * 64], srcv, [rd[nm]], [cn2[ri]])
        t3 = ct2[ri].ap(F32).rearrange("p (o c) -> p o c", c=128)
        for o0 in range(0, 8, 4):
            ps = nextps()
            P.tr([(ps.ap[:, j * 128:(j + 1) * 128], c3[:, o0 + j, :], identF) for j in range(4)], [cn2[ri], identb], [ps.r])
            P.copy(ev_eng(), t3[:, o0:o0 + 4, :], ps.ap[:, :].rearrange("p (a b) -> p a b", a=4), [ps.r], [ct2[ri]])
        dst3 = Clr3 if ri == 0 else Cli3
        dstb = Clr if ri == 0 else Cli
        for o in range(8):
            in0 = t3[:, o:o + 1, :].to_broadcast([128, 4, 128])
            if ri == 0:
                P.tt("dve", dst3[:, 4 * o:4 * o + 4, :], in0, cm3, MUL, [ct2[ri], cmb], [dstb])
            else:
                P.stt(dst3[:, 4 * o:4 * o + 4, :], in0, -1.0, cm3, MUL, MUL, [ct2[ri], cmb], [dstb])
    M.release(mprep)
    BU = [M.alloc(8 * 256 * 2), M.alloc(8 * 256 * 2)]
    VQ = [M.alloc(8 * 256 * 2), M.alloc(8 * 256 * 2)]
    TD = [[M.alloc(8 * 128 * 2) for _ in range(4)] for _ in range(2)]

    qcount = [0]

    def bu_issue(uT3, ubuf, NT, qd):
        slot = qcount[0] % 2
        qcount[0] += 1
        bu3 = BU[slot].ap(BF16).rearrange("p (a c) -> p a c", c=256)
        for k in range(4):
            ps = nextps()
            prs = [qd * 8 + 2 * k + j for j in range(2)]
            P.mm([(ps.ap[0:NT, j * 256:(j + 1) * 256], uT3[:, prs[j] // 4, 0:NT], Brhs3[:, prs[j], :], True, True)
                  for j in range(2)], [ubuf, Brhs], [ps.r])
            P.copy("act", bu3[0:NT, 2 * k:2 * k + 2, :], ps.ap[0:NT, :].rearrange("p (a b) -> p a b", a=2),
                   [ps.r], [BU[slot].sub(k * 1024, (k + 1) * 1024)])
        return slot

    def demod(NT, qd, slot):
        bu3 = BU[slot].ap(BF16).rearrange("p (a c) -> p a c", c=256)
        vqb = VQ[slot]
        vq3 = vqb.ap(BF16).rearrange("p (a c) -> p a c", c=256)
        mr = Mt3[0:NT, qd * 8:(qd + 1) * 8, 0:128]; mi = Mt3[0:NT, qd * 8:(qd + 1) * 8, 128:256]
        br = bu3[0:NT, :, 0:128]; bi = bu3[0:NT, :, 128:256]
        td = TD[slot]
        t = [td[i].ap(BF16).rearrange("p (a c) -> p a c", c=128)[0:NT] for i in range(4)]
        P.tt("dve", t[0], mr, br, MUL, [Mtab, BU[slot]], [td[0]])
        P.tt("dve", t[1], mi, bi, MUL, [Mtab, BU[slot]], [td[1]])
        P.tt("dve", t[2], mr, bi, MUL, [Mtab, BU[slot]], [td[2]])
        P.tt("dve", t[3], mi, br, MUL, [Mtab, BU[slot]], [td[3]])
        return t, td

    def colsum_quarter(qd, t, td):
        items = []
        for j in range(8):
            pr = qd * 8 + j
            items.append((ZS.ap[:, pr:pr + 1], t[0][:, j, :], ones, True, False))
            items.append((ZS.ap[:, pr:pr + 1], t[1][:, j, :], nones, False, True))
            items.append((ZS.ap[:, 32 + pr:33 + pr], t[2][:, j, :], ones, True, False))
            items.append((ZS.ap[:, 32 + pr:33 + pr], t[3][:, j, :], ones, False, True))
        P.mm(items, list(td) + [onesb], [ZS.r])

    def carry_update():
        P.tt("dve", zc2, ZS.ap[:, 0:64], c2, ADD, [ZS.r, cbuf], [zcb])
        P.tt("dve", s_("k1"), zc2[:, 0:32], s_("A128r"), MUL, [zcb, kp], [kp])
        P.tt("dve", s_("k2"), zc2[:, 32:64], s_("A128i"), MUL, [zcb, kp], [kp])
        P.tt("dve", s_("k3"), zc2[:, 0:32], s_("A128i"), MUL, [zcb, kp], [kp])
        P.tt("dve", s_("k4"), zc2[:, 32:64], s_("A128r"), MUL, [zcb, kp], [kp])

    def carry_commit():
        P.tt("dve", c2[:, 0:32], s_("k1"), s_("k2"), SUB, [kp], [cbuf])
        P.tt("dve", c2[:, 32:64], s_("k3"), s_("k4"), ADD, [kp], [cbuf])

    mpre = M.mark()
    Wu = M.alloc(16 * 1024 * 2); Wu3 = Wu.ap(BF16).rearrange("p (k f) -> p k f", k=16)
    P.dma("pool", Wu3, w_in[:, 4096:5120].rearrange("(k p) f -> p k f", p=128), [rd["w_in"]], [Wu])
    xst = [M.alloc(D * 4), M.alloc(D * 4)]
    XTt = [M.alloc(16 * 128 * 2), M.alloc(16 * 128 * 2)]
    uTb = [M.alloc(8 * 128 * 2), M.alloc(8 * 128 * 2)]
    npre_tiles = E.get("npre_tiles")
    if npre_tiles is None:
        npre_tiles = NPRE * 16
    def stage_a(tt_, part):
        xb = xst[tt_ % 2]; xa = xb.ap(F32)
        xtb = XTt[tt_ % 2]; xt3 = xtb.ap(BF16).rearrange("p (k c) -> p k c", c=128)
        ub = uTb[tt_ % 2]; u3 = ub.ap(BF16).rearrange("p (o t) -> p o t", t=128)
        if part == 0:
            P.dma("sp", xa, xp[tt_ * 128:(tt_ + 1) * 128, :], [rd["xp"]], [xb])
        if part in (0, 1):
            for qq in (2 * part, 2 * part + 1):
                ps = nextps()
                P.tr([(ps.ap[:, j * 128:(j + 1) * 128], xa[:, (4 * qq + j) * 128:(4 * qq + j + 1) * 128], identF) for j in range(4)],
                     [xb, identb], [ps.r])
                P.copy("act" if qq % 2 == 0 else "dve", xt3[:, 4 * qq:4 * qq + 4, :], ps.ap.rearrange("p (a b) -> p a b", a=4), [ps.r],
                       [xtb.sub(qq * 1024, (qq + 1) * 1024)])
        else:
            half = part - 2
            ps = nextps()
            for f4 in range(4):
                ft = half * 4 + f4
                P.mm([(ps.ap[:, f4 * 128:(f4 + 1) * 128], Wu3[:, kt, ft * 128:(ft + 1) * 128], xt3[:, kt, :], kt == 0, kt == 15)
                      for kt in range(16)], [xtb, Wu], [ps.r])
            P.copy("act" if half == 0 else "dve", u3[:, half * 4:half * 4 + 4, :], ps.ap.rearrange("p (a b) -> p a b", a=4), [ps.r],
                   [ub.sub(half * 1024, (half + 1) * 1024)])
        return u3, ub

    if npre_tiles > 0:
        for part in range(4):
            stage_a(0, part)
    for tt_ in range(npre_tiles):
        ub = uTb[tt_ % 2]; u3 = ub.ap(BF16).rearrange("p (o t) -> p o t", t=128)
        slots = {0: bu_issue(u3, ub, 128, 0)}
        for qd in range(4):
            if qd + 1 < 4:
                slots[qd + 1] = bu_issue(u3, ub, 128, qd + 1)
            tq, tdq = demod(128, qd, slots[qd])
            if tt_ + 1 < npre_tiles:
                stage_a(tt_ + 1, qd)
            colsum_quarter(qd, tq, tdq)
        carry_update()
        carry_commit()
    M.release(mpre)

    dskb = E["dskb"]; dsk = dskb.ap(F32)
    bglb = E["bglb"]; bgl = bglb.ap(F32)

    ZC = [[M.alloc(4 * 128 * 2), M.alloc(4 * 128 * 2)] for _ in range(2)]
    RT = [[M.alloc(4 * 128 * 2) for _ in range(4)] for _ in range(2)]
    XG = [[M.alloc(4 * 128 * 2), M.alloc(4 * 128 * 2)] for _ in range(2)]
    Yb = M.alloc(8 * 128 * 4); Zb = M.alloc(8 * 128 * 2); Gb = M.alloc(8 * 128 * 2); SGb = M.alloc(8 * 128 * 2)
    W1 = M.alloc(8 * 128 * 4); W2 = M.alloc(8 * 128 * 2); BRt = [M.alloc(8 * 128 * 2)]
    YPS = [YP0, YP1]

    def y_octet(o, NT, xr3, xi3, xbufs):
        items = []
        for j in range(4):
            pr = 4 * o + j
            dst = YPS[o // 4].ap[:, (o % 4) * NT:(o % 4 + 1) * NT]
            items.append((dst, Clr3[:, pr, :], xr3[:, j, 0:NT], j == 0, False))
            items.append((dst, Cli3[:, pr, :], xi3[:, j, 0:NT], False, j == 3))
        P.mm(items, xbufs + [Clr, Cli], [YPS[o // 4].r])

    def glu_tail(NT, uT3, ubuf, gsrc3, gsbuf, br_out3, br_buf):
        y3 = Yb.ap(F32)[:, 0:8 * NT].rearrange("p (o t) -> p o t", t=NT)
        for o in range(8):
            P.stt(y3[:, o, :], uT3[:, o, 0:NT], dsk[:, o:o + 1], YPS[o // 4].ap[:, (o % 4) * NT:(o % 4 + 1) * NT], MUL, ADD,
                  [ubuf, dskb, YPS[o // 4].r], [Yb])
        yf = Yb.ap(F32)[:, 0:8 * NT]; w1 = W1.ap(F32)[:, 0:8 * NT]; w2 = W2.ap(BF16)[:, 0:8 * NT]
        zf = Zb.ap(BF16)[:, 0:8 * NT]; z3 = zf.rearrange("p (o t) -> p o t", t=NT)
        P.act(w1, yf, AF.Square, [Yb], [W1])
        P.ts("dve", w1, w1, 0.044715, 1.0, MUL, ADD, [W1], [W1])
        P.tt("dve", w1, w1, yf, MUL, [W1, Yb], [W1])
        P.act(w2, w1, AF.Sigmoid, [W1], [W2], scale=1.5957691216057308)
        P.tt("dve", zf, yf, w2, MUL, [Yb, W2], [Zb])
        gps = [nextps(), nextps()]
        for j in range(8):
            P.mm([(gps[j // 4].ap[:, (j % 4) * NT:(j % 4 + 1) * NT], wgl3[:, i, j * 128:(j + 1) * 128], z3[:, i, :], i == 0, i == 7)
                  for i in range(8)], [Zb, wglb], [gps[j // 4].r])
        g3 = Gb.ap(BF16)[:, 0:8 * NT].rearrange("p (o t) -> p o t", t=NT)
        for j in range(8):
            P.act(g3[:, j, :], gps[j // 4].ap[:, (j % 4) * NT:(j % 4 + 1) * NT], AF.Sigmoid, [gps[j // 4].r, bglb], [Gb],
                  bias=bgl[:, j:j + 1])
        sg = SGb.ap(BF16)[:, 0:8 * NT]
        P.act(sg.rearrange("p (o t) -> p o t", t=NT), gsrc3, AF.Silu, [gsbuf], [SGb])
        w3_ = W2.ap(BF16)[:, 0:8 * NT]
        P.tt("dve", w3_, zf, Gb.ap(BF16)[:, 0:8 * NT], MUL, [Zb, Gb], [W2])
        P.tt("dve", br_out3, w3_.rearrange("p (o t) -> p o t", t=NT), sg.rearrange("p (o t) -> p o t", t=NT), MUL, [W2, SGb], [br_buf])

    ms = M.mark()
    us = M.alloc(8 * 16 * 2); us3 = us.ap(BF16).rearrange("p (o t) -> p o t", t=16)
    P.copy("dve", us3, HS3[:, 32:40, :], [HS], [us])
    bur, bui = nextps(), nextps()
    P.mm([(bur.ap[:, pr * 16:(pr + 1) * 16], Brhs3[:, pr, 0:128], us3[:, pr // 4, :], True, True) for pr in range(32)], [us, Brhs], [bur.r])
    P.mm([(bui.ap[:, pr * 16:(pr + 1) * 16], Brhs3[:, pr, 128:256], us3[:, pr // 4, :], True, True) for pr in range(32)], [us, Brhs], [bui.r])
    XS = [M.alloc(32 * 16 * 4), M.alloc(32 * 16 * 4)]
    XS4 = [b.ap(F32).rearrange("p (a b t) -> p a b t", b=4, t=4) for b in XS]
    x0 = list(E["x0in"]) + [M.alloc(32 * 4 * 4) for _ in range(2)]
    x03 = [b.ap(F32).rearrange("p (a b) -> p a b", b=4) for b in x0]
    abr_b = s_("abr").unsqueeze(2).to_broadcast([128, 32, 4]); abi_b = s_("abi").unsqueeze(2).to_broadcast([128, 32, 4])
    bur4 = bur.ap.rearrange("p (a b t) -> p a b t", b=4, t=4); bui4 = bui.ap.rearrange("p (a b t) -> p a b t", b=4, t=4)
    for tau in range(4):
        pr_, pi_ = (x03[0], x03[1]) if tau == 0 else (XS4[0][:, :, :, tau - 1], XS4[1][:, :, :, tau - 1])
        srcb = [x0[0], x0[1]] if tau == 0 else [XS[0], XS[1]]
        P.tt("dve", x03[2], pr_, abr_b, MUL, srcb + [kp], [x0[2]]); P.tt("dve", x03[3], pi_, abi_b, MUL, srcb + [kp], [x0[3]])
        P.tt("dve", x03[2], x03[2], x03[3], SUB, [x0[2], x0[3]], [x0[2]])
        P.tt("dve", XS4[0][:, :, :, tau], x03[2], bur4[:, :, :, tau], ADD, [x0[2], bur.r], [XS[0]])
        P.tt("dve", x03[2], pr_, abi_b, MUL, srcb + [kp], [x0[2]]); P.tt("dve", x03[3], pi_, abr_b, MUL, srcb + [kp], [x0[3]])
        P.tt("dve", x03[2], x03[2], x03[3], ADD, [x0[2], x0[3]], [x0[2]])
        P.tt("dve", XS4[1][:, :, :, tau], x03[2], bui4[:, :, :, tau], ADD, [x0[2], bui.r], [XS[1]])
    for ri, nm in enumerate(("sre_s", "sim_s")):
        for b in range(4):
            P.dma("pool", sl_ap(E[nm], b * 4096), XS4[ri][:, :, b, 3], [XS[ri]], [rd[nm]], slow=True)
    P.dbg("hs", HS3, [128, 48, 16], F32, [HS])
    P.dbg("xs0", XS[0].ap(F32), [128, 512], F32, [XS[0]])
    P.dbg("x0r", x0[0].ap(F32), [128, 128], F32, [x0[0]])
    XSb = [M.alloc(32 * 16 * 2), M.alloc(32 * 16 * 2)]
    XSb3 = [b.ap(BF16).rearrange("p (a t) -> p a t", t=16) for b in XSb]
    for ri in range(2):
        P.copy("dve", XSb3[ri], XS[ri].ap(F32).rearrange("p (a t) -> p a t", t=16), [XS[ri]], [XSb[ri]])
    for o in range(8):
        y_octet(o, 16, XSb3[0][:, 4 * o:4 * o + 4, :], XSb3[1][:, 4 * o:4 * o + 4, :], [XSb[0], XSb[1]])
    brs3 = BRS.ap(BF16).rearrange("p (f t) -> p f t", t=16)
    glu_tail(16, us3, us, HS3[:, 40:48, :], HS, brs3[:, 8:16, :], BRS)
    M.release(ms)

    uTo = [M.alloc(8 * 128 * 2) for _ in range(3)]
    gso = [M.alloc(8 * 128 * 2) for _ in range(3)]
    gi = [0]

    def own_load(tt_):
        ub = uTo[tt_ % 3]; u3 = ub.ap(BF16).rearrange("p (o t) -> p o t", t=128)
        P.dma("sp", u3, UT[:, tt_ * 128:(tt_ + 1) * 128].rearrange("(o p) t -> p o t", p=128), [rd["UT"]], [ub])
        gb = gso[tt_ % 3]; g3 = gb.ap(BF16).rearrange("p (o t) -> p o t", t=128)
        P.dma("sp", g3, GST[:, tt_ * 128:(tt_ + 1) * 128].rearrange("(o p) t -> p o t", p=128), [rd["GST"]], [gb])

    ZC4 = {(a, g): [M.alloc(4 * 128 * 2), M.alloc(4 * 128 * 2)] for a in range(2) for g in range(2)}
    XG4 = {(a, g): [M.alloc(4 * 128 * 2), M.alloc(4 * 128 * 2)] for a in range(2) for g in range(2)}
    st = {}

    def tile_bufs(t):
        ub = uTo[t % 3]; u3 = ub.ap(BF16).rearrange("p (o t) -> p o t", t=128)
        gb = gso[t % 3]; g3 = gb.ap(BF16).rearrange("p (o t) -> p o t", t=128)
        return u3, ub, g3, gb

    def S1(Q):
        t, q = divmod(Q, 4)
        u3, ub, g3, gb = tile_bufs(t)
        if q == 0 and t + 1 < 16:
            own_load(t + 1)
        st[("slot", Q)] = bu_issue(u3, ub, 128, q)

    def S2(Q):
        t, q = divmod(Q, 4)
        st[("vq", Q)] = demod(128, q, st.pop(("slot", Q)))

    def S3(Q):
        t, q = divmod(Q, 4)
        tq, tdq = st.pop(("vq", Q))
        colsum_quarter(q, tq, tdq)
        for gg in range(2):
            o = q * 2 + gg
            zr, zi = nextps(), nextps()
            itr, iti = [], []
            for j in range(4):
                itr.append((zr.ap[:, j * 128:(j + 1) * 128], tq[0][:, gg * 4 + j, :], tri, True, False))
                itr.append((zr.ap[:, j * 128:(j + 1) * 128], tq[1][:, gg * 4 + j, :], ntri, False, True))
                iti.append((zi.ap[:, j * 128:(j + 1) * 128], tq[2][:, gg * 4 + j, :], tri, True, False))
                iti.append((zi.ap[:, j * 128:(j + 1) * 128], tq[3][:, gg * 4 + j, :], tri, False, True))
            P.mm(itr, list(tdq) + [trib, ntrib], [zr.r])
            P.mm(iti, list(tdq) + [trib], [zi.r])
            zcs = ZC4[(Q % 2, gg)]
            zcr = zcs[0].ap(BF16).rearrange("p (a t) -> p a t", t=128); zci = zcs[1].ap(BF16).rearrange("p (a t) -> p a t", t=128)
            for j in range(4):
                P.act(zcr[:, j, :], zr.ap[:, j * 128:(j + 1) * 128], AF.Identity, [zr.r, cbuf], [zcs[0]],
                      bias=c2[:, 4 * o + j:4 * o + j + 1])
                P.act(zci[:, j, :], zi.ap[:, j * 128:(j + 1) * 128], AF.Identity, [zi.r, cbuf], [zcs[1]],
                      bias=c2[:, 32 + 4 * o + j:32 + 4 * o + j + 1])
        if q == 3:
            carry_update()
            if t == 15:
                p127r = s_("p127r"); p127i = s_("p127i")
                P.tt("dve", s_("f1"), zc2[:, 0:32], p127r, MUL, [zcb, kp], [kp]); P.tt("dve", s_("f2"), zc2[:, 32:64], p127i, MUL, [zcb, kp], [kp])
                P.tt("dve", s_("fr"), s_("f1"), s_("f2"), SUB, [kp], [kp])
                P.tt("dve", s_("f1"), zc2[:, 0:32], p127i, MUL, [zcb, kp], [kp]); P.tt("dve", s_("f2"), zc2[:, 32:64], p127r, MUL, [zcb, kp], [kp])
                P.tt("dve", s_("fi"), s_("f1"), s_("f2"), ADD, [kp], [kp])
                P.dma("pool", sl_ap(E["sre_o"]), s_("fr"), [kp], [rd["sre_o"]], slow=True)
                P.dma("pool", sl_ap(E["sim_o"]), s_("fi"), [kp], [rd["sim_o"]], slow=True)
            carry_commit()

    def S4(Q):
        t, q = divmod(Q, 4)
        for gg in range(2):
            o = q * 2 + gg
            zcs = ZC4[(Q % 2, gg)]; xs = XG4[(Q % 2, gg)]; rts = RT[gg]
            zcr = zcs[0].ap(BF16).rearrange("p (a t) -> p a t", t=128); zci = zcs[1].ap(BF16).rearrange("p (a t) -> p a t", t=128)
            pr_ = PTr3[:, 4 * o:4 * o + 4, :]; pi_ = PTi3[:, 4 * o:4 * o + 4, :]
            r = [rts[i].ap(BF16).rearrange("p (a t) -> p a t", t=128) for i in range(4)]
            xr3 = xs[0].ap(BF16).rearrange("p (a t) -> p a t", t=128); xi3 = xs[1].ap(BF16).rearrange("p (a t) -> p a t", t=128)
            P.tt("dve", r[0], zcr, pr_, MUL, [zcs[0], PTr], [rts[0]])
            P.tt("dve", r[1], zci, pi_, MUL, [zcs[1], PTi], [rts[1]])
            P.tt("dve", r[2], zcr, pi_, MUL, [zcs[0], PTi], [rts[2]])
            P.tt("dve", r[3], zci, pr_, MUL, [zcs[1], PTr], [rts[3]])
            P.tt("dve", xr3, r[0], r[1], SUB, [rts[0], rts[1]], [xs[0]])
            P.tt("dve", xi3, r[2], r[3], ADD, [rts[2], rts[3]], [xs[1]])

    def S5(Q):
        t, q = divmod(Q, 4)
        for gg in range(2):
            o = q * 2 + gg
            xs = XG4[(Q % 2, gg)]
            xr3 = xs[0].ap(BF16).rearrange("p (a t) -> p a t", t=128); xi3 = xs[1].ap(BF16).rearrange("p (a t) -> p a t", t=128)
            y_octet(o, 128, xr3, xi3, [xs[0], xs[1]])
        if q == 3:
            u3, ub, g3, gb = tile_bufs(t)
            bt = BRt[0]
            bt3 = bt.ap(BF16).rearrange("p (o t) -> p o t", t=128)
            glu_tail(128, u3, ub, g3, gb, bt3, bt)
            P.dma("sp", BR[1024:2048, t * 128:(t + 1) * 128].rearrange("(o p) t -> p o t", p=128), bt3, [bt], [rd["BR"]])

    own_load(0)
    NQ = 64
    for k in range(NQ + 4):
        if k < NQ:
            S1(k)
        if 0 <= k - 1 < NQ:
            S2(k - 1)
        if 0 <= k - 3 < NQ:
            S4(k - 3)
        if 0 <= k - 2 < NQ:
            S3(k - 2)
        if 0 <= k - 4 < NQ:
            S5(k - 4)


DILS = (1, 4, 16)


def attn_tables(E):
    P, M, rd, banks = E["P"], E["M"], E["rd"], E["banks"]
    nextps = E["nextps"]
    TT = M.alloc(48 * 256 * 2); TT3 = TT.ap(BF16).rearrange("p (a c) -> p a c", c=256)
    E32 = M.alloc(16 * 4)
    P.dma("sp", E32.ap(F32)[0:32, :], E["rel_bias"][:, :], [rd["rel_bias"]], [E32])
    P.act(E32.ap(F32)[0:32, :], E32.ap(F32)[0:32, :], AF.Exp, [E32], [E32])
    m0 = M.mark()
    OH = M.alloc(3 * 384 * 4); OH3 = OH.ap(F32).rearrange("p (a c) -> p a c", c=384)
    P.dma("sp", OH.ap(F32)[0:32, :], E["coh"][:, :], [rd["coh"]], [OH])
    Jb = M.alloc(128 * 4)
    P.dma("sp", Jb.ap(F32), E["cj"][:, :], [rd["cj"]], [Jb])
    gst = M.alloc(384 * 4)
    GV = E["GV"]
    for p in range(3):
        ps = nextps()
        P.mm([(ps.ap[0:16, 0:384], E32.ap(F32)[0:32, :], OH3[0:32, p, :], True, True)], [E32, OH], [ps.r])
        P.copy("dve", gst.ap(F32)[0:16, :], ps.ap[0:16, 0:384], [ps.r], [gst])
        P.dma("sp", GV[p * 16:(p + 1) * 16, :], gst.ap(F32)[0:16, :], [gst], [rd["GV"]])
    hst = [M.alloc(512 * 4), M.alloc(512 * 4)]
    for i in range(24):
        hb = hst[i % 2]
        P.dma("sp", hb.ap(F32).rearrange("p (a c) -> p a c", c=256),
              bass.AP(GV.tensor, i * 2 * 384, [[1, 128], [384, 2], [1, 256]]), [rd["GV"]], [hb])
        ps = nextps()
        P.mm([(ps.ap[:, :], Jb.ap(F32), hb.ap(F32), True, True)], [Jb, hb], [ps.r])
        P.copy("act" if i % 2 == 0 else "dve", TT3[:, 2 * i:2 * i + 2, :], ps.ap.rearrange("p (a c) -> p a c", c=256), [ps.r], [TT])
    M.release(m0)
    E["TT"], E["TT3"], E["E32"] = TT, TT3, E32


def attn_phase(E):
    P, M, rd, banks = E["P"], E["M"], E["rd"], E["banks"]
    nextps = E["nextps"]
    QT, KT, GT, VS, BR = E["QT"], E["KT"], E["GT"], E["VS"], E["BR"]
    TT, TT3 = E["TT"], E["TT3"]
    MUL, ADD = ALU.mult, ALU.add
    hvb = M.alloc(4)
    P.dma("sp", hvb.ap(F32), E["hv"][:, :], [rd["hv"]], [hvb])
    onf = M.alloc(64 * 4)
    P.memset("dve", onf.ap(F32), 1.0, [onf])
    NT_H = 21
    HOFF = (0, 1, 5)
    OOFF = (21, 37, 53)
    VA = [M.alloc(69 * 65 * 2), M.alloc(69 * 65 * 2)]
    VA3 = [v.ap(BF16).rearrange("p (t c) -> p t c", c=65) for v in VA]
    for i in range(2):
        P.memset("pool", VA[i].ap(BF16), 1.0, [VA[i]])
        P.copy("dve", VA3[i][:, 0:NT_H, 64:65], hvb.ap(F32)[:, 0:1].unsqueeze(1).to_broadcast([128, NT_H, 1]), [hvb], [VA[i]])
    KTh = [M.alloc(4096 * 2), M.alloc(4096 * 2)]
    QTh = [M.alloc(2048 * 2), M.alloc(2048 * 2)]
    GTh = [M.alloc(2048 * 2), M.alloc(2048 * 2)]
    ACC = [M.alloc(2048 * 4), M.alloc(2048 * 4)]
    NEB = 6
    EB = [M.alloc(256 * 2) for _ in range(NEB)]
    PB = [M.alloc(256 * 2) for _ in range(NEB)]
    SG = M.alloc(2048 * 2)
    BRh = [M.alloc(2048 * 2), M.alloc(2048 * 2)]
    LN_ = M.alloc(2048 * 4)
    ei = [0]
    def head_loads(h):
        k2 = h % 2
        kt = KTh[k2].ap(BF16); qt = QTh[k2].ap(BF16); gt = GTh[k2].ap(BF16)
        P.dma("sp", kt[0:64, :], KT[h * 64:(h + 1) * 64, :], [rd["KT"]], [KTh[k2]])
        P.dma("sp", qt[0:64, :], QT[h * 64:(h + 1) * 64, :], [rd["QT"]], [QTh[k2]])
        P.dma("sp", gt[0:64, :], GT[h * 64:(h + 1) * 64, :], [rd["GT"]], [GTh[k2]])
        va3 = VA3[k2]
        for p, d in enumerate(DILS):
            nsp = 16 // d
            if d == 1:
                src = bass.AP(VS.tensor, TC * 1024 + h * 64, [[1024, 128], [128 * 1024, 16], [1, 64]])
                P.dma("sp", va3[:, OOFF[p]:OOFF[p] + 16, 0:64], src, [rd["VS"]], [VA[k2]])
            else:
                for s_ in range(nsp):
                    src = bass.AP(VS.tensor, (TC + s_ * 128 * d) * 1024 + h * 64, [[d * 1024, 128], [1024, d], [1, 64]])
                    P.dma("sp", va3[:, OOFF[p] + s_ * d:OOFF[p] + (s_ + 1) * d, 0:64], src, [rd["VS"]], [VA[k2]])
            src = bass.AP(VS.tensor, (TC - 128 * d) * 1024 + h * 64, [[d * 1024, 128], [1024, d], [1, 64]])
            P.dma("sp", va3[:, HOFF[p]:HOFF[p] + d, 0:64], src, [rd["VS"]], [VA[k2]])

    head_loads(0)
    for h in range(NH):
        k2 = h % 2
        kt = KTh[k2].ap(BF16); qt = QTh[k2].ap(BF16); gt = GTh[k2].ap(BF16)
        va3 = VA3[k2]
        if h + 1 < NH:
            head_loads(h + 1)
        acc = ACC[k2].ap(F32)
        blks = []
        for p, d in enumerate(DILS):
            nsp = 16 // d
            blocks = [(s_, r_) for s_ in range(nsp) for r_ in range(d)]
            for g0 in range(0, 16, 4):
                for j in range(4):
                    blks.append((p, d, g0, j, blocks[g0 + j][0], blocks[g0 + j][1]))
        LA = 3
        pbs = {}
        pos = {}

        def stage_s(i):
            p, d, g0, j, sg_, r_ = blks[i]
            tcol = TT3[:, p * 16 + h, :]
            start = sg_ * 128 * d + r_
            qv = qt[0:64, start:start + 127 * d + 1:d] if d > 1 else qt[0:64, start:start + 128]
            kc0 = TC + start
            kp0 = TC + start - 128 * d
            kcur = kt[0:64, kc0:kc0 + 127 * d + 1:d] if d > 1 else kt[0:64, kc0:kc0 + 128]
            kprv = kt[0:64, kp0:kp0 + 127 * d + 1:d] if d > 1 else kt[0:64, kp0:kp0 + 128]
            ps = nextps()
            P.mm([(ps.ap[:, 0:128], kcur, qv, True, True), (ps.ap[:, 128:256], kprv, qv, True, True)],
                 [KTh[k2], QTh[k2]], [ps.r])
            eb = EB[ei[0] % NEB]; pb = PB[ei[0] % NEB]
            P.act(eb.ap(BF16), ps.ap[:, 0:256], AF.Exp, [ps.r], [eb], scale=SCALE)
            P.tt("pool" if ei[0] % 4 == 3 else "dve", pb.ap(BF16), eb.ap(BF16), tcol, MUL, [eb, TT], [pb])
            ei[0] += 1
            pbs[i] = pb

        def stage_v(i):
            p, d, g0, j, sg_, r_ = blks[i]
            if j == 0:
                pos[(p, g0)] = nextps()
            po = pos[(p, g0)]
            pb = pbs.pop(i)
            tcur = OOFF[p] + sg_ * d + r_
            tprv = (OOFF[p] + (sg_ - 1) * d + r_) if sg_ > 0 else (HOFF[p] + r_)
            P.mm([(po.ap[0:65, j * 128:(j + 1) * 128], va3[:, tcur, :], pb.ap(BF16)[:, 0:128], True, False),
                  (po.ap[0:65, j * 128:(j + 1) * 128], va3[:, tprv, :], pb.ap(BF16)[:, 128:256], False, True)],
                 [VA[k2], pb], [po.r])
            if j == 3:
                if d == 1:
                    outv = acc[0:65, g0 * 128:(g0 + 4) * 128]
                    P.copy("dve", outv, po.ap[0:65, :], [po.r], [ACC[k2]])
                elif d == 4:
                    outv = acc[0:65, (g0 // 4) * 512:(g0 // 4 + 1) * 512].rearrange("p (i j) -> p j i", j=4)
                    P.tt("dve", outv, outv, po.ap[0:65, :].rearrange("p (j i) -> p j i", j=4), ADD, [po.r, ACC[k2]], [ACC[k2]])
                else:
                    outv = acc[0:65, :].rearrange("p (i r) -> p r i", r=16)[:, g0:g0 + 4, :]
                    P.tt("dve", outv, outv, po.ap[0:65, :].rearrange("p (j i) -> p j i", j=4), ADD, [po.r, ACC[k2]], [ACC[k2]])

        for i in range(len(blks) + LA):
            if i < len(blks):
                stage_s(i)
            if i - LA >= 0:
                stage_v(i - LA)
        P.act(acc[64:65, :], acc[64:65, :], AF.Ln, [ACC[k2]], [ACC[k2]])
        P.act(acc[64:65, :], acc[64:65, :], AF.Exp, [ACC[k2]], [ACC[k2]], scale=-1.0)
        P.act(SG.ap(BF16)[0:64, :], gt[0:64, :], AF.Silu, [GTh[k2]], [SG])
        brh = BRh[k2]
        for n in range(4):
            ps = nextps()
            P.mm([(ps.ap[0:64, :], onf.ap(F32)[64:65, 0:64], acc[64:65, n * 512:(n + 1) * 512], True, True)], [onf, ACC[k2]], [ps.r])
            P.tt("dve", LN_.ap(F32)[0:64, n * 512:(n + 1) * 512], acc[0:64, n * 512:(n + 1) * 512], ps.ap[0:64, :], MUL,
                 [ps.r, ACC[k2]], [LN_.sub(n * 2048, (n + 1) * 2048)])
            P.tt("pool", brh.ap(BF16)[0:64, n * 512:(n + 1) * 512], LN_.ap(F32)[0:64, n * 512:(n + 1) * 512],
                 SG.ap(BF16)[0:64, n * 512:(n + 1) * 512], MUL, [LN_.sub(n * 2048, (n + 1) * 2048), SG], [brh.sub(n * 1024, (n + 1) * 1024)])
        P.dma("pool", BR[h * 64:(h + 1) * 64, :], brh.ap(BF16)[0:64, :], [brh], [rd["BR"]])


def out_phase(E):
    P, M, rd, banks = E["P"], E["M"], E["rd"], E["banks"]
    nextps = E["nextps"]
    BR, xo, y_o, xs, y_s = E["BR"], E["xo"], E["y_o"], E["xs"], E["y_s"]
    BRS = E["BRS"]
    MUL, ADD, SUB = ALU.mult, ALU.add, ALU.subtract
    Wo = M.alloc(16 * 2048 * 2); wo3 = Wo.ap(BF16).rearrange("p (k d) -> p k d", k=16)
    for half in range(2):
        P.dma("pool", wo3[:, half * 8:(half + 1) * 8, :], E["w_out"][half * 1024:(half + 1) * 1024, :].rearrange("(k p) d -> p k d", p=128),
              [rd["w_out"]], [Wo.sub(half * 32768, (half + 1) * 32768)])
    bob = M.alloc(2048 * 2)
    P.dma("pool", bob.ap(BF16)[0:1, :], E["b_out"][:, :], [rd["b_out"]], [bob])
    onb = M.alloc(128 * 2)
    P.memset("dve", onb.ap(BF16), 1.0, [onb])
    gB = M.alloc(2048 * 4); bB = M.alloc(2048 * 4)
    P.dma("sp", gB.ap(F32), bass.AP(E["ln_g"].tensor, 0, [[0, 128], [1, 2048]]), [rd["ln_g"]], [gB])
    P.dma("sp", bB.ap(F32), bass.AP(E["ln_b"].tensor, 0, [[0, 128], [1, 2048]]), [rd["ln_b"]], [bB])
    xst = [M.alloc(D * 4), M.alloc(D * 4)]
    brt = [M.alloc(16 * 128 * 2), M.alloc(16 * 128 * 2)]
    Vb = M.alloc(D * 4)
    Ob = [M.alloc(D * 4), M.alloc(D * 4)]
    stb = M.alloc(64 * 4)

    def do_tile(NT, br3, brbuf, xsrc, xsrc_r, ydst, ydst_r, ti):
        xb = xst[ti % 2]; xa = xb.ap(F32)
        if xsrc is not None:
            P.dma("sp", xa[0:NT, :], xsrc, [xsrc_r], [xb])
        va = Vb.ap(F32)
        for n in range(4):
            ps = nextps()
            items = [(ps.ap[0:NT, :], br3[:, ft, 0:NT], wo3[:, ft, n * 512:(n + 1) * 512], ft == 0, False) for ft in range(16)]
            items.append((ps.ap[0:NT, :], onb.ap(BF16)[0:1, 0:NT], bob.ap(BF16)[0:1, n * 512:(n + 1) * 512], False, True))
            P.mm(items, [brbuf, Wo, onb, bob], [ps.r])
            P.stt(va[0:NT, n * 512:(n + 1) * 512], xa[0:NT, n * 512:(n + 1) * 512], DN_ALPHA, ps.ap[0:NT, :], MUL, ADD,
                  [xb, ps.r], [Vb.sub(n * 2048, (n + 1) * 2048)])
        st = stb.ap(F32)
        for n in range(4):
            def fn(h, n=n):
                return h.bn_stats(out=st[0:NT, n * 6:(n + 1) * 6], in_=va[0:NT, n * 512:(n + 1) * 512])
            P.S.op("dve", fn, res_of([Vb]), res_of([stb]))

        def fn2(h):
            return h.bn_aggr(out=st[0:NT, 32:34], in_=st[0:NT, 0:24])
        P.S.op("dve", fn2, res_of([stb]), res_of([stb]))
        P.ts("dve", st[0:NT, 34:35], st[0:NT, 33:34], LN_EPS, None, ADD, None, [stb], [stb])
        P.act(st[0:NT, 35:36], st[0:NT, 34:35], AF.Sqrt, [stb], [stb])
        P.recip(st[0:NT, 36:37], st[0:NT, 35:36], [stb], [stb])
        P.stt(st[0:NT, 37:38], st[0:NT, 32:33], -1.0, st[0:NT, 36:37], MUL, MUL, [stb], [stb])
        ob = Ob[ti % 2]; oa = ob.ap(F32)
        P.act(oa[0:NT, :], va[0:NT, :], AF.Identity, [Vb, stb], [ob], scale=st[0:NT, 36:37], bias=st[0:NT, 37:38])
        P.tt("dve", oa[0:NT, :], oa[0:NT, :], gB.ap(F32)[0:NT, :], MUL, [ob, gB], [ob])
        P.tt("pool", oa[0:NT, 0:1024], oa[0:NT, 0:1024], bB.ap(F32)[0:NT, 0:1024], ADD, [ob.sub(0, 4096), bB], [ob.sub(0, 4096)])
        P.tt("dve", oa[0:NT, 1024:2048], oa[0:NT, 1024:2048], bB.ap(F32)[0:NT, 1024:2048], ADD, [ob.sub(4096, 8192), bB], [ob.sub(4096, 8192)])
        P.dma("pool", ydst, oa[0:NT, :], [ob], [ydst_r])

    def out_loads(tt_):
        bt = brt[tt_ % 2]
        bt3 = bt.ap(BF16).rearrange("p (f t) -> p f t", t=128)
        P.dma("sp", bt3, BR[:, tt_ * 128:(tt_ + 1) * 128].rearrange("(f p) t -> p f t", p=128), [rd["BR"]], [bt])
        xb = xst[tt_ % 2]
        P.dma("sp", xb.ap(F32), xo[tt_ * 128:(tt_ + 1) * 128, :], [rd["xo"]], [xb])

    out_loads(0)
    for tt_ in range(16):
        bt = brt[tt_ % 2]
        bt3 = bt.ap(BF16).rearrange("p (f t) -> p f t", t=128)
        if tt_ + 1 < 16:
            out_loads(tt_ + 1)
        do_tile(128, bt3, bt, None, rd["xo"], y_o[tt_ * 128:(tt_ + 1) * 128, :], rd["y_o"], tt_)
    if E.get("sample_attn_done"):
        brs3 = BRS.ap(BF16).rearrange("p (f t) -> p f t", t=16)
        do_tile(16, brs3, BRS, xs[:, :], rd["xs"], y_s[:, :], rd["y_s"], 16)


def sample_attn_phase(E):
    P, M, rd, banks = E["P"], E["M"], E["rd"], E["banks"]
    rot = [0]

    def nextps():
        b_ = banks[rot[0] % 6]
        rot[0] += 1
        return b_
    HS3, HS, VSs, BRS = E["HS3"], E["HS"], E["VSs"], E["BRS"]
    E32 = E["E32"]
    ck, cv, BRSD = E["ck"], E["cv"], E["BRSD"]
    MUL = ALU.mult
    cnt = M.alloc(32 * 128 * 4); cnt3 = cnt.ap(F32).rearrange("p (a k) -> p a k", k=128)
    P.dma("sp", cnt.ap(F32)[0:32, :], E["ccnt"][:, :], [rd["ccnt"]], [cnt])
    WS = M.alloc(32 * 16 * 4); ws3 = WS.ap(F32).rearrange("p (a h) -> p a h", h=16)
    ps = nextps()
    P.mm([(ps.ap[:, a * 16:(a + 1) * 16], cnt3[0:32, a, :], E32.ap(F32)[0:32, :], True, True) for a in range(32)], [cnt, E32], [ps.r])
    P.copy("dve", WS.ap(F32), ps.ap[:, :], [ps.r], [WS])
    wsv = ws3.rearrange("p (t s) h -> p t h s", s=4)
    HSb = M.alloc(16 * 16 * 2); hsb3 = HSb.ap(BF16).rearrange("p (f t) -> p f t", t=16)
    P.copy("dve", hsb3, HS3[:, 0:16, :], [HS], [HSb])
    onf = M.alloc(64 * 4)
    P.memset("dve", onf.ap(F32), 1.0, [onf])
    QP = M.alloc(16 * 16 * 2); qp3 = QP.ap(BF16).rearrange("p (h t) -> p h t", t=16)
    P.memset("dve", QP.ap(BF16), 0.0, [QP])
    P.copy("dve", qp3[0:64, 0:16:2, :], hsb3[0:64, 0:8, :], [HSb], [QP])
    P.copy("dve", qp3[64:128, 1:16:2, :], hsb3[64:128, 0:8, :], [HSb], [QP])
    STOP = int(os.environ.get("KSA_STOP", "99"))
    if STOP <= 1:
        return
    KC = [M.alloc(1024 * 4), M.alloc(1024 * 4)]
    VC = [M.alloc(1024 * 4), M.alloc(1024 * 4)]
    VAs = M.alloc(8 * 16 * 65 * 2); vas4 = VAs.ap(BF16).rearrange("p (t h c) -> p t h c", h=16, c=65)
    P.memset("pool", VAs.ap(BF16), 1.0, [VAs])
    KTs = [M.alloc(8 * 128 * 2), M.alloc(8 * 128 * 2)]
    EBs = M.alloc(512 * 2); PMs = M.alloc(512 * 2)
    eb4 = EBs.ap(BF16).rearrange("p (t h s) -> p t h s", h=16, s=4); pm4 = PMs.ap(BF16).rearrange("p (t h s) -> p t h s", h=16, s=4)
    NUM = M.alloc(64 * 4); AT = M.alloc(64 * 4)
    identF, identb = E["identF"], E["identb"]
    li = [0]
    for b in range(4):
        sp = banks[6]
        sp4 = sp.ap.rearrange("p (t h s) -> p t h s", h=16, s=4)
        for tile in range(7):
            kc = KC[li[0] % 2]; vc = VC[li[0] % 2]; kts = KTs[li[0] % 2]
            li[0] += 1
            if tile < 4:
                r0 = 1536 + 128 * tile
                P.dma("sp", kc.ap(F32), ck[b, r0:r0 + 128, :], [rd["ck"]], [kc])
                P.dma("act", vc.ap(F32), cv[b, r0:r0 + 128, :], [rd["cv"]], [vc])
            else:
                u = tile - 4
                for sr in range(4):
                    off = b * CL * 1024 + (16 * 32 * u + sr) * 1024
                    P.dma("sp", kc.ap(F32)[sr * 32:(sr + 1) * 32, :], bass.AP(ck.tensor, off, [[16 * 1024, 32], [1, 1024]]), [rd["ck"]], [kc])
                    P.dma("act", vc.ap(F32)[sr * 32:(sr + 1) * 32, :], bass.AP(cv.tensor, off, [[16 * 1024, 32], [1, 1024]]), [rd["cv"]], [vc])
            P.copy("pool", vas4[:, tile, :, 0:64], vc.ap(F32).rearrange("p (h c) -> p h c", c=64), [vc], [VAs])
            kt3 = kts.ap(BF16).rearrange("p (f k) -> p f k", k=128)
            for half in range(2):
                pt = nextps()
                P.tr([(pt.ap[:, j * 128:(j + 1) * 128], kc.ap(F32)[:, (half * 4 + j) * 128:(half * 4 + j + 1) * 128], identF) for j in range(4)],
                     [kc, identb], [pt.r])
                P.copy("act" if half == 0 else "dve", kt3[:, half * 4:half * 4 + 4, :], pt.ap.rearrange("p (a k) -> p a k", a=4), [pt.r],
                       [kts.sub(half * 1024, (half + 1) * 1024)])
            if os.environ.get("KSA_VAR") == "a":
                continue
            P.mm([(sp4[:, tile, h, :], kt3[:, h // 2, :], qp3[:, h, b * 4:(b + 1) * 4], True, True)
                  for h in range(NH)], [kts, QP], [sp.r])
        if os.environ.get("KSA_VAR") not in ("a", "b"):
            P.mm([(sp4[0:4, 7, h, :], hsb3[:, 8 + h // 2, b * 4:(b + 1) * 4], qp3[:, h, b * 4:(b + 1) * 4], True, True)
                  for h in range(NH)], [HSb, QP], [sp.r])
        if STOP <= 2:
            continue
        P.dma("sp", vas4[0:4, 7, :, 0:64], VSs.ap(BF16)[b * 4:(b + 1) * 4, :].rearrange("p (h c) -> p h c", c=64), [VSs], [VAs])
        if STOP <= 3:
            continue
        P.act(eb4[:, 0:7, :, :], sp4[:, 0:7, :, :], AF.Exp, [sp.r], [EBs], scale=SCALE)
        P.act(eb4[0:4, 7, :, :], sp4[0:4, 7, :, :], AF.Exp, [sp.r], [EBs], scale=SCALE)
        P.tt("dve", pm4[:, 0:7, :, :], eb4[:, 0:7, :, :], wsv[:, 0:7, :, :], MUL, [EBs, WS], [PMs])
        P.tt("dve", pm4[0:4, 7, :, :], eb4[0:4, 7, :, :], wsv[0:4, 7, :, :], MUL, [EBs, WS], [PMs])
        if STOP <= 4:
            continue
        po = banks[7]
        items = []
        for h in range(NH):
            for tile in range(7):
                items.append((po.ap[0:65, h * 4:(h + 1) * 4], vas4[:, tile, h, :], pm4[:, tile, h, :], tile == 0, False))
            items.append((po.ap[0:65, h * 4:(h + 1) * 4], vas4[0:4, 7, h, :], pm4[0:4, 7, h, :], False, True))
        P.mm(items, [VAs, PMs], [po.r])
        if STOP <= 5:
            continue
        num = NUM.ap(F32)
        P.copy("dve", num[0:65, :], po.ap[0:65, 0:64], [po.r], [NUM])
        P.act(num[64:65, :], num[64:65, :], AF.Ln, [NUM], [NUM])
        P.act(num[64:65, :], num[64:65, :], AF.Exp, [NUM], [NUM], scale=-1.0)
        pb_ = nextps()
        P.mm([(pb_.ap[0:64, 0:64], onf.ap(F32)[64:65, 0:64], num[64:65, :], True, True)], [onf, NUM], [pb_.r])
        P.tt("dve", AT.ap(F32)[0:64, :], num[0:64, :], pb_.ap[0:64, 0:64], MUL, [pb_.r, NUM], [AT])
        P.dma("sp", bass.AP(BRSD.tensor, b * 4, [[16, 64], [64 * 16, 16], [1, 4]]),
              AT.ap(F32)[0:64, :].rearrange("p (h s) -> p h s", s=4), [AT], [rd["BRSD"]], slow=True)
    if STOP <= 6:
        return
    atb = M.alloc(8 * 16 * 4); sgb = M.alloc(8 * 16 * 4)
    at3 = atb.ap(F32).rearrange("p (f t) -> p f t", t=16); sg3 = sgb.ap(F32).rearrange("p (f t) -> p f t", t=16)
    P.dma("sp", at3, BRSD.rearrange("(f p) t -> p f t", p=128), [rd["BRSD"]], [atb], slow=True)
    P.act(sg3, HS3[:, 24:32, :], AF.Silu, [HS], [sgb])
    brs3 = BRS.ap(BF16).rearrange("p (f t) -> p f t", t=16)
    P.tt("dve", brs3[:, 0:8, :], at3, sg3, MUL, [atb, sgb], [BRS])
    E["sample_attn_done"] = True
```

```python
import math
import numpy as np
import ml_dtypes
import concourse.bass as bass
import concourse.mybir as mybir
from concourse.bass_utils import run_bass_kernel_spmd

F32 = mybir.dt.float32
BF16 = mybir.dt.bfloat16
AF = mybir.ActivationFunctionType
ALU = mybir.AluOpType

D = 2048
TC = 2048
NPRE = 3
NS = 16
NPROJ = 6144
NH = 16
HD = 64
CL = 2048
SCALE = HD ** -0.5
DN_ALPHA = 2.0 ** 0.25
LN_EPS = 1e-5
import os
DEBUG = bool(int(os.environ.get("KDEBUG", "0")))
DBGSET = [x for x in os.environ.get("KDBGSET", "").split(",") if x]


class R:
    __slots__ = ("name", "writer", "readers")

    def __init__(self, name):
        self.name = name
        self.writer = None
        self.readers = []


class Op:
    __slots__ = ("eng", "fn", "deps", "signal", "is_dma", "sem", "val", "prev_same_sem")

    def __init__(self, eng, fn, is_dma):
        self.eng = eng
        self.fn = fn
        self.deps = []
        self.signal = False
        self.is_dma = is_dma
        self.sem = None
        self.val = 0
        self.prev_same_sem = None


class Sched:
    ENGS = ("pe", "act", "dve", "pool", "sp")

    def __init__(self):
        self.ops = {e: [] for e in self.ENGS}
        self.all_dma = []

    def op(self, eng, fn, reads=(), writes=(), dma=False):
        o = Op(eng, fn, dma)
        deps = []
        for r in reads:
            if r.writer is not None:
                deps.append(r.writer)
        for w in writes:
            if w.writer is not None:
                deps.append(w.writer)
            deps.extend(w.readers)
        seen = set()
        for d in deps:
            if d is o or id(d) in seen:
                continue
            seen.add(id(d))
            if d.eng == "pe" and eng == "pe" and not d.is_dma and not dma:
                continue
            d.signal = True
            o.deps.append(d)
        for r in reads:
            if not dma:
                r.readers = [x for x in r.readers if x.is_dma or x.eng != eng]
            r.readers.append(o)
        for w in writes:
            w.writer = o
            w.readers = []
        if dma:
            o.signal = True
            self.all_dma.append(o)
        self.ops[eng].append(o)
        return o

    def emit(self, nc, block, eng_sems, dma_sems):
        NDS = {e: len(dma_sems[e]) for e in dma_sems}
        for e in self.ENGS:
            cnt = 0
            dcnt = 0
            last_on_sem = {}
            for o in self.ops[e]:
                if o.is_dma:
                    k = dcnt % NDS[e]
                    dcnt += 1
                    o.sem = dma_sems[e][k]
                    o.prev_same_sem = last_on_sem.get(k)
                    o.val = (o.prev_same_sem.val if o.prev_same_sem is not None else 0) + 16
                    last_on_sem[k] = o
                elif o.signal:
                    cnt += 1
                    o.sem = eng_sems[e]
                    o.val = cnt
        handles = {"pe": nc.tensor, "act": nc.scalar, "dve": nc.vector, "pool": nc.gpsimd, "sp": nc.sync}
        final_dma = {}
        for o in self.all_dma:
            final_dma[id(o.sem)] = (o.sem, max(o.val, final_dma.get(id(o.sem), (None, 0))[1]))

        def run(e):
            h = handles[e]
            waited = {}

            def wait(sem, val):
                if waited.get(id(sem), 0) >= val:
                    return
                waited[id(sem)] = val
                h.wait_ge(sem, val)

            for o in self.ops[e]:
                if o.is_dma and o.prev_same_sem is not None:
                    wait(o.prev_same_sem.sem, o.prev_same_sem.val)
                for d in o.deps:
                    wait(d.sem, d.val)
                inst = o.fn(h)
                if o.signal:
                    inst.then_inc(o.sem, 16 if o.is_dma else 1)
            if e == "sp":
                for sem, val in final_dma.values():
                    wait(sem, val)

        block.sync(lambda _e: run("sp"))
        block.tensor(lambda _e: run("pe"))
        block.scalar(lambda _e: run("act"))
        block.vector(lambda _e: run("dve"))
        block.gpsimd(lambda _e: run("pool"))


PAGE = 512
SB_BYTES = 206 * 1024


class Buf:
    def __init__(self, mem, off, nbytes, req=None):
        self.mem = mem
        self.off = off
        self.nbytes = nbytes
        self.req = nbytes if req is None else req
        self.res = mem.pages[off // PAGE:(off + nbytes + PAGE - 1) // PAGE]

    def ap(self, dtype, np_=128):
        return self.mem.big[0:np_, self.off:self.off + self.req].bitcast(dtype)

    def sub(self, b0, b1):
        return Buf(self.mem, self.off + b0, b1 - b0)


class Mem:
    def __init__(self, big):
        self.big = big
        self.pages = [R("pg%d" % i) for i in range(SB_BYTES // PAGE)]
        self.top = 0
        self.peak = 0

    def alloc(self, nbytes):
        nb = (nbytes + PAGE - 1) // PAGE * PAGE
        assert self.top + nb <= SB_BYTES, ("SBUF overflow", self.top, nb)
        b = Buf(self, self.top, nb, nbytes)
        self.top += nb
        self.peak = max(self.peak, self.top)
        return b

    def mark(self):
        return self.top

    def release(self, m):
        self.top = m


def res_of(items):
    out = []
    for it in items:
        if isinstance(it, R):
            out.append(it)
        elif isinstance(it, Buf):
            out.extend(it.res)
        else:
            out.extend(res_of(it))
    return out


class Prog:
    def __init__(self):
        self.nc = bass.Bass("TRN2", target_bir_lowering=False)
        self.S = Sched()
        self.din = {}
        self.dout = {}
        self.rdram = {}

    def inp(self, name, shape, dtype=F32):
        t = self.nc.dram_tensor(name, list(shape), dtype, kind="ExternalInput").ap()
        self.din[name] = t
        self.rdram[name] = R(name)
        return t

    def outp(self, name, shape, dtype=F32):
        t = self.nc.dram_tensor(name, list(shape), dtype, kind="ExternalOutput").ap()
        self.dout[name] = t
        self.rdram[name] = R(name)
        return t

    def scratch(self, name, shape, dtype):
        t = self.nc.dram_tensor(name, list(shape), dtype, kind="Internal").ap()
        self.rdram[name] = R(name)
        return t

    def dbg(self, name, ap, shape, dtype, reads):
        if not DEBUG:
            return
        if DBGSET and name not in DBGSET:
            return
        t = self.nc.dram_tensor("dbg_" + name, list(shape), dtype, kind="ExternalOutput").ap()
        self.rdram["dbg_" + name] = R("dbg_" + name)
        self.dma("sp", t, ap, reads, [self.rdram["dbg_" + name]], slow=True)

    def dma(self, eng, out, in_, reads, writes, slow=False):
        def fn(h):
            if slow:
                return h.dma_start(out=out, in_=in_, allow_slow_non_contiguous=True)
            return h.dma_start(out=out, in_=in_)
        return self.S.op(eng, fn, res_of(reads), res_of(writes), dma=True)

    def mm(self, items, reads, writes):
        def fn(h):
            inst = None
            for (o, l, r, st, sp) in items:
                inst = h.matmul(o, lhsT=l, rhs=r, start=st, stop=sp)
            return inst
        return self.S.op("pe", fn, res_of(reads), res_of(writes))

    def tr(self, items, reads, writes):
        def fn(h):
            inst = None
            for (o, i, idn) in items:
                inst = h.transpose(out=o, in_=i, identity=idn)
            return inst
        return self.S.op("pe", fn, res_of(reads), res_of(writes))

    def act(self, out, in_, func, reads, writes, scale=None, bias=None, eng="act"):
        def fn(h):
            kw = {}
            if scale is not None:
                kw["scale"] = scale
            if bias is not None:
                kw["bias"] = bias
            return h.activation(out=out, in_=in_, func=func, **kw)
        return self.S.op(eng, fn, res_of(reads), res_of(writes))

    def copy(self, eng, out, in_, reads, writes):
        if eng == "act":
            return self.act(out, in_, AF.Copy, reads, writes)

        def fn(h):
            return h.tensor_copy(out=out, in_=in_)
        return self.S.op(eng, fn, res_of(reads), res_of(writes))

    def tt(self, eng, out, in0, in1, op, reads, writes):
        def fn(h):
            return h.tensor_tensor(out=out, in0=in0, in1=in1, op=op)
        return self.S.op(eng, fn, res_of(reads), res_of(writes))

    def ts(self, eng, out, in0, s1, s2, op0, op1, reads, writes):
        def fn(h):
            if op1 is None:
                return h.tensor_scalar(out=out, in0=in0, scalar1=s1, scalar2=None, op0=op0)
            return h.tensor_scalar(out=out, in0=in0, scalar1=s1, scalar2=s2, op0=op0, op1=op1)
        return self.S.op(eng, fn, res_of(reads), res_of(writes))

    def stt(self, out, in0, scalar, in1, op0, op1, reads, writes):
        def fn(h):
            return h.scalar_tensor_tensor(out=out, in0=in0, scalar=scalar, in1=in1, op0=op0, op1=op1)
        return self.S.op("dve", fn, res_of(reads), res_of(writes))

    def memset(self, eng, ap, val, writes):
        def fn(h):
            return h.memset(ap, val)
        return self.S.op(eng, fn, [], res_of(writes))

    def recip(self, out, in_, reads, writes):
        def fn(h):
            return h.reciprocal(out=out, in_=in_)
        return self.S.op("dve", fn, res_of(reads), res_of(writes))


U8 = mybir.dt.uint8


class PSBank:
    def __init__(self, ap, r):
        self.ap = ap
        self.r = r


def build_program(stage=99, npre_tiles=None):
    from contextlib import ExitStack
    P = Prog()
    nc = P.nc
    rd = P.rdram
    xo = P.inp("xo", [TC, D]); xh = P.inp("xh", [TC, D]); xp = P.inp("xp", [NPRE * TC, D]); xs = P.inp("xs", [NS, D])
    w_in = P.inp("w_in", [D, NPROJ]); w_out = P.inp("w_out", [D, D]); w_glu = P.inp("w_glu", [1024, 1024])
    ident = P.inp("ident", [128, 128])
    k_o = P.outp("k_o", [TC, 1024]); v_o = P.outp("v_o", [TC, 1024])
    k_s = P.outp("k_s", [NS, 1024]); v_s = P.outp("v_s", [NS, 1024])
    y_o = P.outp("y_o", [TC, D]); y_s = P.outp("y_s", [NS, D])
    sre_o = P.outp("sre_o", [64, 64]); sim_o = P.outp("sim_o", [64, 64])
    sre_s = P.outp("sre_s", [4, 64, 64]); sim_s = P.outp("sim_s", [4, 64, 64])
    lam_re = P.inp("lam_re", [64, 64]); lam_im = P.inp("lam_im", [64, 64]); log_dt = P.inp("log_dt", [1, 64])
    b_re = P.inp("b_re", [64, 64, 16]); b_im = P.inp("b_im", [64, 64, 16])
    c_re = P.inp("c_re", [64, 16, 64]); c_im = P.inp("c_im", [64, 16, 64])
    d_skip = P.inp("d_skip", [1024, 1]); b_glu = P.inp("b_glu", [1024, 1])
    st_re = P.inp("st_re", [4, 64, 64]); st_im = P.inp("st_im", [4, 64, 64])
    cmask = P.inp("cmask", [128, 512]); ctri = P.inp("ctri", [128, 128]); ctris = P.inp("ctris", [16, 16])
    UT = P.scratch("UT", [1024, TC], BF16); GST = P.scratch("GST", [1024, TC], BF16)
    BR = P.scratch("BR", [2048, TC], BF16)
    GV = P.scratch("GV", [48, 384], F32)
    BRSD = P.scratch("BRSD", [1024, NS], F32)
    ck = P.inp("ck", [4, CL, 1024]); cv = P.inp("cv", [4, CL, 1024]); ccnt = P.inp("ccnt", [32, 32 * 128])
    rel_bias = P.inp("rel_bias", [32, 16]); coh = P.inp("coh", [32, 3 * 384]); cj = P.inp("cj", [128, 128]); hv = P.inp("hv", [128, 1])
    b_out = P.inp("b_out", [1, D]); ln_g = P.inp("ln_g", [1, D]); ln_b = P.inp("ln_b", [1, D])
    QT = P.scratch("QT", [1024, TC], BF16); GT = P.scratch("GT", [1024, TC], BF16)
    KT = P.scratch("KT", [1024, 2 * TC], BF16); VS = P.scratch("VS", [2 * TC, 1024], BF16)

    es = ExitStack()
    with es:
        big = es.enter_context(nc.sbuf_tensor("big", [128, SB_BYTES], U8))
        banks = []
        for i in range(8):
            t = es.enter_context(nc.psum_tensor("ps%d" % i, [128, 512], F32))
            banks.append(PSBank(t[:, :], R("ps%d" % i)))
        eng_sems = {e: es.enter_context(nc.semaphore("sem_" + e)) for e in Sched.ENGS}
        dma_sems = {e: [es.enter_context(nc.semaphore("dsem_%s%d" % (e, i))) for i in range(n)]
                    for e, n in (("sp", 24), ("act", 8), ("pool", 16), ("dve", 2), ("pe", 2))}
        M = Mem(big)
        psi = [0]

        def nextps():
            b = banks[psi[0] % 8]
            psi[0] += 1
            return b

        identb = M.alloc(128 * 4)
        identF = identb.ap(F32)
        P.dma("sp", identF, ident[:, :], [rd["ident"]], [identb])
        HS = M.alloc(48 * NS * 4)
        HS3 = HS.ap(F32).rearrange("p (f t) -> p f t", t=NS)
        XTs = M.alloc(16 * NS * 2)
        XTs3 = XTs.ap(BF16).rearrange("p (k t) -> p k t", t=NS)
        VSs = M.alloc(1024 * 2)
        BRS = M.alloc(16 * NS * 2)

        def sl_ap0(t, off=0):
            return bass.AP(t.tensor, off, [[1, 128], [128, 32]])
        kin = M.alloc(3 * 128); kina = kin.ap(F32)
        bin_ = [M.alloc(32 * 16 * 4), M.alloc(32 * 16 * 4)]
        x0in = [M.alloc(32 * 4 * 4), M.alloc(32 * 4 * 4)]
        dskb = M.alloc(32); bglb = M.alloc(32)

        def early_prefetch_list():
            L_ = []
            L_.append(lambda: P.dma("act", kina[:, 0:32], sl_ap0(lam_re), [rd["lam_re"]], [kin], slow=True))
            L_.append(lambda: P.dma("act", kina[:, 32:64], sl_ap0(lam_im), [rd["lam_im"]], [kin], slow=True))
            for g2 in range(2):
                L_.append(lambda g2=g2: P.dma("act", kina[g2 * 64:(g2 + 1) * 64, 64:96], bass.AP(log_dt.tensor, g2, [[0, 64], [2, 32]]),
                                              [rd["log_dt"]], [kin], slow=True))
            for i, (nm, t_) in enumerate((("b_re", b_re), ("b_im", b_im))):
                L_.append(lambda i=i, nm=nm, t_=t_: P.dma("act", bin_[i].ap(F32).rearrange("p (a c) -> p a c", c=16),
                                                          bass.AP(t_.tensor, 0, [[16, 128], [2048, 32], [1, 16]]), [rd[nm]], [bin_[i]], slow=True))
            for ri, (nm, t_) in enumerate((("st_re", st_re), ("st_im", st_im))):
                for b in range(4):
                    L_.append(lambda ri=ri, nm=nm, t_=t_, b=b: P.dma("act", x0in[ri].ap(F32).rearrange("p (a b) -> p a b", b=4)[:, :, b],
                                                                     sl_ap0(t_, b * 4096), [rd[nm]], [x0in[ri]], slow=True))
            L_.append(lambda: P.dma("act", dskb.ap(F32), d_skip.rearrange("(o p) one -> p (o one)", p=128), [rd["d_skip"]], [dskb], slow=True))
            L_.append(lambda: P.dma("act", bglb.ap(F32), b_glu.rearrange("(o p) one -> p (o one)", p=128), [rd["b_glu"]], [bglb], slow=True))
            return L_
        prefetch_q = early_prefetch_list()

        def load_xT(src, src_r, ntiles, XT, xst):
            XT4 = XT.ap(BF16).rearrange("p (t k c) -> p t k c", k=16, c=128)
            for tt in range(ntiles):
                xb = xst[tt % 2]
                xa = xb.ap(F32)
                P.dma("sp", xa, src[tt * 128:(tt + 1) * 128, :], [src_r], [xb])
                for q in range(4):
                    ps = nextps()
                    P.tr([(ps.ap[:, j * 128:(j + 1) * 128], xa[:, (4 * q + j) * 128:(4 * q + j + 1) * 128], identF)
                          for j in range(4)], [xb, identb], [ps.r])
                    P.copy("act" if q % 2 == 0 else "dve", XT4[:, tt, 4 * q:4 * q + 4, :],
                           ps.ap.rearrange("p (a b) -> p a b", a=4), [ps.r], [XT.sub(tt * 4096 + q * 1024, tt * 4096 + (q + 1) * 1024)])
            return XT4

        def load_wtile(wb, c0, ncols):
            w3 = wb.ap(BF16).rearrange("p (k f) -> p k f", k=16)
            P.dma("pool", w3, w_in[:, c0:c0 + ncols].rearrange("(k p) f -> p k f", p=128), [rd["w_in"]], [wb])
            return w3

        def mm_fm(ps, w3, wb, XT4, XT, tb):
            P.mm([(ps.ap[:, :], w3[:, kt, :], XT4[:, 4 * tb:4 * tb + 4, kt, :], kt == 0, kt == 15) for kt in range(16)],
                 [XT.sub(tb * 16384, (tb + 1) * 16384), wb], [ps.r])

        def mm_fm_sample(ps, w3, wb):
            P.mm([(ps.ap[:, 0:NS], w3[:, kt, :], XTs3[:, kt, :], kt == 0, kt == 15) for kt in range(16)],
                 [XTs, wb], [ps.r])

        m0 = M.mark()
        xsb = M.alloc(D * 4)
        xsa = xsb.ap(F32)
        P.dma("sp", xsa[0:NS, :], xs[:, :], [rd["xs"]], [xsb])
        ps = nextps()
        P.tr([(ps.ap[:, kt * NS:(kt + 1) * NS], xsa[0:NS, kt * 128:(kt + 1) * 128], identF[0:NS, 0:NS]) for kt in range(16)],
             [xsb, identb], [ps.r])
        P.copy("dve", XTs3[:, :, :], ps.ap[:, 0:16 * NS].rearrange("p (k t) -> p k t", t=NS), [ps.r], [XTs])
        M.release(m0)

        mB1 = M.mark()
        XT = M.alloc(16 * TC * 2)
        xst = [M.alloc(D * 4), M.alloc(D * 4)]
        wt = [M.alloc(16 * 128 * 2), M.alloc(16 * 128 * 2)]
        wblk = [M.alloc(16 * 512 * 2), M.alloc(16 * 512 * 2)]
        stg = [M.alloc(TC * 2), M.alloc(TC * 2)]
        ost = [M.alloc(512 * 4), M.alloc(512 * 4)]
        vbs = [M.alloc(512 * 2), M.alloc(512 * 2)]
        wi = [0]
        evq = [0]

        def ev_eng():
            evq[0] += 1
            return "act" if evq[0] % 2 == 0 else "dve"

        for grp in ("halo", "own"):
            src, src_r = (xh, rd["xh"]) if grp == "halo" else (xo, rd["xo"])
            XT4 = load_xT(src, src_r, 16, XT, xst)
            tok0 = 0 if grp == "halo" else TC
            fts = [("k", 8 + i) for i in range(8)]
            if grp == "own":
                fts = ([("q", i) for i in range(8)] + fts + [("g", 24 + i) for i in range(8)]
                       + [("u", 32 + i) for i in range(8)] + [("gs", 40 + i) for i in range(8)])
            for (kind, ft) in fts:
                wb = wt[wi[0] % 2]
                w3 = load_wtile(wb, ft * 128, 128)
                sb = stg[wi[0] % 2]
                wi[0] += 1
                sa = sb.ap(BF16)
                for tb in range(4):
                    ps = nextps()
                    mm_fm(ps, w3, wb, XT4, XT, tb)
                    P.copy(ev_eng(), sa[:, tb * 512:(tb + 1) * 512], ps.ap[:, :], [ps.r], [sb.sub(tb * 1024, (tb + 1) * 1024)])
                if kind == "q":
                    P.dma("sp", QT[ft * 128:(ft + 1) * 128, :], sa, [sb], [rd["QT"]])
                elif kind == "k":
                    r0 = (ft - 8) * 128
                    P.dma("sp", KT[r0:r0 + 128, tok0:tok0 + TC], sa, [sb], [rd["KT"]])
                elif kind == "g":
                    r0 = (ft - 24) * 128
                    P.dma("sp", GT[r0:r0 + 128, :], sa, [sb], [rd["GT"]])
                elif kind == "u":
                    r0 = (ft - 32) * 128
                    P.dma("sp", UT[r0:r0 + 128, :], sa, [sb], [rd["UT"]])
                else:
                    r0 = (ft - 40) * 128
                    P.dma("sp", GST[r0:r0 + 128, :], sa, [sb], [rd["GST"]])
                if grp == "own":
                    ps = nextps()
                    mm_fm_sample(ps, w3, wb)
                    P.copy(ev_eng(), HS3[:, ft, :], ps.ap[:, 0:NS], [ps.r], [HS])
                    if prefetch_q:
                        prefetch_q.pop(0)()
            blks = [("v", 2048 + 512 * i) for i in range(2)]
            if grp == "own":
                blks = [("k", 1024 + 512 * i) for i in range(2)] + blks
            for bi, (kind, c0) in enumerate(blks):
                wb = wblk[bi % 2]
                w3 = load_wtile(wb, c0, 512)
                fcol = (c0 % 1024)
                for tt in range(16 + (1 if grp == "own" else 0)):
                    ps = nextps()
                    if tt < 16:
                        P.mm([(ps.ap[:, :], XT4[:, tt, kt, :], w3[:, kt, :], kt == 0, kt == 15) for kt in range(16)],
                             [XT.sub(tt * 4096, (tt + 1) * 4096), wb], [ps.r])
                        if grp == "own":
                            ob = ost[tt % 2]
                            P.copy("act", ob.ap(F32), ps.ap[:, :], [ps.r], [ob])
                            dst = k_o if kind == "k" else v_o
                            P.dma("sp", dst[tt * 128:(tt + 1) * 128, fcol:fcol + 512], ob.ap(F32), [ob], [rd[dst.tensor.name]])
                            if kind == "v":
                                vb = vbs[tt % 2]
                                P.copy("pool", vb.ap(BF16), ob.ap(F32), [ob], [vb])
                                P.dma("sp", VS[TC + tt * 128:TC + (tt + 1) * 128, fcol:fcol + 512], vb.ap(BF16), [vb], [rd["VS"]])
                        else:
                            vb = vbs[tt % 2]
                            P.copy(ev_eng(), vb.ap(BF16), ps.ap[:, :], [ps.r], [vb])
                            P.dma("sp", VS[tt * 128:(tt + 1) * 128, fcol:fcol + 512], vb.ap(BF16), [vb], [rd["VS"]])
                    else:
                        P.mm([(ps.ap[0:NS, :], XTs3[:, kt, :], w3[:, kt, :], kt == 0, kt == 15) for kt in range(16)],
                             [XTs, wb], [ps.r])
                        ob = ost[tt % 2]
                        P.copy("act", ob.ap(F32)[0:NS, :], ps.ap[0:NS, :], [ps.r], [ob])
                        dst = k_s if kind == "k" else v_s
                        P.dma("sp", dst[:, fcol:fcol + 512], ob.ap(F32)[0:NS, :], [ob], [rd[dst.tensor.name]])
                        if kind == "v":
                            P.copy("pool", VSs.ap(BF16)[0:NS, fcol:fcol + 512], ob.ap(F32)[0:NS, :], [ob], [VSs])
        while prefetch_q:
            prefetch_q.pop(0)()
        M.release(mB1)

        if stage >= 2:
            if npre_tiles is not None:
                pass
            _E = dict(locals())
            if npre_tiles is not None:
                _E["npre_tiles"] = npre_tiles
            mS = M.mark()
            ssm_phase(_E)
            M.release(mS)
        if stage >= 3:
            _E = dict(locals())
            m3 = M.mark()
            attn_tables(_E)
            m4 = M.mark()
            attn_phase(_E)
            M.release(m4)
            if not os.environ.get("KSKIP_SA"):
                sample_attn_phase(_E)
            M.release(m3)
            out_phase(_E)

        block = es.enter_context(nc.Block())
        P.S.emit(nc, block, eng_sems, dma_sems)
    print("ops:", {e: len(v) for e, v in P.S.ops.items()}, "sbuf peak", M.peak)
    return nc


def host_inputs(inputs):
    xpr = inputs["x_prompt"]
    maps = []
    zeros_chunk = np.zeros((TC, D), np.float32)
    for core in range(8):
        b, c = divmod(core, 4)
        xo = np.ascontiguousarray(xpr[b, c * TC:(c + 1) * TC])
        xh = np.ascontiguousarray(xpr[b, (c - 1) * TC:c * TC]) if c > 0 else zeros_chunk
        pre = []
        for j in range(NPRE):
            cc = c - NPRE + j
            pre.append(xpr[b, cc * TC:(cc + 1) * TC] if cc >= 0 else zeros_chunk)
        xp = np.ascontiguousarray(np.concatenate(pre, axis=0))
        xs = np.ascontiguousarray(inputs["x_sample"][core * 4:(core + 1) * 4].reshape(NS, D))
        m = {
            "xo": xo, "xh": xh, "xp": xp, "xs": xs,
            "w_in": np.ascontiguousarray(inputs["w_in"][0]),
            "w_out": np.ascontiguousarray(inputs["w_out"][0]),
            "w_glu": np.ascontiguousarray(inputs["w_glu"][0]),
            "ident": np.eye(128, dtype=np.float32),
            "lam_re": np.ascontiguousarray(inputs["lam_re"][0]), "lam_im": np.ascontiguousarray(inputs["lam_im"][0]),
            "log_dt": np.ascontiguousarray(inputs["log_dt"][0][None, :]),
            "b_re": np.ascontiguousarray(inputs["b_re"][0]), "b_im": np.ascontiguousarray(inputs["b_im"][0]),
            "c_re": np.ascontiguousarray(inputs["c_re"][0]), "c_im": np.ascontiguousarray(inputs["c_im"][0]),
            "d_skip": np.ascontiguousarray(inputs["d_skip"][0][:, None]), "b_glu": np.ascontiguousarray(inputs["b_glu"][0][:, None]),
            "st_re": np.ascontiguousarray(inputs["state_ssm_re"][0, core * 4:(core + 1) * 4]),
            "st_im": np.ascontiguousarray(inputs["state_ssm_im"][0, core * 4:(core + 1) * 4]),
            "cmask": CMASK, "ctri": CTRI, "ctris": CTRIS,
            "rel_bias": np.ascontiguousarray(inputs["rel_bias"]), "coh": COH, "cj": CJ,
            "hv": np.full((128, 1), 1.0 if c > 0 else 0.0, np.float32),
            "ck": np.ascontiguousarray(inputs["cache_k"][0, core * 4:(core + 1) * 4].reshape(4, CL, 1024)),
            "cv": np.ascontiguousarray(inputs["cache_v"][0, core * 4:(core + 1) * 4].reshape(4, CL, 1024)),
            "ccnt": CCNT,
            "b_out": np.ascontiguousarray(inputs["b_out"][0][None, :]),
            "ln_g": np.ascontiguousarray(inputs["ln_g"][0][None, :]), "ln_b": np.ascontiguousarray(inputs["ln_b"][0][None, :]),
        }
        maps.append(m)
    return maps


def _consts():
    cm = np.zeros((128, 4, 128), np.float32)
    for p in range(128):
        g2 = p // 64
        for q in range(4):
            cm[p, q, q * 32 + g2 * 16:q * 32 + g2 * 16 + 16] = 1.0
    tri = np.triu(np.ones((128, 128), np.float32))
    tris = np.zeros((16, 16), np.float32)
    for b in range(4):
        tris[b * 4:(b + 1) * 4, b * 4:(b + 1) * 4] = np.triu(np.ones((4, 4), np.float32))
    return cm.reshape(128, 512), tri, tris


def _t5_bucket(dist):
    dist = np.asarray(dist, np.int64)
    d_f = np.maximum(dist, 1).astype(np.float32)
    large = 16 + (np.log(d_f / np.float32(16.0)) / np.float32(math.log(2048 / 16)) * np.float32(16.0)).astype(np.int32)
    large = np.minimum(large, 31)
    return np.where(dist < 16, dist, large)


def _attn_consts():
    oh = np.zeros((32, 3, 384), np.float32)
    for p, d in enumerate((1, 4, 16)):
        for j in range(129):
            oh[_t5_bucket(j * d), p, 127 + j] = 1.0
    cj = np.ascontiguousarray(np.eye(128, dtype=np.float32)[::-1])
    return oh.reshape(32, 3 * 384), cj


def _sample_consts():
    cnt = np.zeros((32, 8, 4, 128), np.float32)
    for tile in range(8):
        for k in range(128):
            if tile < 4:
                idx = 1536 + 128 * tile + k
            elif tile < 7:
                sr, mm = divmod(k, 32)
                idx = 16 * (32 * (tile - 4) + mm) + sr
            else:
                if k >= 4:
                    continue
                idx = CL + k
            for s in range(4):
                dd = CL + s - idx
                if dd < 0:
                    continue
                mult = (1 if dd <= 128 else 0) + (1 if (dd % 4 == 0 and dd <= 512) else 0) + (1 if (dd % 16 == 0 and dd <= 2048) else 0)
                if mult:
                    cnt[int(_t5_bucket(dd)), tile, s, k] += mult
    return cnt.reshape(32, 32 * 128)


CMASK, CTRI, CTRIS = _consts()
COH, CJ = _attn_consts()
CCNT = _sample_consts()
_NC_CACHE = {}


def kernel(**inputs):
    inputs = {k: np.asarray(v) for k, v in inputs.items()}
    if "nc" not in _NC_CACHE:
        _NC_CACHE["nc"] = build_program()
    nc = _NC_CACHE["nc"]
    maps = host_inputs(inputs)
    res = run_bass_kernel_spmd(nc, maps, core_ids=list(range(8)))
    rs = res.results
    B = 2
    y_prompt = np.stack([np.concatenate([rs[b * 4 + c]["y_o"] for c in range(4)], axis=0) for b in range(B)])
    y_sample = np.concatenate([rs[i]["y_s"].reshape(4, 4, D) for i in range(8)], axis=0)
    k_prompt = np.stack([rs[b * 4 + 3]["k_o"].reshape(TC, NH, HD) for b in range(B)])[None]
    v_prompt = np.stack([rs[b * 4 + 3]["v_o"].reshape(TC, NH, HD) for b in range(B)])[None]
    sre_p = np.stack([rs[b * 4 + 3]["sre_o"] for b in range(B)])[None]
    sim_p = np.stack([rs[b * 4 + 3]["sim_o"] for b in range(B)])[None]
    k_sample = np.concatenate([rs[i]["k_s"].reshape(4, 4, NH, HD) for i in range(8)], axis=0)[None]
    v_sample = np.concatenate([rs[i]["v_s"].reshape(4, 4, NH, HD) for i in range(8)], axis=0)[None]
    sre_s = np.concatenate([rs[i]["sre_s"] for i in range(8)], axis=0)[None]
    sim_s = np.concatenate([rs[i]["sim_s"] for i in range(8)], axis=0)[None]
    outs = (y_prompt, y_sample, k_prompt, v_prompt, sre_p, sim_p, k_sample, v_sample, sre_s, sim_s)
    return tuple(np.ascontiguousarray(o, dtype=np.float32) for o in outs)


def ssm_phase(E):
    P, M, rd, banks = E["P"], E["M"], E["rd"], E["banks"]
    identF, identb = E["identF"], E["identb"]
    HS3, HS = E["HS3"], E["HS"]
    xp, w_in, w_glu = E["xp"], E["w_in"], E["w_glu"]
    UT, GST, BR = E["UT"], E["GST"], E["BR"]
    BRS = E["BRS"]
    PI = math.pi
    MUL, ADD, SUB = ALU.mult, ALU.add, ALU.subtract
    rot = [0]

    def nextps():
        b = banks[rot[0] % 5]
        rot[0] += 1
        return b
    ZS, YP0, YP1 = banks[7], banks[5], banks[6]

    Mtab = M.alloc(32 * 256 * 2); Mt3 = Mtab.ap(BF16).rearrange("p (a c) -> p a c", c=256)
    Brhs = M.alloc(32 * 256 * 2); Brhs3 = Brhs.ap(BF16).rearrange("p (a c) -> p a c", c=256)
    onesb = M.alloc(4); ones = onesb.ap(BF16)[:, 0:1]; nones = onesb.ap(BF16)[:, 1:2]
    P.memset("dve", onesb.ap(BF16)[:, 0:1], 1.0, [onesb])
    P.memset("dve", onesb.ap(BF16)[:, 1:2], -1.0, [onesb])
    PTr = M.alloc(32 * 128 * 2); PTi = M.alloc(32 * 128 * 2)
    Clr = M.alloc(32 * 128 * 2); Cli = M.alloc(32 * 128 * 2)
    Clr3 = Clr.ap(BF16).rearrange("p (a c) -> p a c", c=128); Cli3 = Cli.ap(BF16).rearrange("p (a c) -> p a c", c=128)
    wglb = M.alloc(8 * 1024 * 2); wgl3 = wglb.ap(BF16).rearrange("p (i j) -> p i j", j=1024)
    P.dma("pool", wgl3, w_glu.rearrange("(i p) j -> p i j", p=128), [rd["w_glu"]], [wglb])
    trib = M.alloc(256); tri = trib.ap(BF16)
    P.dma("pool", tri, E["ctri"][:, :], [rd["ctri"]], [trib])
    ntrib = M.alloc(256); ntri = ntrib.ap(BF16)
    P.ts("dve", ntri, tri, -1.0, None, ALU.mult, None, [trib], [ntrib])
    kp = M.alloc(26 * 128); kpa = kp.ap(F32)
    KEEP = ["abr", "abi", "A128r", "A128i", "k1", "k2", "k3", "k4", "f1", "f2", "fr", "fi", "alr", "ali", "q1", "q2", "q3", "air", "aii",
            "junkr", "junki", "p127r", "p127i"]
    cbuf = M.alloc(64 * 4); c2 = cbuf.ap(F32)
    P.memset("dve", c2, 0.0, [cbuf])
    zcb = M.alloc(64 * 4); zc2 = zcb.ap(F32)

    mprep = M.mark()
    pp = M.alloc(72 * 128)
    ppa = pp.ap(F32)
    names = {}

    def slot(n):
        if n in KEEP:
            i = KEEP.index(n)
            return kpa[:, i * 32:(i + 1) * 32], kp
        if n not in names:
            names[n] = len(names)
            assert len(names) <= 72
        i = names[n]
        return ppa[:, i * 32:(i + 1) * 32], pp

    def s_(n):
        return slot(n)[0]

    def sb_(*ns):
        return [slot(n)[1] for n in ns]

    def tt(o, a, b, op):
        P.tt("dve", s_(o), s_(a), s_(b), op, sb_(a, b), sb_(o))

    def tsc(o, a, c1, op0, c2_=None, op1=None):
        P.ts("dve", s_(o), s_(a), c1, c2_, op0, op1, sb_(a), sb_(o))

    def sl_ap(t, off=0):
        return bass.AP(t.tensor, off, [[1, 128], [128, 32]])

    kin = E["kin"]
    P.copy("dve", s_("lamre"), kin.ap(F32)[:, 0:32], [kin], [pp])
    P.copy("dve", s_("lamim"), kin.ap(F32)[:, 32:64], [kin], [pp])
    P.copy("dve", s_("logdt"), kin.ap(F32)[:, 64:96], [kin], [pp])
    tsc("lr", "lamre", -1e-4, ALU.min)
    tsc("x16", "logdt", 1.0 / 16.0, MUL)
    P.memset("dve", s_("pe"), 1.0, [pp])
    for k in range(10, 0, -1):
        tt("pe", "pe", "x16", MUL)
        tsc("pe", "pe", 1.0 / k, MUL, 1.0, ADD)
    for _ in range(4):
        tt("pe", "pe", "pe", MUL)
    tt("a", "lr", "pe", MUL)
    tt("th", "lamim", "pe", MUL)
    P.act(s_("mag"), s_("a"), AF.Exp, [pp], [pp])
    P.act(s_("magi"), s_("a"), AF.Exp, [pp], [pp], scale=-1.0)
    tsc("thc", "th", PI / 2, ADD)
    for nm in ("th", "thc"):
        for _ in range(4):
            tsc("m", nm, PI, ALU.is_gt)
            P.stt(s_(nm), s_("m"), -2.0 * PI, s_(nm), MUL, ADD, [pp], [pp])
    P.act(s_("sn"), s_("th"), AF.Sin, [pp], [pp])
    P.act(s_("cs"), s_("thc"), AF.Sin, [pp], [pp])
    tt("abr", "mag", "cs", MUL); tt("abi", "mag", "sn", MUL)
    tt("air", "magi", "cs", MUL)
    P.stt(s_("aii"), s_("magi"), -1.0, s_("sn"), MUL, MUL, [pp], [kp])
    tt("t1", "lr", "lr", MUL); tt("t2", "lamim", "lamim", MUL); tt("den", "t1", "t2", ADD)
    P.recip(s_("rden"), s_("den"), [pp], [pp])
    tt("invr", "lr", "rden", MUL)
    P.stt(s_("invi"), s_("lamim"), -1.0, s_("rden"), MUL, MUL, [pp], [pp])
    tsc("nr", "abr", -1.0, ADD)
    tt("t1", "nr", "invr", MUL); tt("t2", "abi", "invi", MUL); tt("cfr", "t1", "t2", SUB)
    tt("t1", "nr", "invi", MUL); tt("t2", "abi", "invr", MUL); tt("cfi", "t1", "t2", ADD)
    tsc("A128r", "abr", 1.0, MUL); tsc("A128i", "abi", 1.0, MUL)
    for _ in range(7):
        tt("q1", "A128r", "A128r", MUL); tt("q2", "A128i", "A128i", MUL); tt("q3", "A128r", "A128i", MUL)
        tt("A128r", "q1", "q2", SUB); tsc("A128i", "q3", 2.0, MUL)

    for nm in ("pe", "abr", "abi", "cfr", "cfi", "A128r", "A128i", "air", "aii"):
        P.dbg(nm, s_(nm), [128, 32], F32, sb_(nm))

    evq = [0]

    def ev_eng():
        evq[0] += 1
        return "act" if evq[0] % 2 == 0 else "dve"

    def to_time_major(dst3, dstbuf, src3, srcbuf, coff, NT):
        for pr0 in range(0, 32, 4):
            ps = nextps()
            P.tr([(ps.ap[0:NT, j * 128:(j + 1) * 128], src3[:, pr0 + j, 0:NT], identF) for j in range(4)],
                 [srcbuf, identb], [ps.r])
            P.copy(ev_eng(), dst3[0:NT, pr0:pr0 + 4, coff:coff + 128],
                   ps.ap[0:NT, :].rearrange("p (a b) -> p a b", a=4), [ps.r], [dstbuf])

    m1 = M.mark()
    cmb = M.alloc(512 * 4); cm3 = cmb.ap(F32).rearrange("p (q c) -> p q c", c=128)
    P.dma("sp", cmb.ap(F32), E["cmask"][:, :], [rd["cmask"]], [cmb])
    bb = list(E["bin_"]) + [M.alloc(32 * 16 * 4) for _ in range(4)]
    b3 = [b.ap(F32).rearrange("p (a c) -> p a c", c=16) for b in bb]
    cfr_b = s_("cfr").unsqueeze(2).to_broadcast([128, 32, 16]); cfi_b = s_("cfi").unsqueeze(2).to_broadcast([128, 32, 16])
    P.tt("dve", b3[2], b3[0], cfr_b, MUL, [bb[0], pp], [bb[2]]); P.tt("dve", b3[3], b3[1], cfi_b, MUL, [bb[1], pp], [bb[3]])
    P.tt("dve", b3[4], b3[2], b3[3], SUB, [bb[2], bb[3]], [bb[4]])
    P.tt("dve", b3[2], b3[1], cfr_b, MUL, [bb[1], pp], [bb[2]]); P.tt("dve", b3[3], b3[0], cfi_b, MUL, [bb[0], pp], [bb[3]])
    P.tt("dve", b3[5], b3[2], b3[3], ADD, [bb[2], bb[3]], [bb[5]])
    YW = [M.alloc(32 * 128 * 4), M.alloc(32 * 128 * 4)]
    YW3 = [y.ap(F32).rearrange("p (a c) -> p a c", c=128) for y in YW]
    for ri in range(2):
        for o in range(8):
            outv = YW3[ri][:, 4 * o:4 * o + 4, :].rearrange("p q (r c) -> p q r c", c=16)
            in0 = b3[4 + ri][:, 4 * o:4 * o + 4, :].unsqueeze(2).to_broadcast([128, 4, 8, 16])
            in1 = cm3.rearrange("p q (r c) -> p q r c", c=16)
            P.tt("dve", outv, in0, in1, MUL, [bb[4 + ri], cmb], [YW[ri]])
    to_time_major(Brhs3, Brhs, YW3[0], YW[0], 0, 128)
    to_time_major(Brhs3, Brhs, YW3[1], YW[1], 128, 128)
    P.dbg("bbr", b3[4], [128, 32, 16], F32, [bb[4]])
    P.dbg("yw0", YW3[0], [128, 32, 128], F32, [YW[0]])
    P.dbg("brhs", Brhs3, [128, 32, 256], BF16, [Brhs])
    M.release(m1)

    def power_tables(tabs):
        for T in tabs:
            T["Tr3"] = T["Tr"].ap(F32).rearrange("p (a t) -> p a t", t=128); T["Ti3"] = T["Ti"].ap(F32).rearrange("p (a t) -> p a t", t=128)
            T["tA3"] = T["tA"].ap(F32).rearrange("p (a t) -> p a t", t=64); T["tB3"] = T["tB"].ap(F32).rearrange("p (a t) -> p a t", t=64)
            T["sc"] = [M.alloc(128) for _ in range(5)]
            P.memset("dve", T["Tr3"][:, :, 0:1], 1.0, [T["Tr"]]); P.memset("dve", T["Ti3"][:, :, 0:1], 0.0, [T["Ti"]])
            P.ts("dve", T["sc"][0].ap(F32), s_(T["br"]), 1.0, None, MUL, None, [kp], [T["sc"][0]])
            P.ts("dve", T["sc"][1].ap(F32), s_(T["bi"]), 1.0, None, MUL, None, [kp], [T["sc"][1]])
        L = 1
        while L < 128:
            for T in tabs:
                alr, ali, q1, q2, q3 = T["sc"]
                Tr, Ti, tA, tB = T["Tr"], T["Ti"], T["tA"], T["tB"]
                br = alr.ap(F32).unsqueeze(2).to_broadcast([128, 32, L]); bi = ali.ap(F32).unsqueeze(2).to_broadcast([128, 32, L])
                sr = T["Tr3"][:, :, 0:L]; si = T["Ti3"][:, :, 0:L]
                t1 = T["tA3"][:, :, 0:L]; t2 = T["tB3"][:, :, 0:L]
                P.tt("dve", t1, sr, br, MUL, [Tr, alr], [tA]); P.tt("dve", t2, si, bi, MUL, [Ti, ali], [tB])
                P.tt("dve", T["Tr3"][:, :, L:2 * L], t1, t2, SUB, [tA, tB], [Tr])
                P.tt("dve", t1, sr, bi, MUL, [Tr, ali], [tA]); P.tt("dve", t2, si, br, MUL, [Ti, alr], [tB])
                P.tt("dve", T["Ti3"][:, :, L:2 * L], t1, t2, ADD, [tA, tB], [Ti])
                P.tt("dve", q1.ap(F32), alr.ap(F32), alr.ap(F32), MUL, [alr], [q1])
                P.tt("dve", q2.ap(F32), ali.ap(F32), ali.ap(F32), MUL, [ali], [q2])
                P.tt("dve", q3.ap(F32), alr.ap(F32), ali.ap(F32), MUL, [alr, ali], [q3])
                P.tt("dve", alr.ap(F32), q1.ap(F32), q2.ap(F32), SUB, [q1, q2], [alr])
                P.ts("dve", ali.ap(F32), q3.ap(F32), 2.0, None, MUL, None, [q3], [ali])
            L *= 2
    M.release(mprep)
    m1 = M.mark()
    TM = dict(Tr=M.alloc(32 * 128 * 4), Ti=M.alloc(32 * 128 * 4), tA=M.alloc(32 * 64 * 4), tB=M.alloc(32 * 64 * 4), br="air", bi="aii")
    TP = dict(Tr=M.alloc(32 * 128 * 4), Ti=M.alloc(32 * 128 * 4), tA=M.alloc(32 * 64 * 4), tB=M.alloc(32 * 64 * 4), br="abr", bi="abi")
    power_tables([TM, TP])
    MIr, MIi, MIr3, MIi3 = TM["Tr"], TM["Ti"], TM["Tr3"], TM["Ti3"]
    to_time_major(Mt3, Mtab, MIr3, MIr, 0, 128)
    to_time_major(Mt3, Mtab, MIi3, MIi, 128, 128)
    PTrf, PTif, PTrf3, PTif3 = TP["Tr"], TP["Ti"], TP["Tr3"], TP["Ti3"]
    PTr3 = PTr.ap(BF16).rearrange("p (a t) -> p a t", t=128); PTi3 = PTi.ap(BF16).rearrange("p (a t) -> p a t", t=128)
    P.copy("act", PTr3, PTrf3, [PTrf], [PTr]); P.copy("act", PTi3, PTif3, [PTif], [PTi])
    P.copy("dve", s_("p127r"), PTrf3[:, :, 127], [PTrf], [kp]); P.copy("dve", s_("p127i"), PTif3[:, :, 127], [PTif], [kp])
    P.dbg("mir", MIr3, [128, 32, 128], F32, [MIr])
    P.dbg("mtab", Mt3, [128, 32, 256], BF16, [Mtab])
    M.release(m1)
    cmb = M.alloc(512 * 4); cm3 = cmb.ap(F32).rearrange("p (q c) -> p q c", c=128)
    P.dma("sp", cmb.ap(F32), E["cmask"][:, :], [rd["cmask"]], [cmb])
    cn2 = [M.alloc(8 * 128 * 4), M.alloc(8 * 128 * 4)]
    ct2 = [M.alloc(8 * 128 * 4), M.alloc(8 * 128 * 4)]
    for ri, nm in enumerate(("c_re", "c_im")):
        c3 = cn2[ri].ap(F32).rearrange("p (o c) -> p o c", c=128)
        srcv = E[nm].rearrange("(o g) c n -> (g c) o n", o=8)
        for dup in range(2):
            P.dma("sp", c3[:, :, dup * 64:(dup + 1) * 64], srcv, [rd[nm]], [cn2[ri]])
        t3 = ct2[ri].ap(F32).rearrange("p (o c) -> p o c", c=128)
        for o0 in range(0, 8, 4):
            ps = nextps()
            P.tr([(ps.ap[:, j * 128:(j + 1) * 128], c3[:, o0 + j, :], identF) for j in range(4)], [cn2[ri], identb], [ps.r])
            P.copy(ev_eng(), t3[:, o0:o0 + 4, :], ps.ap[:, :].rearrange("p (a b) -> p a b", a=4), [ps.r], [ct2[ri]])
        dst3 = Clr3 if ri == 0 else Cli3
        dstb = Clr if ri == 0 else Cli
        for o in range(8):
            in0 = t3[:, o:o + 1, :].to_broadcast([128, 4, 128])
            if ri == 0:
                P.tt("dve", dst3[:, 4 * o:4 * o + 4, :], in0, cm3, MUL, [ct2[ri], cmb], [dstb])
            else:
                P.stt(dst3[:, 4 * o:4 * o + 4, :], in0, -1.0, cm3, MUL, MUL, [ct2[ri], cmb], [dstb])
    M.release(mprep)
    BU = [M.alloc(8 * 256 * 2), M.alloc(8 * 256 * 2)]
    VQ = [M.alloc(8 * 256 * 2), M.alloc(8 * 256 * 2)]
    TD = [[M.alloc(8 * 128 * 2) for _ in range(4)] for _ in range(2)]

    qcount = [0]

    def bu_issue(uT3, ubuf, NT, qd):
        slot = qcount[0] % 2
        qcount[0] += 1
        bu3 = BU[slot].ap(BF16).rearrange("p (a c) -> p a c", c=256)
        for k in range(4):
            ps = nextps()
            prs = [qd * 8 + 2 * k + j for j in range(2)]
            P.mm([(ps.ap[0:NT, j * 256:(j + 1) * 256], uT3[:, prs[j] // 4, 0:NT], Brhs3[:, prs[j], :], True, True)
                  for j in range(2)], [ubuf, Brhs], [ps.r])
            P.copy("act", bu3[0:NT, 2 * k:2 * k + 2, :], ps.ap[0:NT, :].rearrange("p (a b) -> p a b", a=2),
                   [ps.r], [BU[slot].sub(k * 1024, (k + 1) * 1024)])
        return slot

    def demod(NT, qd, slot):
        bu3 = BU[slot].ap(BF16).rearrange("p (a c) -> p a c", c=256)
        vqb = VQ[slot]
        vq3 = vqb.ap(BF16).rearrange("p (a c) -> p a c", c=256)
        mr = Mt3[0:NT, qd * 8:(qd + 1) * 8, 0:128]; mi = Mt3[0:NT, qd * 8:(qd + 1) * 8, 128:256]
        br = bu3[0:NT, :, 0:128]; bi = bu3[0:NT, :, 128:256]
        td = TD[slot]
        t = [td[i].ap(BF16).rearrange("p (a c) -> p a c", c=128)[0:NT] for i in range(4)]
        P.tt("dve", t[0], mr, br, MUL, [Mtab, BU[slot]], [td[0]])
        P.tt("dve", t[1], mi, bi, MUL, [Mtab, BU[slot]], [td[1]])
        P.tt("dve", t[2], mr, bi, MUL, [Mtab, BU[slot]], [td[2]])
        P.tt("dve", t[3], mi, br, MUL, [Mtab, BU[slot]], [td[3]])
        return t, td

    def colsum_quarter(qd, t, td):
        items = []
        for j in range(8):
            pr = qd * 8 + j
            items.append((ZS.ap[:, pr:pr + 1], t[0][:, j, :], ones, True, False))
            items.append((ZS.ap[:, pr:pr + 1], t[1][:, j, :], nones, False, True))
            items.append((ZS.ap[:, 32 + pr:33 + pr], t[2][:, j, :], ones, True, False))
            items.append((ZS.ap[:, 32 + pr:33 + pr], t[3][:, j, :], ones, False, True))
        P.mm(items, list(td) + [onesb], [ZS.r])

    def carry_update():
        P.tt("dve", zc2, ZS.ap[:, 0:64], c2, ADD, [ZS.r, cbuf], [zcb])
        P.tt("dve", s_("k1"), zc2[:, 0:32], s_("A128r"), MUL, [zcb, kp], [kp])
        P.tt("dve", s_("k2"), zc2[:, 32:64], s_("A128i"), MUL, [zcb, kp], [kp])
        P.tt("dve", s_("k3"), zc2[:, 0:32], s_("A128i"), MUL, [zcb, kp], [kp])
        P.tt("dve", s_("k4"), zc2[:, 32:64], s_("A128r"), MUL, [zcb, kp], [kp])

    def carry_commit():
        P.tt("dve", c2[:, 0:32], s_("k1"), s_("k2"), SUB, [kp], [cbuf])
        P.tt("dve", c2[:, 32:64], s_("k3"), s_("k4"), ADD, [kp], [cbuf])

    mpre = M.mark()
    Wu = M.alloc(16 * 1024 * 2); Wu3 = Wu.ap(BF16).rearrange("p (k f) -> p k f", k=16)
    P.dma("pool", Wu3, w_in[:, 4096:5120].rearrange("(k p) f -> p k f", p=128), [rd["w_in"]], [Wu])
    xst = [M.alloc(D * 4), M.alloc(D * 4)]
    XTt = [M.alloc(16 * 128 * 2), M.alloc(16 * 128 * 2)]
    uTb = [M.alloc(8 * 128 * 2), M.alloc(8 * 128 * 2)]
    npre_tiles = E.get("npre_tiles")
    if npre_tiles is None:
        npre_tiles = NPRE * 16
    def stage_a(tt_, part):
        xb = xst[tt_ % 2]; xa = xb.ap(F32)
        xtb = XTt[tt_ % 2]; xt3 = xtb.ap(BF16).rearrange("p (k c) -> p k c", c=128)
        ub = uTb[tt_ % 2]; u3 = ub.ap(BF16).rearrange("p (o t) -> p o t", t=128)
        if part == 0:
            P.dma("sp", xa, xp[tt_ * 128:(tt_ + 1) * 128, :], [rd["xp"]], [xb])
        if part in (0, 1):
            for qq in (2 * part, 2 * part + 1):
                ps = nextps()
                P.tr([(ps.ap[:, j * 128:(j + 1) * 128], xa[:, (4 * qq + j) * 128:(4 * qq + j + 1) * 128], identF) for j in range(4)],
                     [xb, identb], [ps.r])
                P.copy("act" if qq % 2 == 0 else "dve", xt3[:, 4 * qq:4 * qq + 4, :], ps.ap.rearrange("p (a b) -> p a b", a=4), [ps.r],
                       [xtb.sub(qq * 1024, (qq + 1) * 1024)])
        else:
            half = part - 2
            ps = nextps()
            for f4 in range(4):
                ft = half * 4 + f4
                P.mm([(ps.ap[:, f4 * 128:(f4 + 1) * 128], Wu3[:, kt, ft * 128:(ft + 1) * 128], xt3[:, kt, :], kt == 0, kt == 15)
                      for kt in range(16)], [xtb, Wu], [ps.r])
            P.copy("act" if half == 0 else "dve", u3[:, half * 4:half * 4 + 4, :], ps.ap.rearrange("p (a b) -> p a b", a=4), [ps.r],
                   [ub.sub(half * 1024, (half + 1) * 1024)])
        return u3, ub

    if npre_tiles > 0:
        for part in range(4):
            stage_a(0, part)
    for tt_ in range(npre_tiles):
        ub = uTb[tt_ % 2]; u3 = ub.ap(BF16).rearrange("p (o t) -> p o t", t=128)
        slots = {0: bu_issue(u3, ub, 128, 0)}
        for qd in range(4):
            if qd + 1 < 4:
                slots[qd + 1] = bu_issue(u3, ub, 128, qd + 1)
            tq, tdq = demod(128, qd, slots[qd])
            if tt_ + 1 < npre_tiles:
                stage_a(tt_ + 1, qd)
            colsum_quarter(qd, tq, tdq)
        carry_update()
        carry_commit()
    M.release(mpre)

    dskb = E["dskb"]; dsk = dskb.ap(F32)
    bglb = E["bglb"]; bgl = bglb.ap(F32)

    ZC = [[M.alloc(4 * 128 * 2), M.alloc(4 * 128 * 2)] for _ in range(2)]
    RT = [[M.alloc(4 * 128 * 2) for _ in range(4)] for _ in range(2)]
    XG = [[M.alloc(4 * 128 * 2), M.alloc(4 * 128 * 2)] for _ in range(2)]
    Yb = M.alloc(8 * 128 * 4); Zb = M.alloc(8 * 128 * 2); Gb = M.alloc(8 * 128 * 2); SGb = M.alloc(8 * 128 * 2)
    W1 = M.alloc(8 * 128 * 4); W2 = M.alloc(8 * 128 * 2); BRt = [M.alloc(8 * 128 * 2)]
    YPS = [YP0, YP1]

    def y_octet(o, NT, xr3, xi3, xbufs):
        items = []
        for j in range(4):
            pr = 4 * o + j
            dst = YPS[o // 4].ap[:, (o % 4) * NT:(o % 4 + 1) * NT]
            items.append((dst, Clr3[:, pr, :], xr3[:, j, 0:NT], j == 0, False))
            items.append((dst, Cli3[:, pr, :], xi3[:, j, 0:NT], False, j == 3))
        P.mm(items, xbufs + [Clr, Cli], [YPS[o // 4].r])

    def glu_tail(NT, uT3, ubuf, gsrc3, gsbuf, br_out3, br_buf):
        y3 = Yb.ap(F32)[:, 0:8 * NT].rearrange("p (o t) -> p o t", t=NT)
        for o in range(8):
            P.stt(y3[:, o, :], uT3[:, o, 0:NT], dsk[:, o:o + 1], YPS[o // 4].ap[:, (o % 4) * NT:(o % 4 + 1) * NT], MUL, ADD,
                  [ubuf, dskb, YPS[o // 4].r], [Yb])
        yf = Yb.ap(F32)[:, 0:8 * NT]; w1 = W1.ap(F32)[:, 0:8 * NT]; w2 = W2.ap(BF16)[:, 0:8 * NT]
        zf = Zb.ap(BF16)[:, 0:8 * NT]; z3 = zf.rearrange("p (o t) -> p o t", t=NT)
        P.act(w1, yf, AF.Square, [Yb], [W1])
        P.ts("dve", w1, w1, 0.044715, 1.0, MUL, ADD, [W1], [W1])
        P.tt("dve", w1, w1, yf, MUL, [W1, Yb], [W1])
        P.act(w2, w1, AF.Sigmoid, [W1], [W2], scale=1.5957691216057308)
        P.tt("dve", zf, yf, w2, MUL, [Yb, W2], [Zb])
        gps = [nextps(), nextps()]
        for j in range(8):
            P.mm([(gps[j // 4].ap[:, (j % 4) * NT:(j % 4 + 1) * NT], wgl3[:, i, j * 128:(j + 1) * 128], z3[:, i, :], i == 0, i == 7)
                  for i in range(8)], [Zb, wglb], [gps[j // 4].r])
        g3 = Gb.ap(BF16)[:, 0:8 * NT].rearrange("p (o t) -> p o t", t=NT)
        for j in range(8):
            P.act(g3[:, j, :], gps[j // 4].ap[:, (j % 4) * NT:(j % 4 + 1) * NT], AF.Sigmoid, [gps[j // 4].r, bglb], [Gb],
                  bias=bgl[:, j:j + 1])
        sg = SGb.ap(BF16)[:, 0:8 * NT]
        P.act(sg.rearrange("p (o t) -> p o t", t=NT), gsrc3, AF.Silu, [gsbuf], [SGb])
        w3_ = W2.ap(BF16)[:, 0:8 * NT]
        P.tt("dve", w3_, zf, Gb.ap(BF16)[:, 0:8 * NT], MUL, [Zb, Gb], [W2])
        P.tt("dve", br_out3, w3_.rearrange("p (o t) -> p o t", t=NT), sg.rearrange("p (o t) -> p o t", t=NT), MUL, [W2, SGb], [br_buf])

    ms = M.mark()
    us = M.alloc(8 * 16 * 2); us3 = us.ap(BF16).rearrange("p (o t) -> p o t", t=16)
    P.copy("dve", us3, HS3[:, 32:40, :], [HS], [us])
    bur, bui = nextps(), nextps()
    P.mm([(bur.ap[:, pr * 16:(pr + 1) * 16], Brhs3[:, pr, 0:128], us3[:, pr // 4, :], True, True) for pr in range(32)], [us, Brhs], [bur.r])
    P.mm([(bui.ap[:, pr * 16:(pr + 1) * 16], Brhs3[:, pr, 128:256], us3[:, pr // 4, :], True, True) for pr in range(32)], [us, Brhs], [bui.r])
    XS = [M.alloc(32 * 16 * 4), M.alloc(32 * 16 * 4)]
    XS4 = [b.ap(F32).rearrange("p (a b t) -> p a b t", b=4, t=4) for b in XS]
    x0 = list(E["x0in"]) + [M.alloc(32 * 4 * 4) for _ in range(2)]
    x03 = [b.ap(F32).rearrange("p (a b) -> p a b", b=4) for b in x0]
    abr_b = s_("abr").unsqueeze(2).to_broadcast([128, 32, 4]); abi_b = s_("abi").unsqueeze(2).to_broadcast([128, 32, 4])
    bur4 = bur.ap.rearrange("p (a b t) -> p a b t", b=4, t=4); bui4 = bui.ap.rearrange("p (a b t) -> p a b t", b=4, t=4)
    for tau in range(4):
        pr_, pi_ = (x03[0], x03[1]) if tau == 0 else (XS4[0][:, :, :, tau - 1], XS4[1][:, :, :, tau - 1])
        srcb = [x0[0], x0[1]] if tau == 0 else [XS[0], XS[1]]
        P.tt("dve", x03[2], pr_, abr_b, MUL, srcb + [kp], [x0[2]]); P.tt("dve", x03[3], pi_, abi_b, MUL, srcb + [kp], [x0[3]])
        P.tt("dve", x03[2], x03[2], x03[3], SUB, [x0[2], x0[3]], [x0[2]])
        P.tt("dve", XS4[0][:, :, :, tau], x03[2], bur4[:, :, :, tau], ADD, [x0[2], bur.r], [XS[0]])
        P.tt("dve", x03[2], pr_, abi_b, MUL, srcb + [kp], [x0[2]]); P.tt("dve", x03[3], pi_, abr_b, MUL, srcb + [kp], [x0[3]])
        P.tt("dve", x03[2], x03[2], x03[3], ADD, [x0[2], x0[3]], [x0[2]])
        P.tt("dve", XS4[1][:, :, :, tau], x03[2], bui4[:, :, :, tau], ADD, [x0[2], bui.r], [XS[1]])
    for ri, nm in enumerate(("sre_s", "sim_s")):
        for b in range(4):
            P.dma("pool", sl_ap(E[nm], b * 4096), XS4[ri][:, :, b, 3], [XS[ri]], [rd[nm]], slow=True)
    P.dbg("hs", HS3, [128, 48, 16], F32, [HS])
    P.dbg("xs0", XS[0].ap(F32), [128, 512], F32, [XS[0]])
    P.dbg("x0r", x0[0].ap(F32), [128, 128], F32, [x0[0]])
    XSb = [M.alloc(32 * 16 * 2), M.alloc(32 * 16 * 2)]
    XSb3 = [b.ap(BF16).rearrange("p (a t) -> p a t", t=16) for b in XSb]
    for ri in range(2):
        P.copy("dve", XSb3[ri], XS[ri].ap(F32).rearrange("p (a t) -> p a t", t=16), [XS[ri]], [XSb[ri]])
    for o in range(8):
        y_octet(o, 16, XSb3[0][:, 4 * o:4 * o + 4, :], XSb3[1][:, 4 * o:4 * o + 4, :], [XSb[0], XSb[1]])
    brs3 = BRS.ap(BF16).rearrange("p (f t) -> p f t", t=16)
    glu_tail(16, us3, us, HS3[:, 40:48, :], HS, brs3[:, 8:16, :], BRS)
    M.release(ms)

    uTo = [M.alloc(8 * 128 * 2) for _ in range(3)]
    gso = [M.alloc(8 * 128 * 2) for _ in range(3)]
    gi = [0]

    def own_load(tt_):
        ub = uTo[tt_ % 3]; u3 = ub.ap(BF16).rearrange("p (o t) -> p o t", t=128)
        P.dma("sp", u3, UT[:, tt_ * 128:(tt_ + 1) * 128].rearrange("(o p) t -> p o t", p=128), [rd["UT"]], [ub])
        gb = gso[tt_ % 3]; g3 = gb.ap(BF16).rearrange("p (o t) -> p o t", t=128)
        P.dma("sp", g3, GST[:, tt_ * 128:(tt_ + 1) * 128].rearrange("(o p) t -> p o t", p=128), [rd["GST"]], [gb])

    ZC4 = {(a, g): [M.alloc(4 * 128 * 2), M.alloc(4 * 128 * 2)] for a in range(2) for g in range(2)}
    XG4 = {(a, g): [M.alloc(4 * 128 * 2), M.alloc(4 * 128 * 2)] for a in range(2) for g in range(2)}
    st = {}

    def tile_bufs(t):
        ub = uTo[t % 3]; u3 = ub.ap(BF16).rearrange("p (o t) -> p o t", t=128)
        gb = gso[t % 3]; g3 = gb.ap(BF16).rearrange("p (o t) -> p o t", t=128)
        return u3, ub, g3, gb

    def S1(Q):
        t, q = divmod(Q, 4)
        u3, ub, g3, gb = tile_bufs(t)
        if q == 0 and t + 1 < 16:
            own_load(t + 1)
        st[("slot", Q)] = bu_issue(u3, ub, 128, q)

    def S2(Q):
        t, q = divmod(Q, 4)
        st[("vq", Q)] = demod(128, q, st.pop(("slot", Q)))

    def S3(Q):
        t, q = divmod(Q, 4)
        tq, tdq = st.pop(("vq", Q))
        colsum_quarter(q, tq, tdq)
        for gg in range(2):
            o = q * 2 + gg
            zr, zi = nextps(), nextps()
            itr, iti = [], []
            for j in range(4):
                itr.append((zr.ap[:, j * 128:(j + 1) * 128], tq[0][:, gg * 4 + j, :], tri, True, False))
                itr.append((zr.ap[:, j * 128:(j + 1) * 128], tq[1][:, gg * 4 + j, :], ntri, False, True))
                iti.append((zi.ap[:, j * 128:(j + 1) * 128], tq[2][:, gg * 4 + j, :], tri, True, False))
                iti.append((zi.ap[:, j * 128:(j + 1) * 128], tq[3][:, gg * 4 + j, :], tri, False, True))
            P.mm(itr, list(tdq) + [trib, ntrib], [zr.r])
            P.mm(iti, list(tdq) + [trib], [zi.r])
            zcs = ZC4[(Q % 2, gg)]
            zcr = zcs[0].ap(BF16).rearrange("p (a t) -> p a t", t=128); zci = zcs[1].ap(BF16).rearrange("p (a t) -> p a t", t=128)
            for j in range(4):
                P.act(zcr[:, j, :], zr.ap[:, j * 128:(j + 1) * 128], AF.Identity, [zr.r, cbuf], [zcs[0]],
                      bias=c2[:, 4 * o + j:4 * o + j + 1])
                P.act(zci[:, j, :], zi.ap[:, j * 128:(j + 1) * 128], AF.Identity, [zi.r, cbuf], [zcs[1]],
                      bias=c2[:, 32 + 4 * o + j:32 + 4 * o + j + 1])
        if q == 3:
            carry_update()
            if t == 15:
                p127r = s_("p127r"); p127i = s_("p127i")
                P.tt("dve", s_("f1"), zc2[:, 0:32], p127r, MUL, [zcb, kp], [kp]); P.tt("dve", s_("f2"), zc2[:, 32:64], p127i, MUL, [zcb, kp], [kp])
                P.tt("dve", s_("fr"), s_("f1"), s_("f2"), SUB, [kp], [kp])
                P.tt("dve", s_("f1"), zc2[:, 0:32], p127i, MUL, [zcb, kp], [kp]); P.tt("dve", s_("f2"), zc2[:, 32:64], p127r, MUL, [zcb, kp], [kp])
                P.tt("dve", s_("fi"), s_("f1"), s_("f2"), ADD, [kp], [kp])
                P.dma("pool", sl_ap(E["sre_o"]), s_("fr"), [kp], [rd["sre_o"]], slow=True)
                P.dma("pool", sl_ap(E["sim_o"]), s_("fi"), [kp], [rd["sim_o"]], slow=True)
            carry_commit()

    def S4(Q):
        t, q = divmod(Q, 4)
        for gg in range(2):
            o = q * 2 + gg
            zcs = ZC4[(Q % 2, gg)]; xs = XG4[(Q % 2, gg)]; rts = RT[gg]
            zcr = zcs[0].ap(BF16).rearrange("p (a t) -> p a t", t=128); zci = zcs[1].ap(BF16).rearrange("p (a t) -> p a t", t=128)
            pr_ = PTr3[:, 4 * o:4 * o + 4, :]; pi_ = PTi3[:, 4 * o:4 * o + 4, :]
            r = [rts[i].ap(BF16).rearrange("p (a t) -> p a t", t=128) for i in range(4)]
            xr3 = xs[0].ap(BF16).rearrange("p (a t) -> p a t", t=128); xi3 = xs[1].ap(BF16).rearrange("p (a t) -> p a t", t=128)
            P.tt("dve", r[0], zcr, pr_, MUL, [zcs[0], PTr], [rts[0]])
            P.tt("dve", r[1], zci, pi_, MUL, [zcs[1], PTi], [rts[1]])
            P.tt("dve", r[2], zcr, pi_, MUL, [zcs[0], PTi], [rts[2]])
            P.tt("dve", r[3], zci, pr_, MUL, [zcs[1], PTr], [rts[3]])
            P.tt("dve", xr3, r[0], r[1], SUB, [rts[0], rts[1]], [xs[0]])
            P.tt("dve", xi3, r[2], r[3], ADD, [rts[2], rts[3]], [xs[1]])

    def S5(Q):
        t, q = divmod(Q, 4)
        for gg in range(2):
            o = q * 2 + gg
            xs = XG4[(Q % 2, gg)]
            xr3 = xs[0].ap(BF16).rearrange("p (a t) -> p a t", t=128); xi3 = xs[1].ap(BF16).rearrange("p (a t) -> p a t", t=128)
            y_octet(o, 128, xr3, xi3, [xs[0], xs[1]])
        if q == 3:
            u3, ub, g3, gb = tile_bufs(t)
            bt = BRt[0]
            bt3 = bt.ap(BF16).rearrange("p (o t) -> p o t", t=128)
            glu_tail(128, u3, ub, g3, gb, bt3, bt)
            P.dma("sp", BR[1024:2048, t * 128:(t + 1) * 128].rearrange("(o p) t -> p o t", p=128), bt3, [bt], [rd["BR"]])

    own_load(0)
    NQ = 64
    for k in range(NQ + 4):
        if k < NQ:
            S1(k)
        if 0 <= k - 1 < NQ:
            S2(k - 1)
        if 0 <= k - 3 < NQ:
            S4(k - 3)
        if 0 <= k - 2 < NQ:
            S3(k - 2)
        if 0 <= k - 4 < NQ:
            S5(k - 4)


DILS = (1, 4, 16)


def attn_tables(E):
    P, M, rd, banks = E["P"], E["M"], E["rd"], E["banks"]
    nextps = E["nextps"]
    TT = M.alloc(48 * 256 * 2); TT3 = TT.ap(BF16).rearrange("p (a c) -> p a c", c=256)
    E32 = M.alloc(16 * 4)
    P.dma("sp", E32.ap(F32)[0:32, :], E["rel_bias"][:, :], [rd["rel_bias"]], [E32])
    P.act(E32.ap(F32)[0:32, :], E32.ap(F32)[0:32, :], AF.Exp, [E32], [E32])
    m0 = M.mark()
    OH = M.alloc(3 * 384 * 4); OH3 = OH.ap(F32).rearrange("p (a c) -> p a c", c=384)
    P.dma("sp", OH.ap(F32)[0:32, :], E["coh"][:, :], [rd["coh"]], [OH])
    Jb = M.alloc(128 * 4)
    P.dma("sp", Jb.ap(F32), E["cj"][:, :], [rd["cj"]], [Jb])
    gst = M.alloc(384 * 4)
    GV = E["GV"]
    for p in range(3):
        ps = nextps()
        P.mm([(ps.ap[0:16, 0:384], E32.ap(F32)[0:32, :], OH3[0:32, p, :], True, True)], [E32, OH], [ps.r])
        P.copy("dve", gst.ap(F32)[0:16, :], ps.ap[0:16, 0:384], [ps.r], [gst])
        P.dma("sp", GV[p * 16:(p + 1) * 16, :], gst.ap(F32)[0:16, :], [gst], [rd["GV"]])
    hst = [M.alloc(512 * 4), M.alloc(512 * 4)]
    for i in range(24):
        hb = hst[i % 2]
        P.dma("sp", hb.ap(F32).rearrange("p (a c) -> p a c", c=256),
              bass.AP(GV.tensor, i * 2 * 384, [[1, 128], [384, 2], [1, 256]]), [rd["GV"]], [hb])
        ps = nextps()
        P.mm([(ps.ap[:, :], Jb.ap(F32), hb.ap(F32), True, True)], [Jb, hb], [ps.r])
        P.copy("act" if i % 2 == 0 else "dve", TT3[:, 2 * i:2 * i + 2, :], ps.ap.rearrange("p (a c) -> p a c", c=256), [ps.r], [TT])
    M.release(m0)
    E["TT"], E["TT3"], E["E32"] = TT, TT3, E32


def attn_phase(E):
    P, M, rd, banks = E["P"], E["M"], E["rd"], E["banks"]
    nextps = E["nextps"]
    QT, KT, GT, VS, BR = E["QT"], E["KT"], E["GT"], E["VS"], E["BR"]
    TT, TT3 = E["TT"], E["TT3"]
    MUL, ADD = ALU.mult, ALU.add
    hvb = M.alloc(4)
    P.dma("sp", hvb.ap(F32), E["hv"][:, :], [rd["hv"]], [hvb])
    onf = M.alloc(64 * 4)
    P.memset("dve", onf.ap(F32), 1.0, [onf])
    NT_H = 21
    HOFF = (0, 1, 5)
    OOFF = (21, 37, 53)
    VA = [M.alloc(69 * 65 * 2), M.alloc(69 * 65 * 2)]
    VA3 = [v.ap(BF16).rearrange("p (t c) -> p t c", c=65) for v in VA]
    for i in range(2):
        P.memset("pool", VA[i].ap(BF16), 1.0, [VA[i]])
        P.copy("dve", VA3[i][:, 0:NT_H, 64:65], hvb.ap(F32)[:, 0:1].unsqueeze(1).to_broadcast([128, NT_H, 1]), [hvb], [VA[i]])
    KTh = [M.alloc(4096 * 2), M.alloc(4096 * 2)]
    QTh = [M.alloc(2048 * 2), M.alloc(2048 * 2)]
    GTh = [M.alloc(2048 * 2), M.alloc(2048 * 2)]
    ACC = [M.alloc(2048 * 4), M.alloc(2048 * 4)]
    NEB = 6
    EB = [M.alloc(256 * 2) for _ in range(NEB)]
    PB = [M.alloc(256 * 2) for _ in range(NEB)]
    SG = M.alloc(2048 * 2)
    BRh = [M.alloc(2048 * 2), M.alloc(2048 * 2)]
    LN_ = M.alloc(2048 * 4)
    ei = [0]
    def head_loads(h):
        k2 = h % 2
        kt = KTh[k2].ap(BF16); qt = QTh[k2].ap(BF16); gt = GTh[k2].ap(BF16)
        P.dma("sp", kt[0:64, :], KT[h * 64:(h + 1) * 64, :], [rd["KT"]], [KTh[k2]])
        P.dma("sp", qt[0:64, :], QT[h * 64:(h + 1) * 64, :], [rd["QT"]], [QTh[k2]])
        P.dma("sp", gt[0:64, :], GT[h * 64:(h + 1) * 64, :], [rd["GT"]], [GTh[k2]])
        va3 = VA3[k2]
        for p, d in enumerate(DILS):
            nsp = 16 // d
            if d == 1:
                src = bass.AP(VS.tensor, TC * 1024 + h * 64, [[1024, 128], [128 * 1024, 16], [1, 64]])
                P.dma("sp", va3[:, OOFF[p]:OOFF[p] + 16, 0:64], src, [rd["VS"]], [VA[k2]])
            else:
                for s_ in range(nsp):
                    src = bass.AP(VS.tensor, (TC + s_ * 128 * d) * 1024 + h * 64, [[d * 1024, 128], [1024, d], [1, 64]])
                    P.dma("sp", va3[:, OOFF[p] + s_ * d:OOFF[p] + (s_ + 1) * d, 0:64], src, [rd["VS"]], [VA[k2]])
            src = bass.AP(VS.tensor, (TC - 128 * d) * 1024 + h * 64, [[d * 1024, 128], [1024, d], [1, 64]])
            P.dma("sp", va3[:, HOFF[p]:HOFF[p] + d, 0:64], src, [rd["VS"]], [VA[k2]])

    head_loads(0)
    for h in range(NH):
        k2 = h % 2
        kt = KTh[k2].ap(BF16); qt = QTh[k2].ap(BF16); gt = GTh[k2].ap(BF16)
        va3 = VA3[k2]
        if h + 1 < NH:
            head_loads(h + 1)
        acc = ACC[k2].ap(F32)
        blks = []
        for p, d in enumerate(DILS):
            nsp = 16 // d
            blocks = [(s_, r_) for s_ in range(nsp) for r_ in range(d)]
            for g0 in range(0, 16, 4):
                for j in range(4):
                    blks.append((p, d, g0, j, blocks[g0 + j][0], blocks[g0 + j][1]))
        LA = 3
        pbs = {}
        pos = {}

        def stage_s(i):
            p, d, g0, j, sg_, r_ = blks[i]
            tcol = TT3[:, p * 16 + h, :]
            start = sg_ * 128 * d + r_
            qv = qt[0:64, start:start + 127 * d + 1:d] if d > 1 else qt[0:64, start:start + 128]
            kc0 = TC + start
            kp0 = TC + start - 128 * d
            kcur = kt[0:64, kc0:kc0 + 127 * d + 1:d] if d > 1 else kt[0:64, kc0:kc0 + 128]
            kprv = kt[0:64, kp0:kp0 + 127 * d + 1:d] if d > 1 else kt[0:64, kp0:kp0 + 128]
            ps = nextps()
            P.mm([(ps.ap[:, 0:128], kcur, qv, True, True), (ps.ap[:, 128:256], kprv, qv, True, True)],
                 [KTh[k2], QTh[k2]], [ps.r])
            eb = EB[ei[0] % NEB]; pb = PB[ei[0] % NEB]
            P.act(eb.ap(BF16), ps.ap[:, 0:256], AF.Exp, [ps.r], [eb], scale=SCALE)
            P.tt("pool" if ei[0] % 4 == 3 else "dve", pb.ap(BF16), eb.ap(BF16), tcol, MUL, [eb, TT], [pb])
            ei[0] += 1
            pbs[i] = pb

        def stage_v(i):
            p, d, g0, j, sg_, r_ = blks[i]
            if j == 0:
                pos[(p, g0)] = nextps()
            po = pos[(p, g0)]
            pb = pbs.pop(i)
            tcur = OOFF[p] + sg_ * d + r_
            tprv = (OOFF[p] + (sg_ - 1) * d + r_) if sg_ > 0 else (HOFF[p] + r_)
            P.mm([(po.ap[0:65, j * 128:(j + 1) * 128], va3[:, tcur, :], pb.ap(BF16)[:, 0:128], True, False),
                  (po.ap[0:65, j * 128:(j + 1) * 128], va3[:, tprv, :], pb.ap(BF16)[:, 128:256], False, True)],
                 [VA[k2], pb], [po.r])
            if j == 3:
                if d == 1:
                    outv = acc[0:65, g0 * 128:(g0 + 4) * 128]
                    P.copy("dve", outv, po.ap[0:65, :], [po.r], [ACC[k2]])
                elif d == 4:
                    outv = acc[0:65, (g0 // 4) * 512:(g0 // 4 + 1) * 512].rearrange("p (i j) -> p j i", j=4)
                    P.tt("dve", outv, outv, po.ap[0:65, :].rearrange("p (j i) -> p j i", j=4), ADD, [po.r, ACC[k2]], [ACC[k2]])
                else:
                    outv = acc[0:65, :].rearrange("p (i r) -> p r i", r=16)[:, g0:g0 + 4, :]
                    P.tt("dve", outv, outv, po.ap[0:65, :].rearrange("p (j i) -> p j i", j=4), ADD, [po.r, ACC[k2]], [ACC[k2]])

        for i in range(len(blks) + LA):
            if i < len(blks):
                stage_s(i)
            if i - LA >= 0:
                stage_v(i - LA)
        P.act(acc[64:65, :], acc[64:65, :], AF.Ln, [ACC[k2]], [ACC[k2]])
        P.act(acc[64:65, :], acc[64:65, :], AF.Exp, [ACC[k2]], [ACC[k2]], scale=-1.0)
        P.act(SG.ap(BF16)[0:64, :], gt[0:64, :], AF.Silu, [GTh[k2]], [SG])
        brh = BRh[k2]
        for n in range(4):
            ps = nextps()
            P.mm([(ps.ap[0:64, :], onf.ap(F32)[64:65, 0:64], acc[64:65, n * 512:(n + 1) * 512], True, True)], [onf, ACC[k2]], [ps.r])
            P.tt("dve", LN_.ap(F32)[0:64, n * 512:(n + 1) * 512], acc[0:64, n * 512:(n + 1) * 512], ps.ap[0:64, :], MUL,
                 [ps.r, ACC[k2]], [LN_.sub(n * 2048, (n + 1) * 2048)])
            P.tt("pool", brh.ap(BF16)[0:64, n * 512:(n + 1) * 512], LN_.ap(F32)[0:64, n * 512:(n + 1) * 512],
                 SG.ap(BF16)[0:64, n * 512:(n + 1) * 512], MUL, [LN_.sub(n * 2048, (n + 1) * 2048), SG], [brh.sub(n * 1024, (n + 1) * 1024)])
        P.dma("pool", BR[h * 64:(h + 1) * 64, :], brh.ap(BF16)[0:64, :], [brh], [rd["BR"]])


def out_phase(E):
    P, M, rd, banks = E["P"], E["M"], E["rd"], E["banks"]
    nextps = E["nextps"]
    BR, xo, y_o, xs, y_s = E["BR"], E["xo"], E["y_o"], E["xs"], E["y_s"]
    BRS = E["BRS"]
    MUL, ADD, SUB = ALU.mult, ALU.add, ALU.subtract
    Wo = M.alloc(16 * 2048 * 2); wo3 = Wo.ap(BF16).rearrange("p (k d) -> p k d", k=16)
    for half in range(2):
        P.dma("pool", wo3[:, half * 8:(half + 1) * 8, :], E["w_out"][half * 1024:(half + 1) * 1024, :].rearrange("(k p) d -> p k d", p=128),
              [rd["w_out"]], [Wo.sub(half * 32768, (half + 1) * 32768)])
    bob = M.alloc(2048 * 2)
    P.dma("pool", bob.ap(BF16)[0:1, :], E["b_out"][:, :], [rd["b_out"]], [bob])
    onb = M.alloc(128 * 2)
    P.memset("dve", onb.ap(BF16), 1.0, [onb])
    gB = M.alloc(2048 * 4); bB = M.alloc(2048 * 4)
    P.dma("sp", gB.ap(F32), bass.AP(E["ln_g"].tensor, 0, [[0, 128], [1, 2048]]), [rd["ln_g"]], [gB])
    P.dma("sp", bB.ap(F32), bass.AP(E["ln_b"].tensor, 0, [[0, 128], [1, 2048]]), [rd["ln_b"]], [bB])
    xst = [M.alloc(D * 4), M.alloc(D * 4)]
    brt = [M.alloc(16 * 128 * 2), M.alloc(16 * 128 * 2)]
    Vb = M.alloc(D * 4)
    Ob = [M.alloc(D * 4), M.alloc(D * 4)]
    stb = M.alloc(64 * 4)

    def do_tile(NT, br3, brbuf, xsrc, xsrc_r, ydst, ydst_r, ti):
        xb = xst[ti % 2]; xa = xb.ap(F32)
        if xsrc is not None:
            P.dma("sp", xa[0:NT, :], xsrc, [xsrc_r], [xb])
        va = Vb.ap(F32)
        for n in range(4):
            ps = nextps()
            items = [(ps.ap[0:NT, :], br3[:, ft, 0:NT], wo3[:, ft, n * 512:(n + 1) * 512], ft == 0, False) for ft in range(16)]
            items.append((ps.ap[0:NT, :], onb.ap(BF16)[0:1, 0:NT], bob.ap(BF16)[0:1, n * 512:(n + 1) * 512], False, True))
            P.mm(items, [brbuf, Wo, onb, bob], [ps.r])
            P.stt(va[0:NT, n * 512:(n + 1) * 512], xa[0:NT, n * 512:(n + 1) * 512], DN_ALPHA, ps.ap[0:NT, :], MUL, ADD,
                  [xb, ps.r], [Vb.sub(n * 2048, (n + 1) * 2048)])
        st = stb.ap(F32)
        for n in range(4):
            def fn(h, n=n):
                return h.bn_stats(out=st[0:NT, n * 6:(n + 1) * 6], in_=va[0:NT, n * 512:(n + 1) * 512])
            P.S.op("dve", fn, res_of([Vb]), res_of([stb]))

        def fn2(h):
            return h.bn_aggr(out=st[0:NT, 32:34], in_=st[0:NT, 0:24])
        P.S.op("dve", fn2, res_of([stb]), res_of([stb]))
        P.ts("dve", st[0:NT, 34:35], st[0:NT, 33:34], LN_EPS, None, ADD, None, [stb], [stb])
        P.act(st[0:NT, 35:36], st[0:NT, 34:35], AF.Sqrt, [stb], [stb])
        P.recip(st[0:NT, 36:37], st[0:NT, 35:36], [stb], [stb])
        P.stt(st[0:NT, 37:38], st[0:NT, 32:33], -1.0, st[0:NT, 36:37], MUL, MUL, [stb], [stb])
        ob = Ob[ti % 2]; oa = ob.ap(F32)
        P.act(oa[0:NT, :], va[0:NT, :], AF.Identity, [Vb, stb], [ob], scale=st[0:NT, 36:37], bias=st[0:NT, 37:38])
        P.tt("dve", oa[0:NT, :], oa[0:NT, :], gB.ap(F32)[0:NT, :], MUL, [ob, gB], [ob])
        P.tt("pool", oa[0:NT, 0:1024], oa[0:NT, 0:1024], bB.ap(F32)[0:NT, 0:1024], ADD, [ob.sub(0, 4096), bB], [ob.sub(0, 4096)])
        P.tt("dve", oa[0:NT, 1024:2048], oa[0:NT, 1024:2048], bB.ap(F32)[0:NT, 1024:2048], ADD, [ob.sub(4096, 8192), bB], [ob.sub(4096, 8192)])
        P.dma("pool", ydst, oa[0:NT, :], [ob], [ydst_r])

    def out_loads(tt_):
        bt = brt[tt_ % 2]
        bt3 = bt.ap(BF16).rearrange("p (f t) -> p f t", t=128)
        P.dma("sp", bt3, BR[:, tt_ * 128:(tt_ + 1) * 128].rearrange("(f p) t -> p f t", p=128), [rd["BR"]], [bt])
        xb = xst[tt_ % 2]
        P.dma("sp", xb.ap(F32), xo[tt_ * 128:(tt_ + 1) * 128, :], [rd["xo"]], [xb])

    out_loads(0)
    for tt_ in range(16):
        bt = brt[tt_ % 2]
        bt3 = bt.ap(BF16).rearrange("p (f t) -> p f t", t=128)
        if tt_ + 1 < 16:
            out_loads(tt_ + 1)
        do_tile(128, bt3, bt, None, rd["xo"], y_o[tt_ * 128:(tt_ + 1) * 128, :], rd["y_o"], tt_)
    if E.get("sample_attn_done"):
        brs3 = BRS.ap(BF16).rearrange("p (f t) -> p f t", t=16)
        do_tile(16, brs3, BRS, xs[:, :], rd["xs"], y_s[:, :], rd["y_s"], 16)


def sample_attn_phase(E):
    P, M, rd, banks = E["P"], E["M"], E["rd"], E["banks"]
    rot = [0]

    def nextps():
        b_ = banks[rot[0] % 6]
        rot[0] += 1
        return b_
    HS3, HS, VSs, BRS = E["HS3"], E["HS"], E["VSs"], E["BRS"]
    E32 = E["E32"]
    ck, cv, BRSD = E["ck"], E["cv"], E["BRSD"]
    MUL = ALU.mult
    cnt = M.alloc(32 * 128 * 4); cnt3 = cnt.ap(F32).rearrange("p (a k) -> p a k", k=128)
    P.dma("sp", cnt.ap(F32)[0:32, :], E["ccnt"][:, :], [rd["ccnt"]], [cnt])
    WS = M.alloc(32 * 16 * 4); ws3 = WS.ap(F32).rearrange("p (a h) -> p a h", h=16)
    ps = nextps()
    P.mm([(ps.ap[:, a * 16:(a + 1) * 16], cnt3[0:32, a, :], E32.ap(F32)[0:32, :], True, True) for a in range(32)], [cnt, E32], [ps.r])
    P.copy("dve", WS.ap(F32), ps.ap[:, :], [ps.r], [WS])
    wsv = ws3.rearrange("p (t s) h -> p t h s", s=4)
    HSb = M.alloc(16 * 16 * 2); hsb3 = HSb.ap(BF16).rearrange("p (f t) -> p f t", t=16)
    P.copy("dve", hsb3, HS3[:, 0:16, :], [HS], [HSb])
    onf = M.alloc(64 * 4)
    P.memset("dve", onf.ap(F32), 1.0, [onf])
    QP = M.alloc(16 * 16 * 2); qp3 = QP.ap(BF16).rearrange("p (h t) -> p h t", t=16)
    P.memset("dve", QP.ap(BF16), 0.0, [QP])
    P.copy("dve", qp3[0:64, 0:16:2, :], hsb3[0:64, 0:8, :], [HSb], [QP])
    P.copy("dve", qp3[64:128, 1:16:2, :], hsb3[64:128, 0:8, :], [HSb], [QP])
    STOP = int(os.environ.get("KSA_STOP", "99"))
    if STOP <= 1:
        return
    KC = [M.alloc(1024 * 4), M.alloc(1024 * 4)]
    VC = [M.alloc(1024 * 4), M.alloc(1024 * 4)]
    VAs = M.alloc(8 * 16 * 65 * 2); vas4 = VAs.ap(BF16).rearrange("p (t h c) -> p t h c", h=16, c=65)
    P.memset("pool", VAs.ap(BF16), 1.0, [VAs])
    KTs = [M.alloc(8 * 128 * 2), M.alloc(8 * 128 * 2)]
    EBs = M.alloc(512 * 2); PMs = M.alloc(512 * 2)
    eb4 = EBs.ap(BF16).rearrange("p (t h s) -> p t h s", h=16, s=4); pm4 = PMs.ap(BF16).rearrange("p (t h s) -> p t h s", h=16, s=4)
    NUM = M.alloc(64 * 4); AT = M.alloc(64 * 4)
    identF, identb = E["identF"], E["identb"]
    li = [0]
    for b in range(4):
        sp = banks[6]
        sp4 = sp.ap.rearrange("p (t h s) -> p t h s", h=16, s=4)
        for tile in range(7):
            kc = KC[li[0] % 2]; vc = VC[li[0] % 2]; kts = KTs[li[0] % 2]
            li[0] += 1
            if tile < 4:
                r0 = 1536 + 128 * tile
                P.dma("sp", kc.ap(F32), ck[b, r0:r0 + 128, :], [rd["ck"]], [kc])
                P.dma("act", vc.ap(F32), cv[b, r0:r0 + 128, :], [rd["cv"]], [vc])
            else:
                u = tile - 4
                for sr in range(4):
                    off = b * CL * 1024 + (16 * 32 * u + sr) * 1024
                    P.dma("sp", kc.ap(F32)[sr * 32:(sr + 1) * 32, :], bass.AP(ck.tensor, off, [[16 * 1024, 32], [1, 1024]]), [rd["ck"]], [kc])
                    P.dma("act", vc.ap(F32)[sr * 32:(sr + 1) * 32, :], bass.AP(cv.tensor, off, [[16 * 1024, 32], [1, 1024]]), [rd["cv"]], [vc])
            P.copy("pool", vas4[:, tile, :, 0:64], vc.ap(F32).rearrange("p (h c) -> p h c", c=64), [vc], [VAs])
            kt3 = kts.ap(BF16).rearrange("p (f k) -> p f k", k=128)
            for half in range(2):
                pt = nextps()
                P.tr([(pt.ap[:, j * 128:(j + 1) * 128], kc.ap(F32)[:, (half * 4 + j) * 128:(half * 4 + j + 1) * 128], identF) for j in range(4)],
                     [kc, identb], [pt.r])
                P.copy("act" if half == 0 else "dve", kt3[:, half * 4:half * 4 + 4, :], pt.ap.rearrange("p (a k) -> p a k", a=4), [pt.r],
                       [kts.sub(half * 1024, (half + 1) * 1024)])
            if os.environ.get("KSA_VAR") == "a":
                continue
            P.mm([(sp4[:, tile, h, :], kt3[:, h // 2, :], qp3[:, h, b * 4:(b + 1) * 4], True, True)
                  for h in range(NH)], [kts, QP], [sp.r])
        if os.environ.get("KSA_VAR") not in ("a", "b"):
            P.mm([(sp4[0:4, 7, h, :], hsb3[:, 8 + h // 2, b * 4:(b + 1) * 4], qp3[:, h, b * 4:(b + 1) * 4], True, True)
                  for h in range(NH)], [HSb, QP], [sp.r])
        if STOP <= 2:
            continue
        P.dma("sp", vas4[0:4, 7, :, 0:64], VSs.ap(BF16)[b * 4:(b + 1) * 4, :].rearrange("p (h c) -> p h c", c=64), [VSs], [VAs])
        if STOP <= 3:
            continue
        P.act(eb4[:, 0:7, :, :], sp4[:, 0:7, :, :], AF.Exp, [sp.r], [EBs], scale=SCALE)
        P.act(eb4[0:4, 7, :, :], sp4[0:4, 7, :, :], AF.Exp, [sp.r], [EBs], scale=SCALE)
        P.tt("dve", pm4[:, 0:7, :, :], eb4[:, 0:7, :, :], wsv[:, 0:7, :, :], MUL, [EBs, WS], [PMs])
        P.tt("dve", pm4[0:4, 7, :, :], eb4[0:4, 7, :, :], wsv[0:4, 7, :, :], MUL, [EBs, WS], [PMs])
        if STOP <= 4:
            continue
        po = banks[7]
        items = []
        for h in range(NH):
            for tile in range(7):
                items.append((po.ap[0:65, h * 4:(h + 1) * 4], vas4[:, tile, h, :], pm4[:, tile, h, :], tile == 0, False))
            items.append((po.ap[0:65, h * 4:(h + 1) * 4], vas4[0:4, 7, h, :], pm4[0:4, 7, h, :], False, True))
        P.mm(items, [VAs, PMs], [po.r])
        if STOP <= 5:
            continue
        num = NUM.ap(F32)
        P.copy("dve", num[0:65, :], po.ap[0:65, 0:64], [po.r], [NUM])
        P.act(num[64:65, :], num[64:65, :], AF.Ln, [NUM], [NUM])
        P.act(num[64:65, :], num[64:65, :], AF.Exp, [NUM], [NUM], scale=-1.0)
        pb_ = nextps()
        P.mm([(pb_.ap[0:64, 0:64], onf.ap(F32)[64:65, 0:64], num[64:65, :], True, True)], [onf, NUM], [pb_.r])
        P.tt("dve", AT.ap(F32)[0:64, :], num[0:64, :], pb_.ap[0:64, 0:64], MUL, [pb_.r, NUM], [AT])
        P.dma("sp", bass.AP(BRSD.tensor, b * 4, [[16, 64], [64 * 16, 16], [1, 4]]),
              AT.ap(F32)[0:64, :].rearrange("p (h s) -> p h s", s=4), [AT], [rd["BRSD"]], slow=True)
    if STOP <= 6:
        return
    atb = M.alloc(8 * 16 * 4); sgb = M.alloc(8 * 16 * 4)
    at3 = atb.ap(F32).rearrange("p (f t) -> p f t", t=16); sg3 = sgb.ap(F32).rearrange("p (f t) -> p f t", t=16)
    P.dma("sp", at3, BRSD.rearrange("(f p) t -> p f t", p=128), [rd["BRSD"]], [atb], slow=True)
    P.act(sg3, HS3[:, 24:32, :], AF.Silu, [HS], [sgb])
    brs3 = BRS.ap(BF16).rearrange("p (f t) -> p f t", t=16)
    P.tt("dve", brs3[:, 0:8, :], at3, sg3, MUL, [atb, sgb], [BRS])
    E["sample_attn_done"] = True
```

```python
import math
import numpy as np
import ml_dtypes
import concourse.bass as bass
import concourse.mybir as mybir
from concourse.bass_utils import run_bass_kernel_spmd

F32 = mybir.dt.float32
BF16 = mybir.dt.bfloat16
AF = mybir.ActivationFunctionType
ALU = mybir.AluOpType

D = 2048
TC = 2048
NPRE = 3
NS = 16
NPROJ = 6144
NH = 16
HD = 64
CL = 2048
SCALE = HD ** -0.5
DN_ALPHA = 2.0 ** 0.25
LN_EPS = 1e-5
import os
DEBUG = bool(int(os.environ.get("KDEBUG", "0")))
DBGSET = [x for x in os.environ.get("KDBGSET", "").split(",") if x]


class R:
    __slots__ = ("name", "writer", "readers")

    def __init__(self, name):
        self.name = name
        self.writer = None
        self.readers = []


class Op:
    __slots__ = ("eng", "fn", "deps", "signal", "is_dma", "sem", "val", "prev_same_sem")

    def __init__(self, eng, fn, is_dma):
        self.eng = eng
        self.fn = fn
        self.deps = []
        self.signal = False
        self.is_dma = is_dma
        self.sem = None
        self.val = 0
        self.prev_same_sem = None


class Sched:
    ENGS = ("pe", "act", "dve", "pool", "sp")

    def __init__(self):
        self.ops = {e: [] for e in self.ENGS}
        self.all_dma = []

    def op(self, eng, fn, reads=(), writes=(), dma=False):
        o = Op(eng, fn, dma)
        deps = []
        for r in reads:
            if r.writer is not None:
                deps.append(r.writer)
        for w in writes:
            if w.writer is not None:
                deps.append(w.writer)
            deps.extend(w.readers)
        seen = set()
        for d in deps:
            if d is o or id(d) in seen:
                continue
            seen.add(id(d))
            if d.eng == "pe" and eng == "pe" and not d.is_dma and not dma:
                continue
            d.signal = True
            o.deps.append(d)
        for r in reads:
            if not dma:
                r.readers = [x for x in r.readers if x.is_dma or x.eng != eng]
            r.readers.append(o)
        for w in writes:
            w.writer = o
            w.readers = []
        if dma:
            o.signal = True
            self.all_dma.append(o)
        self.ops[eng].append(o)
        return o

    def emit(self, nc, block, eng_sems, dma_sems):
        NDS = {e: len(dma_sems[e]) for e in dma_sems}
        for e in self.ENGS:
            cnt = 0
            dcnt = 0
            last_on_sem = {}
            for o in self.ops[e]:
                if o.is_dma:
                    k = dcnt % NDS[e]
                    dcnt += 1
                    o.sem = dma_sems[e][k]
                    o.prev_same_sem = last_on_sem.get(k)
                    o.val = (o.prev_same_sem.val if o.prev_same_sem is not None else 0) + 16
                    last_on_sem[k] = o
                elif o.signal:
                    cnt += 1
                    o.sem = eng_sems[e]
                    o.val = cnt
        handles = {"pe": nc.tensor, "act": nc.scalar, "dve": nc.vector, "pool": nc.gpsimd, "sp": nc.sync}
        final_dma = {}
        for o in self.all_dma:
            final_dma[id(o.sem)] = (o.sem, max(o.val, final_dma.get(id(o.sem), (None, 0))[1]))

        def run(e):
            h = handles[e]
            waited = {}

            def wait(sem, val):
                if waited.get(id(sem), 0) >= val:
                    return
                waited[id(sem)] = val
                h.wait_ge(sem, val)

            for o in self.ops[e]:
                if o.is_dma and o.prev_same_sem is not None:
                    wait(o.prev_same_sem.sem, o.prev_same_sem.val)
                for d in o.deps:
                    wait(d.sem, d.val)
                inst = o.fn(h)
                if o.signal:
                    inst.then_inc(o.sem, 16 if o.is_dma else 1)
            if e == "sp":
                for sem, val in final_dma.values():
                    wait(sem, val)

        block.sync(lambda _e: run("sp"))
        block.tensor(lambda _e: run("pe"))
        block.scalar(lambda _e: run("act"))
        block.vector(lambda _e: run("dve"))
        block.gpsimd(lambda _e: run("pool"))


PAGE = 512
SB_BYTES = 206 * 1024


class Buf:
    def __init__(self, mem, off, nbytes, req=None):
        self.mem = mem
        self.off = off
        self.nbytes = nbytes
        self.req = nbytes if req is None else req
        self.res = mem.pages[off // PAGE:(off + nbytes + PAGE - 1) // PAGE]

    def ap(self, dtype, np_=128):
        return self.mem.big[0:np_, self.off:self.off + self.req].bitcast(dtype)

    def sub(self, b0, b1):
        return Buf(self.mem, self.off + b0, b1 - b0)


class Mem:
    def __init__(self, big):
        self.big = big
        self.pages = [R("pg%d" % i) for i in range(SB_BYTES // PAGE)]
        self.top = 0
        self.peak = 0

    def alloc(self, nbytes):
        nb = (nbytes + PAGE - 1) // PAGE * PAGE
        assert self.top + nb <= SB_BYTES, ("SBUF overflow", self.top, nb)
        b = Buf(self, self.top, nb, nbytes)
        self.top += nb
        self.peak = max(self.peak, self.top)
        return b

    def mark(self):
        return self.top

    def release(self, m):
        self.top = m


def res_of(items):
    out = []
    for it in items:
        if isinstance(it, R):
            out.append(it)
        elif isinstance(it, Buf):
            out.extend(it.res)
        else:
            out.extend(res_of(it))
    return out


class Prog:
    def __init__(self):
        self.nc = bass.Bass("TRN2", target_bir_lowering=False)
        self.S = Sched()
        self.din = {}
        self.dout = {}
        self.rdram = {}

    def inp(self, name, shape, dtype=F32):
        t = self.nc.dram_tensor(name, list(shape), dtype, kind="ExternalInput").ap()
        self.din[name] = t
        self.rdram[name] = R(name)
        return t

    def outp(self, name, shape, dtype=F32):
        t = self.nc.dram_tensor(name, list(shape), dtype, kind="ExternalOutput").ap()
        self.dout[name] = t
        self.rdram[name] = R(name)
        return t

    def scratch(self, name, shape, dtype):
        t = self.nc.dram_tensor(name, list(shape), dtype, kind="Internal").ap()
        self.rdram[name] = R(name)
        return t

    def dbg(self, name, ap, shape, dtype, reads):
        if not DEBUG:
            return
        if DBGSET and name not in DBGSET:
            return
        t = self.nc.dram_tensor("dbg_" + name, list(shape), dtype, kind="ExternalOutput").ap()
        self.rdram["dbg_" + name] = R("dbg_" + name)
        self.dma("sp", t, ap, reads, [self.rdram["dbg_" + name]], slow=True)

    def dma(self, eng, out, in_, reads, writes, slow=False):
        def fn(h):
            if slow:
                return h.dma_start(out=out, in_=in_, allow_slow_non_contiguous=True)
            return h.dma_start(out=out, in_=in_)
        return self.S.op(eng, fn, res_of(reads), res_of(writes), dma=True)

    def mm(self, items, reads, writes):
        def fn(h):
            inst = None
            for (o, l, r, st, sp) in items:
                inst = h.matmul(o, lhsT=l, rhs=r, start=st, stop=sp)
            return inst
        return self.S.op("pe", fn, res_of(reads), res_of(writes))

    def tr(self, items, reads, writes):
        def fn(h):
            inst = None
            for (o, i, idn) in items:
                inst = h.transpose(out=o, in_=i, identity=idn)
            return inst
        return self.S.op("pe", fn, res_of(reads), res_of(writes))

    def act(self, out, in_, func, reads, writes, scale=None, bias=None, eng="act"):
        def fn(h):
            kw = {}
            if scale is not None:
                kw["scale"] = scale
            if bias is not None:
                kw["bias"] = bias
            return h.activation(out=out, in_=in_, func=func, **kw)
        return self.S.op(eng, fn, res_of(reads), res_of(writes))

    def copy(self, eng, out, in_, reads, writes):
        if eng == "act":
            return self.act(out, in_, AF.Copy, reads, writes)

        def fn(h):
            return h.tensor_copy(out=out, in_=in_)
        return self.S.op(eng, fn, res_of(reads), res_of(writes))

    def tt(self, eng, out, in0, in1, op, reads, writes):
        def fn(h):
            return h.tensor_tensor(out=out, in0=in0, in1=in1, op=op)
        return self.S.op(eng, fn, res_of(reads), res_of(writes))

    def ts(self, eng, out, in0, s1, s2, op0, op1, reads, writes):
        def fn(h):
            if op1 is None:
                return h.tensor_scalar(out=out, in0=in0, scalar1=s1, scalar2=None, op0=op0)
            return h.tensor_scalar(out=out, in0=in0, scalar1=s1, scalar2=s2, op0=op0, op1=op1)
        return self.S.op(eng, fn, res_of(reads), res_of(writes))

    def stt(self, out, in0, scalar, in1, op0, op1, reads, writes):
        def fn(h):
            return h.scalar_tensor_tensor(out=out, in0=in0, scalar=scalar, in1=in1, op0=op0, op1=op1)
        return self.S.op("dve", fn, res_of(reads), res_of(writes))

    def memset(self, eng, ap, val, writes):
        def fn(h):
            return h.memset(ap, val)
        return self.S.op(eng, fn, [], res_of(writes))

    def recip(self, out, in_, reads, writes):
        def fn(h):
            return h.reciprocal(out=out, in_=in_)
        return self.S.op("dve", fn, res_of(reads), res_of(writes))


U8 = mybir.dt.uint8


class PSBank:
    def __init__(self, ap, r):
        self.ap = ap
        self.r = r


def build_program(stage=99, npre_tiles=None):
    from contextlib import ExitStack
    P = Prog()
    nc = P.nc
    rd = P.rdram
    xo = P.inp("xo", [TC, D]); xh = P.inp("xh", [TC, D]); xp = P.inp("xp", [NPRE * TC, D]); xs = P.inp("xs", [NS, D])
    w_in = P.inp("w_in", [D, NPROJ]); w_out = P.inp("w_out", [D, D]); w_glu = P.inp("w_glu", [1024, 1024])
    ident = P.inp("ident", [128, 128])
    k_o = P.outp("k_o", [TC, 1024]); v_o = P.outp("v_o", [TC, 1024])
    k_s = P.outp("k_s", [NS, 1024]); v_s = P.outp("v_s", [NS, 1024])
    y_o = P.outp("y_o", [TC, D]); y_s = P.outp("y_s", [NS, D])
    sre_o = P.outp("sre_o", [64, 64]); sim_o = P.outp("sim_o", [64, 64])
    sre_s = P.outp("sre_s", [4, 64, 64]); sim_s = P.outp("sim_s", [4, 64, 64])
    lam_re = P.inp("lam_re", [64, 64]); lam_im = P.inp("lam_im", [64, 64]); log_dt = P.inp("log_dt", [1, 64])
    b_re = P.inp("b_re", [64, 64, 16]); b_im = P.inp("b_im", [64, 64, 16])
    c_re = P.inp("c_re", [64, 16, 64]); c_im = P.inp("c_im", [64, 16, 64])
    d_skip = P.inp("d_skip", [1024, 1]); b_glu = P.inp("b_glu", [1024, 1])
    st_re = P.inp("st_re", [4, 64, 64]); st_im = P.inp("st_im", [4, 64, 64])
    cmask = P.inp("cmask", [128, 512]); ctri = P.inp("ctri", [128, 128]); ctris = P.inp("ctris", [16, 16])
    UT = P.scratch("UT", [1024, TC], BF16); GST = P.scratch("GST", [1024, TC], BF16)
    BR = P.scratch("BR", [2048, TC], BF16)
    GV = P.scratch("GV", [48, 384], F32)
    BRSD = P.scratch("BRSD", [1024, NS], F32)
    ck = P.inp("ck", [4, CL, 1024]); cv = P.inp("cv", [4, CL, 1024]); ccnt = P.inp("ccnt", [32, 32 * 128])
    rel_bias = P.inp("rel_bias", [32, 16]); coh = P.inp("coh", [32, 3 * 384]); cj = P.inp("cj", [128, 128]); hv = P.inp("hv", [128, 1])
    b_out = P.inp("b_out", [1, D]); ln_g = P.inp("ln_g", [1, D]); ln_b = P.inp("ln_b", [1, D])
    QT = P.scratch("QT", [1024, TC], BF16); GT = P.scratch("GT", [1024, TC], BF16)
    KT = P.scratch("KT", [1024, 2 * TC], BF16); VS = P.scratch("VS", [2 * TC, 1024], BF16)

    es = ExitStack()
    with es:
        big = es.enter_context(nc.sbuf_tensor("big", [128, SB_BYTES], U8))
        banks = []
        for i in range(8):
            t = es.enter_context(nc.psum_tensor("ps%d" % i, [128, 512], F32))
            banks.append(PSBank(t[:, :], R("ps%d" % i)))
        eng_sems = {e: es.enter_context(nc.semaphore("sem_" + e)) for e in Sched.ENGS}
        dma_sems = {e: [es.enter_context(nc.semaphore("dsem_%s%d" % (e, i))) for i in range(n)]
                    for e, n in (("sp", 24), ("act", 8), ("pool", 16), ("dve", 2), ("pe", 2))}
        M = Mem(big)
        psi = [0]

        def nextps():
            b = banks[psi[0] % 8]
            psi[0] += 1
            return b

        identb = M.alloc(128 * 4)
        identF = identb.ap(F32)
        P.dma("sp", identF, ident[:, :], [rd["ident"]], [identb])
        HS = M.alloc(48 * NS * 4)
        HS3 = HS.ap(F32).rearrange("p (f t) -> p f t", t=NS)
        XTs = M.alloc(16 * NS * 2)
        XTs3 = XTs.ap(BF16).rearrange("p (k t) -> p k t", t=NS)
        VSs = M.alloc(1024 * 2)
        BRS = M.alloc(16 * NS * 2)

        def sl_ap0(t, off=0):
            return bass.AP(t.tensor, off, [[1, 128], [128, 32]])
        kin = M.alloc(3 * 128); kina = kin.ap(F32)
        bin_ = [M.alloc(32 * 16 * 4), M.alloc(32 * 16 * 4)]
        x0in = [M.alloc(32 * 4 * 4), M.alloc(32 * 4 * 4)]
        dskb = M.alloc(32); bglb = M.alloc(32)

        def early_prefetch():
            P.dma("act", kina[:, 0:32], sl_ap0(lam_re), [rd["lam_re"]], [kin], slow=True)
            P.dma("act", kina[:, 32:64], sl_ap0(lam_im), [rd["lam_im"]], [kin], slow=True)
            for g2 in range(2):
                P.dma("act", kina[g2 * 64:(g2 + 1) * 64, 64:96], bass.AP(log_dt.tensor, g2, [[0, 64], [2, 32]]), [rd["log_dt"]], [kin], slow=True)
            for i, (nm, t_) in enumerate((("b_re", b_re), ("b_im", b_im))):
                P.dma("act", bin_[i].ap(F32).rearrange("p (a c) -> p a c", c=16), bass.AP(t_.tensor, 0, [[16, 128], [2048, 32], [1, 16]]),
                      [rd[nm]], [bin_[i]], slow=True)
            for ri, (nm, t_) in enumerate((("st_re", st_re), ("st_im", st_im))):
                for b in range(4):
                    P.dma("act", x0in[ri].ap(F32).rearrange("p (a b) -> p a b", b=4)[:, :, b], sl_ap0(t_, b * 4096), [rd[nm]], [x0in[ri]], slow=True)
            P.dma("act", dskb.ap(F32), d_skip.rearrange("(o p) one -> p (o one)", p=128), [rd["d_skip"]], [dskb], slow=True)
            P.dma("act", bglb.ap(F32), b_glu.rearrange("(o p) one -> p (o one)", p=128), [rd["b_glu"]], [bglb], slow=True)

        def load_xT(src, src_r, ntiles, XT, xst):
            XT4 = XT.ap(BF16).rearrange("p (t k c) -> p t k c", k=16, c=128)
            for tt in range(ntiles):
                xb = xst[tt % 2]
                xa = xb.ap(F32)
                P.dma("sp", xa, src[tt * 128:(tt + 1) * 128, :], [src_r], [xb])
                for q in range(4):
                    ps = nextps()
                    P.tr([(ps.ap[:, j * 128:(j + 1) * 128], xa[:, (4 * q + j) * 128:(4 * q + j + 1) * 128], identF)
                          for j in range(4)], [xb, identb], [ps.r])
                    P.copy("act" if q % 2 == 0 else "dve", XT4[:, tt, 4 * q:4 * q + 4, :],
                           ps.ap.rearrange("p (a b) -> p a b", a=4), [ps.r], [XT.sub(tt * 4096 + q * 1024, tt * 4096 + (q + 1) * 1024)])
            return XT4

        def load_wtile(wb, c0, ncols):
            w3 = wb.ap(BF16).rearrange("p (k f) -> p k f", k=16)
            P.dma("pool", w3, w_in[:, c0:c0 + ncols].rearrange("(k p) f -> p k f", p=128), [rd["w_in"]], [wb])
            return w3

        def mm_fm(ps, w3, wb, XT4, XT, tb):
            P.mm([(ps.ap[:, :], w3[:, kt, :], XT4[:, 4 * tb:4 * tb + 4, kt, :], kt == 0, kt == 15) for kt in range(16)],
                 [XT.sub(tb * 16384, (tb + 1) * 16384), wb], [ps.r])

        def mm_fm_sample(ps, w3, wb):
            P.mm([(ps.ap[:, 0:NS], w3[:, kt, :], XTs3[:, kt, :], kt == 0, kt == 15) for kt in range(16)],
                 [XTs, wb], [ps.r])

        m0 = M.mark()
        xsb = M.alloc(D * 4)
        xsa = xsb.ap(F32)
        P.dma("sp", xsa[0:NS, :], xs[:, :], [rd["xs"]], [xsb])
        ps = nextps()
        P.tr([(ps.ap[:, kt * NS:(kt + 1) * NS], xsa[0:NS, kt * 128:(kt + 1) * 128], identF[0:NS, 0:NS]) for kt in range(16)],
             [xsb, identb], [ps.r])
        P.copy("dve", XTs3[:, :, :], ps.ap[:, 0:16 * NS].rearrange("p (k t) -> p k t", t=NS), [ps.r], [XTs])
        M.release(m0)

        mB1 = M.mark()
        XT = M.alloc(16 * TC * 2)
        xst = [M.alloc(D * 4), M.alloc(D * 4)]
        wt = [M.alloc(16 * 128 * 2), M.alloc(16 * 128 * 2)]
        wblk = [M.alloc(16 * 512 * 2), M.alloc(16 * 512 * 2)]
        stg = [M.alloc(TC * 2), M.alloc(TC * 2)]
        ost = [M.alloc(512 * 4), M.alloc(512 * 4)]
        vbs = [M.alloc(512 * 2), M.alloc(512 * 2)]
        wi = [0]
        evq = [0]

        def ev_eng():
            evq[0] += 1
            return "act" if evq[0] % 2 == 0 else "dve"

        for grp in ("halo", "own"):
            src, src_r = (xh, rd["xh"]) if grp == "halo" else (xo, rd["xo"])
            XT4 = load_xT(src, src_r, 16, XT, xst)
            tok0 = 0 if grp == "halo" else TC
            fts = [("k", 8 + i) for i in range(8)]
            if grp == "own":
                fts = ([("q", i) for i in range(8)] + fts + [("g", 24 + i) for i in range(8)]
                       + [("u", 32 + i) for i in range(8)] + [("gs", 40 + i) for i in range(8)])
            for (kind, ft) in fts:
                wb = wt[wi[0] % 2]
                w3 = load_wtile(wb, ft * 128, 128)
                sb = stg[wi[0] % 2]
                wi[0] += 1
                sa = sb.ap(BF16)
                for tb in range(4):
                    ps = nextps()
                    mm_fm(ps, w3, wb, XT4, XT, tb)
                    P.copy(ev_eng(), sa[:, tb * 512:(tb + 1) * 512], ps.ap[:, :], [ps.r], [sb.sub(tb * 1024, (tb + 1) * 1024)])
                if kind == "q":
                    P.dma("sp", QT[ft * 128:(ft + 1) * 128, :], sa, [sb], [rd["QT"]])
                elif kind == "k":
                    r0 = (ft - 8) * 128
                    P.dma("sp", KT[r0:r0 + 128, tok0:tok0 + TC], sa, [sb], [rd["KT"]])
                elif kind == "g":
                    r0 = (ft - 24) * 128
                    P.dma("sp", GT[r0:r0 + 128, :], sa, [sb], [rd["GT"]])
                elif kind == "u":
                    r0 = (ft - 32) * 128
                    P.dma("sp", UT[r0:r0 + 128, :], sa, [sb], [rd["UT"]])
                else:
                    r0 = (ft - 40) * 128
                    P.dma("sp", GST[r0:r0 + 128, :], sa, [sb], [rd["GST"]])
                if grp == "own":
                    ps = nextps()
                    mm_fm_sample(ps, w3, wb)
                    P.copy(ev_eng(), HS3[:, ft, :], ps.ap[:, 0:NS], [ps.r], [HS])
            blks = [("v", 2048 + 512 * i) for i in range(2)]
            if grp == "own":
                blks = [("k", 1024 + 512 * i) for i in range(2)] + blks
            for bi, (kind, c0) in enumerate(blks):
                wb = wblk[bi % 2]
                w3 = load_wtile(wb, c0, 512)
                fcol = (c0 % 1024)
                for tt in range(16 + (1 if grp == "own" else 0)):
                    ps = nextps()
                    if tt < 16:
                        P.mm([(ps.ap[:, :], XT4[:, tt, kt, :], w3[:, kt, :], kt == 0, kt == 15) for kt in range(16)],
                             [XT.sub(tt * 4096, (tt + 1) * 4096), wb], [ps.r])
                        if grp == "own":
                            ob = ost[tt % 2]
                            P.copy("act", ob.ap(F32), ps.ap[:, :], [ps.r], [ob])
                            dst = k_o if kind == "k" else v_o
                            P.dma("sp", dst[tt * 128:(tt + 1) * 128, fcol:fcol + 512], ob.ap(F32), [ob], [rd[dst.tensor.name]])
                            if kind == "v":
                                vb = vbs[tt % 2]
                                P.copy("pool", vb.ap(BF16), ob.ap(F32), [ob], [vb])
                                P.dma("sp", VS[TC + tt * 128:TC + (tt + 1) * 128, fcol:fcol + 512], vb.ap(BF16), [vb], [rd["VS"]])
                        else:
                            vb = vbs[tt % 2]
                            P.copy(ev_eng(), vb.ap(BF16), ps.ap[:, :], [ps.r], [vb])
                            P.dma("sp", VS[tt * 128:(tt + 1) * 128, fcol:fcol + 512], vb.ap(BF16), [vb], [rd["VS"]])
                    else:
                        P.mm([(ps.ap[0:NS, :], XTs3[:, kt, :], w3[:, kt, :], kt == 0, kt == 15) for kt in range(16)],
                             [XTs, wb], [ps.r])
                        ob = ost[tt % 2]
                        P.copy("act", ob.ap(F32)[0:NS, :], ps.ap[0:NS, :], [ps.r], [ob])
                        dst = k_s if kind == "k" else v_s
                        P.dma("sp", dst[:, fcol:fcol + 512], ob.ap(F32)[0:NS, :], [ob], [rd[dst.tensor.name]])
                        if kind == "v":
                            P.copy("pool", VSs.ap(BF16)[0:NS, fcol:fcol + 512], ob.ap(F32)[0:NS, :], [ob], [VSs])
            if grp == "halo":
                early_prefetch()
        M.release(mB1)

        if stage >= 2:
            if npre_tiles is not None:
                pass
            _E = dict(locals())
            if npre_tiles is not None:
                _E["npre_tiles"] = npre_tiles
            mS = M.mark()
            ssm_phase(_E)
            M.release(mS)
        if stage >= 3:
            _E = dict(locals())
            m3 = M.mark()
            attn_tables(_E)
            m4 = M.mark()
            attn_phase(_E)
            M.release(m4)
            if not os.environ.get("KSKIP_SA"):
                sample_attn_phase(_E)
            M.release(m3)
            out_phase(_E)

        block = es.enter_context(nc.Block())
        P.S.emit(nc, block, eng_sems, dma_sems)
    print("ops:", {e: len(v) for e, v in P.S.ops.items()}, "sbuf peak", M.peak)
    return nc


def host_inputs(inputs):
    xpr = inputs["x_prompt"]
    maps = []
    zeros_chunk = np.zeros((TC, D), np.float32)
    for core in range(8):
        b, c = divmod(core, 4)
        xo = np.ascontiguousarray(xpr[b, c * TC:(c + 1) * TC])
        xh = np.ascontiguousarray(xpr[b, (c - 1) * TC:c * TC]) if c > 0 else zeros_chunk
        pre = []
        for j in range(NPRE):
            cc = c - NPRE + j
            pre.append(xpr[b, cc * TC:(cc + 1) * TC] if cc >= 0 else zeros_chunk)
        xp = np.ascontiguousarray(np.concatenate(pre, axis=0))
        xs = np.ascontiguousarray(inputs["x_sample"][core * 4:(core + 1) * 4].reshape(NS, D))
        m = {
            "xo": xo, "xh": xh, "xp": xp, "xs": xs,
            "w_in": np.ascontiguousarray(inputs["w_in"][0]),
            "w_out": np.ascontiguousarray(inputs["w_out"][0]),
            "w_glu": np.ascontiguousarray(inputs["w_glu"][0]),
            "ident": np.eye(128, dtype=np.float32),
            "lam_re": np.ascontiguousarray(inputs["lam_re"][0]), "lam_im": np.ascontiguousarray(inputs["lam_im"][0]),
            "log_dt": np.ascontiguousarray(inputs["log_dt"][0][None, :]),
            "b_re": np.ascontiguousarray(inputs["b_re"][0]), "b_im": np.ascontiguousarray(inputs["b_im"][0]),
            "c_re": np.ascontiguousarray(inputs["c_re"][0]), "c_im": np.ascontiguousarray(inputs["c_im"][0]),
            "d_skip": np.ascontiguousarray(inputs["d_skip"][0][:, None]), "b_glu": np.ascontiguousarray(inputs["b_glu"][0][:, None]),
            "st_re": np.ascontiguousarray(inputs["state_ssm_re"][0, core * 4:(core + 1) * 4]),
            "st_im": np.ascontiguousarray(inputs["state_ssm_im"][0, core * 4:(core + 1) * 4]),
            "cmask": CMASK, "ctri": CTRI, "ctris": CTRIS,
            "rel_bias": np.ascontiguousarray(inputs["rel_bias"]), "coh": COH, "cj": CJ,
            "hv": np.full((128, 1), 1.0 if c > 0 else 0.0, np.float32),
            "ck": np.ascontiguousarray(inputs["cache_k"][0, core * 4:(core + 1) * 4].reshape(4, CL, 1024)),
            "cv": np.ascontiguousarray(inputs["cache_v"][0, core * 4:(core + 1) * 4].reshape(4, CL, 1024)),
            "ccnt": CCNT,
            "b_out": np.ascontiguousarray(inputs["b_out"][0][None, :]),
            "ln_g": np.ascontiguousarray(inputs["ln_g"][0][None, :]), "ln_b": np.ascontiguousarray(inputs["ln_b"][0][None, :]),
        }
        maps.append(m)
    return maps


def _consts():
    cm = np.zeros((128, 4, 128), np.float32)
    for p in range(128):
        g2 = p // 64
        for q in range(4):
            cm[p, q, q * 32 + g2 * 16:q * 32 + g2 * 16 + 16] = 1.0
    tri = np.triu(np.ones((128, 128), np.float32))
    tris = np.zeros((16, 16), np.float32)
    for b in range(4):
        tris[b * 4:(b + 1) * 4, b * 4:(b + 1) * 4] = np.triu(np.ones((4, 4), np.float32))
    return cm.reshape(128, 512), tri, tris


def _t5_bucket(dist):
    dist = np.asarray(dist, np.int64)
    d_f = np.maximum(dist, 1).astype(np.float32)
    large = 16 + (np.log(d_f / np.float32(16.0)) / np.float32(math.log(2048 / 16)) * np.float32(16.0)).astype(np.int32)
    large = np.minimum(large, 31)
    return np.where(dist < 16, dist, large)


def _attn_consts():
    oh = np.zeros((32, 3, 384), np.float32)
    for p, d in enumerate((1, 4, 16)):
        for j in range(129):
            oh[_t5_bucket(j * d), p, 127 + j] = 1.0
    cj = np.ascontiguousarray(np.eye(128, dtype=np.float32)[::-1])
    return oh.reshape(32, 3 * 384), cj


def _sample_consts():
    cnt = np.zeros((32, 8, 4, 128), np.float32)
    for tile in range(8):
        for k in range(128):
            if tile < 4:
                idx = 1536 + 128 * tile + k
            elif tile < 7:
                sr, mm = divmod(k, 32)
                idx = 16 * (32 * (tile - 4) + mm) + sr
            else:
                if k >= 4:
                    continue
                idx = CL + k
            for s in range(4):
                dd = CL + s - idx
                if dd < 0:
                    continue
                mult = (1 if dd <= 128 else 0) + (1 if (dd % 4 == 0 and dd <= 512) else 0) + (1 if (dd % 16 == 0 and dd <= 2048) else 0)
                if mult:
                    cnt[int(_t5_bucket(dd)), tile, s, k] += mult
    return cnt.reshape(32, 32 * 128)


CMASK, CTRI, CTRIS = _consts()
COH, CJ = _attn_consts()
CCNT = _sample_consts()
_NC_CACHE = {}


def kernel(**inputs):
    inputs = {k: np.asarray(v) for k, v in inputs.items()}
    if "nc" not in _NC_CACHE:
        _NC_CACHE["nc"] = build_program()
    nc = _NC_CACHE["nc"]
    maps = host_inputs(inputs)
    res = run_bass_kernel_spmd(nc, maps, core_ids=list(range(8)))
    rs = res.results
    B = 2
    y_prompt = np.stack([np.concatenate([rs[b * 4 + c]["y_o"] for c in range(4)], axis=0) for b in range(B)])
    y_sample = np.concatenate([rs[i]["y_s"].reshape(4, 4, D) for i in range(8)], axis=0)
    k_prompt = np.stack([rs[b * 4 + 3]["k_o"].reshape(TC, NH, HD) for b in range(B)])[None]
    v_prompt = np.stack([rs[b * 4 + 3]["v_o"].reshape(TC, NH, HD) for b in range(B)])[None]
    sre_p = np.stack([rs[b * 4 + 3]["sre_o"] for b in range(B)])[None]
    sim_p = np.stack([rs[b * 4 + 3]["sim_o"] for b in range(B)])[None]
    k_sample = np.concatenate([rs[i]["k_s"].reshape(4, 4, NH, HD) for i in range(8)], axis=0)[None]
    v_sample = np.concatenate([rs[i]["v_s"].reshape(4, 4, NH, HD) for i in range(8)], axis=0)[None]
    sre_s = np.concatenate([rs[i]["sre_s"] for i in range(8)], axis=0)[None]
    sim_s = np.concatenate([rs[i]["sim_s"] for i in range(8)], axis=0)[None]
    outs = (y_prompt, y_sample, k_prompt, v_prompt, sre_p, sim_p, k_sample, v_sample, sre_s, sim_s)
    return tuple(np.ascontiguousarray(o, dtype=np.float32) for o in outs)


def ssm_phase(E):
    P, M, rd, banks = E["P"], E["M"], E["rd"], E["banks"]
    identF, identb = E["identF"], E["identb"]
    HS3, HS = E["HS3"], E["HS"]
    xp, w_in, w_glu = E["xp"], E["w_in"], E["w_glu"]
    UT, GST, BR = E["UT"], E["GST"], E["BR"]
    BRS = E["BRS"]
    PI = math.pi
    MUL, ADD, SUB = ALU.mult, ALU.add, ALU.subtract
    rot = [0]

    def nextps():
        b = banks[rot[0] % 5]
        rot[0] += 1
        return b
    ZS, YP0, YP1 = banks[7], banks[5], banks[6]

    Mtab = M.alloc(32 * 256 * 2); Mt3 = Mtab.ap(BF16).rearrange("p (a c) -> p a c", c=256)
    Brhs = M.alloc(32 * 256 * 2); Brhs3 = Brhs.ap(BF16).rearrange("p (a c) -> p a c", c=256)
    onesb = M.alloc(4); ones = onesb.ap(BF16)[:, 0:1]; nones = onesb.ap(BF16)[:, 1:2]
    P.memset("dve", onesb.ap(BF16)[:, 0:1], 1.0, [onesb])
    P.memset("dve", onesb.ap(BF16)[:, 1:2], -1.0, [onesb])
    PTr = M.alloc(32 * 128 * 2); PTi = M.alloc(32 * 128 * 2)
    Clr = M.alloc(32 * 128 * 2); Cli = M.alloc(32 * 128 * 2)
    Clr3 = Clr.ap(BF16).rearrange("p (a c) -> p a c", c=128); Cli3 = Cli.ap(BF16).rearrange("p (a c) -> p a c", c=128)
    wglb = M.alloc(8 * 1024 * 2); wgl3 = wglb.ap(BF16).rearrange("p (i j) -> p i j", j=1024)
    P.dma("pool", wgl3, w_glu.rearrange("(i p) j -> p i j", p=128), [rd["w_glu"]], [wglb])
    trib = M.alloc(256); tri = trib.ap(BF16)
    P.dma("pool", tri, E["ctri"][:, :], [rd["ctri"]], [trib])
    ntrib = M.alloc(256); ntri = ntrib.ap(BF16)
    P.ts("dve", ntri, tri, -1.0, None, ALU.mult, None, [trib], [ntrib])
    kp = M.alloc(26 * 128); kpa = kp.ap(F32)
    KEEP = ["abr", "abi", "A128r", "A128i", "k1", "k2", "k3", "k4", "f1", "f2", "fr", "fi", "alr", "ali", "q1", "q2", "q3", "air", "aii",
            "junkr", "junki", "p127r", "p127i"]
    cbuf = M.alloc(64 * 4); c2 = cbuf.ap(F32)
    P.memset("dve", c2, 0.0, [cbuf])
    zcb = M.alloc(64 * 4); zc2 = zcb.ap(F32)

    mprep = M.mark()
    pp = M.alloc(72 * 128)
    ppa = pp.ap(F32)
    names = {}

    def slot(n):
        if n in KEEP:
            i = KEEP.index(n)
            return kpa[:, i * 32:(i + 1) * 32], kp
        if n not in names:
            names[n] = len(names)
            assert len(names) <= 72
        i = names[n]
        return ppa[:, i * 32:(i + 1) * 32], pp

    def s_(n):
        return slot(n)[0]

    def sb_(*ns):
        return [slot(n)[1] for n in ns]

    def tt(o, a, b, op):
        P.tt("dve", s_(o), s_(a), s_(b), op, sb_(a, b), sb_(o))

    def tsc(o, a, c1, op0, c2_=None, op1=None):
        P.ts("dve", s_(o), s_(a), c1, c2_, op0, op1, sb_(a), sb_(o))

    def sl_ap(t, off=0):
        return bass.AP(t.tensor, off, [[1, 128], [128, 32]])

    kin = E["kin"]
    P.copy("dve", s_("lamre"), kin.ap(F32)[:, 0:32], [kin], [pp])
    P.copy("dve", s_("lamim"), kin.ap(F32)[:, 32:64], [kin], [pp])
    P.copy("dve", s_("logdt"), kin.ap(F32)[:, 64:96], [kin], [pp])
    tsc("lr", "lamre", -1e-4, ALU.min)
    tsc("x16", "logdt", 1.0 / 16.0, MUL)
    P.memset("dve", s_("pe"), 1.0, [pp])
    for k in range(10, 0, -1):
        tt("pe", "pe", "x16", MUL)
        tsc("pe", "pe", 1.0 / k, MUL, 1.0, ADD)
    for _ in range(4):
        tt("pe", "pe", "pe", MUL)
    tt("a", "lr", "pe", MUL)
    tt("th", "lamim", "pe", MUL)
    P.act(s_("mag"), s_("a"), AF.Exp, [pp], [pp])
    P.act(s_("magi"), s_("a"), AF.Exp, [pp], [pp], scale=-1.0)
    tsc("thc", "th", PI / 2, ADD)
    for nm in ("th", "thc"):
        for _ in range(4):
            tsc("m", nm, PI, ALU.is_gt)
            P.stt(s_(nm), s_("m"), -2.0 * PI, s_(nm), MUL, ADD, [pp], [pp])
    P.act(s_("sn"), s_("th"), AF.Sin, [pp], [pp])
    P.act(s_("cs"), s_("thc"), AF.Sin, [pp], [pp])
    tt("abr", "mag", "cs", MUL); tt("abi", "mag", "sn", MUL)
    tt("air", "magi", "cs", MUL)
    P.stt(s_("aii"), s_("magi"), -1.0, s_("sn"), MUL, MUL, [pp], [kp])
    tt("t1", "lr", "lr", MUL); tt("t2", "lamim", "lamim", MUL); tt("den", "t1", "t2", ADD)
    P.recip(s_("rden"), s_("den"), [pp], [pp])
    tt("invr", "lr", "rden", MUL)
    P.stt(s_("invi"), s_("lamim"), -1.0, s_("rden"), MUL, MUL, [pp], [pp])
    tsc("nr", "abr", -1.0, ADD)
    tt("t1", "nr", "invr", MUL); tt("t2", "abi", "invi", MUL); tt("cfr", "t1", "t2", SUB)
    tt("t1", "nr", "invi", MUL); tt("t2", "abi", "invr", MUL); tt("cfi", "t1", "t2", ADD)
    tsc("A128r", "abr", 1.0, MUL); tsc("A128i", "abi", 1.0, MUL)
    for _ in range(7):
        tt("q1", "A128r", "A128r", MUL); tt("q2", "A128i", "A128i", MUL); tt("q3", "A128r", "A128i", MUL)
        tt("A128r", "q1", "q2", SUB); tsc("A128i", "q3", 2.0, MUL)

    for nm in ("pe", "abr", "abi", "cfr", "cfi", "A128r", "A128i", "air", "aii"):
        P.dbg(nm, s_(nm), [128, 32], F32, sb_(nm))

    evq = [0]

    def ev_eng():
        evq[0] += 1
        return "act" if evq[0] % 2 == 0 else "dve"

    def to_time_major(dst3, dstbuf, src3, srcbuf, coff, NT):
        for pr0 in range(0, 32, 4):
            ps = nextps()
            P.tr([(ps.ap[0:NT, j * 128:(j + 1) * 128], src3[:, pr0 + j, 0:NT], identF) for j in range(4)],
                 [srcbuf, identb], [ps.r])
            P.copy(ev_eng(), dst3[0:NT, pr0:pr0 + 4, coff:coff + 128],
                   ps.ap[0:NT, :].rearrange("p (a b) -> p a b", a=4), [ps.r], [dstbuf])

    m1 = M.mark()
    cmb = M.alloc(512 * 4); cm3 = cmb.ap(F32).rearrange("p (q c) -> p q c", c=128)
    P.dma("sp", cmb.ap(F32), E["cmask"][:, :], [rd["cmask"]], [cmb])
    bb = list(E["bin_"]) + [M.alloc(32 * 16 * 4) for _ in range(4)]
    b3 = [b.ap(F32).rearrange("p (a c) -> p a c", c=16) for b in bb]
    cfr_b = s_("cfr").unsqueeze(2).to_broadcast([128, 32, 16]); cfi_b = s_("cfi").unsqueeze(2).to_broadcast([128, 32, 16])
    P.tt("dve", b3[2], b3[0], cfr_b, MUL, [bb[0], pp], [bb[2]]); P.tt("dve", b3[3], b3[1], cfi_b, MUL, [bb[1], pp], [bb[3]])
    P.tt("dve", b3[4], b3[2], b3[3], SUB, [bb[2], bb[3]], [bb[4]])
    P.tt("dve", b3[2], b3[1], cfr_b, MUL, [bb[1], pp], [bb[2]]); P.tt("dve", b3[3], b3[0], cfi_b, MUL, [bb[0], pp], [bb[3]])
    P.tt("dve", b3[5], b3[2], b3[3], ADD, [bb[2], bb[3]], [bb[5]])
    YW = [M.alloc(32 * 128 * 4), M.alloc(32 * 128 * 4)]
    YW3 = [y.ap(F32).rearrange("p (a c) -> p a c", c=128) for y in YW]
    for ri in range(2):
        for o in range(8):
            outv = YW3[ri][:, 4 * o:4 * o + 4, :].rearrange("p q (r c) -> p q r c", c=16)
            in0 = b3[4 + ri][:, 4 * o:4 * o + 4, :].unsqueeze(2).to_broadcast([128, 4, 8, 16])
            in1 = cm3.rearrange("p q (r c) -> p q r c", c=16)
            P.tt("dve", outv, in0, in1, MUL, [bb[4 + ri], cmb], [YW[ri]])
    to_time_major(Brhs3, Brhs, YW3[0], YW[0], 0, 128)
    to_time_major(Brhs3, Brhs, YW3[1], YW[1], 128, 128)
    P.dbg("bbr", b3[4], [128, 32, 16], F32, [bb[4]])
    P.dbg("yw0", YW3[0], [128, 32, 128], F32, [YW[0]])
    P.dbg("brhs", Brhs3, [128, 32, 256], BF16, [Brhs])
    M.release(m1)

    def power_tables(tabs):
        for T in tabs:
            T["Tr3"] = T["Tr"].ap(F32).rearrange("p (a t) -> p a t", t=128); T["Ti3"] = T["Ti"].ap(F32).rearrange("p (a t) -> p a t", t=128)
            T["tA3"] = T["tA"].ap(F32).rearrange("p (a t) -> p a t", t=64); T["tB3"] = T["tB"].ap(F32).rearrange("p (a t) -> p a t", t=64)
            T["sc"] = [M.alloc(128) for _ in range(5)]
            P.memset("dve", T["Tr3"][:, :, 0:1], 1.0, [T["Tr"]]); P.memset("dve", T["Ti3"][:, :, 0:1], 0.0, [T["Ti"]])
            P.ts("dve", T["sc"][0].ap(F32), s_(T["br"]), 1.0, None, MUL, None, [kp], [T["sc"][0]])
            P.ts("dve", T["sc"][1].ap(F32), s_(T["bi"]), 1.0, None, MUL, None, [kp], [T["sc"][1]])
        L = 1
        while L < 128:
            for T in tabs:
                alr, ali, q1, q2, q3 = T["sc"]
                Tr, Ti, tA, tB = T["Tr"], T["Ti"], T["tA"], T["tB"]
                br = alr.ap(F32).unsqueeze(2).to_broadcast([128, 32, L]); bi = ali.ap(F32).unsqueeze(2).to_broadcast([128, 32, L])
                sr = T["Tr3"][:, :, 0:L]; si = T["Ti3"][:, :, 0:L]
                t1 = T["tA3"][:, :, 0:L]; t2 = T["tB3"][:, :, 0:L]
                P.tt("dve", t1, sr, br, MUL, [Tr, alr], [tA]); P.tt("dve", t2, si, bi, MUL, [Ti, ali], [tB])
                P.tt("dve", T["Tr3"][:, :, L:2 * L], t1, t2, SUB, [tA, tB], [Tr])
                P.tt("dve", t1, sr, bi, MUL, [Tr, ali], [tA]); P.tt("dve", t2, si, br, MUL, [Ti, alr], [tB])
                P.tt("dve", T["Ti3"][:, :, L:2 * L], t1, t2, ADD, [tA, tB], [Ti])
                P.tt("dve", q1.ap(F32), alr.ap(F32), alr.ap(F32), MUL, [alr], [q1])
                P.tt("dve", q2.ap(F32), ali.ap(F32), ali.ap(F32), MUL, [ali], [q2])
                P.tt("dve", q3.ap(F32), alr.ap(F32), ali.ap(F32), MUL, [alr, ali], [q3])
                P.tt("dve", alr.ap(F32), q1.ap(F32), q2.ap(F32), SUB, [q1, q2], [alr])
                P.ts("dve", ali.ap(F32), q3.ap(F32), 2.0, None, MUL, None, [q3], [ali])
            L *= 2
    M.release(mprep)
    m1 = M.mark()
    TM = dict(Tr=M.alloc(32 * 128 * 4), Ti=M.alloc(32 * 128 * 4), tA=M.alloc(32 * 64 * 4), tB=M.alloc(32 * 64 * 4), br="air", bi="aii")
    TP = dict(Tr=M.alloc(32 * 128 * 4), Ti=M.alloc(32 * 128 * 4), tA=M.alloc(32 * 64 * 4), tB=M.alloc(32 * 64 * 4), br="abr", bi="abi")
    power_tables([TM, TP])
    MIr, MIi, MIr3, MIi3 = TM["Tr"], TM["Ti"], TM["Tr3"], TM["Ti3"]
    to_time_major(Mt3, Mtab, MIr3, MIr, 0, 128)
    to_time_major(Mt3, Mtab, MIi3, MIi, 128, 128)
    PTrf, PTif, PTrf3, PTif3 = TP["Tr"], TP["Ti"], TP["Tr3"], TP["Ti3"]
    PTr3 = PTr.ap(BF16).rearrange("p (a t) -> p a t", t=128); PTi3 = PTi.ap(BF16).rearrange("p (a t) -> p a t", t=128)
    P.copy("act", PTr3, PTrf3, [PTrf], [PTr]); P.copy("act", PTi3, PTif3, [PTif], [PTi])
    P.copy("dve", s_("p127r"), PTrf3[:, :, 127], [PTrf], [kp]); P.copy("dve", s_("p127i"), PTif3[:, :, 127], [PTif], [kp])
    P.dbg("mir", MIr3, [128, 32, 128], F32, [MIr])
    P.dbg("mtab", Mt3, [128, 32, 256], BF16, [Mtab])
    M.release(m1)
    cmb = M.alloc(512 * 4); cm3 = cmb.ap(F32).rearrange("p (q c) -> p q c", c=128)
    P.dma("sp", cmb.ap(F32), E["cmask"][:, :], [rd["cmask"]], [cmb])
    cn2 = [M.alloc(8 * 128 * 4), M.alloc(8 * 128 * 4)]
    ct2 = [M.alloc(8 * 128 * 4), M.alloc(8 * 128 * 4)]
    for ri, nm in enumerate(("c_re", "c_im")):
        c3 = cn2[ri].ap(F32).rearrange("p (o c) -> p o c", c=128)
        srcv = E[nm].rearrange("(o g) c n -> (g c) o n", o=8)
        for dup in range(2):
            P.dma("sp", c3[:, :, dup * 64:(dup + 1) * 64], srcv, [rd[nm]], [cn2[ri]])
        t3 = ct2[ri].ap(F32).rearrange("p (o c) -> p o c", c=128)
        for o0 in range(0, 8, 4):
            ps = nextps()
            P.tr([(ps.ap[:, j * 128:(j + 1) * 128], c3[:, o0 + j, :], identF) for j in range(4)], [cn2[ri], identb], [ps.r])
            P.copy(ev_eng(), t3[:, o0:o0 + 4, :], ps.ap[:, :].rearrange("p (a b) -> p a b", a=4), [ps.r], [ct2[ri]])
        dst3 = Clr3 if ri == 0 else Cli3
        dstb = Clr if ri == 0 else Cli
        for o in range(8):
            in0 = t3[:, o:o + 1, :].to_broadcast([128, 4, 128])
            if ri == 0:
                P.tt("dve", dst3[:, 4 * o:4 * o + 4, :], in0, cm3, MUL, [ct2[ri], cmb], [dstb])
            else:
                P.stt(dst3[:, 4 * o:4 * o + 4, :], in0, -1.0, cm3, MUL, MUL, [ct2[ri], cmb], [dstb])
    M.release(mprep)
    BU = [M.alloc(8 * 256 * 2), M.alloc(8 * 256 * 2)]
    VQ = [M.alloc(8 * 256 * 2), M.alloc(8 * 256 * 2)]
    TD = [[M.alloc(8 * 128 * 2) for _ in range(4)] for _ in range(2)]

    qcount = [0]

    def bu_issue(uT3, ubuf, NT, qd):
        slot = qcount[0] % 2
        qcount[0] += 1
        bu3 = BU[slot].ap(BF16).rearrange("p (a c) -> p a c", c=256)
        for k in range(4):
            ps = nextps()
            pr0 = qd * 8 + 2 * k
            P.mm([(ps.ap[0:NT, 0:512], uT3[:, pr0 // 4, 0:NT], Brhs3[:, pr0:pr0 + 2, :], True, True)],
                 [ubuf, Brhs], [ps.r])
            P.copy("act", bu3[0:NT, 2 * k:2 * k + 2, :], ps.ap[0:NT, :].rearrange("p (a b) -> p a b", a=2),
                   [ps.r], [BU[slot].sub(k * 1024, (k + 1) * 1024)])
        return slot

    def demod(NT, qd, slot):
        bu3 = BU[slot].ap(BF16).rearrange("p (a c) -> p a c", c=256)
        vqb = VQ[slot]
        vq3 = vqb.ap(BF16).rearrange("p (a c) -> p a c", c=256)
        mr = Mt3[0:NT, qd * 8:(qd + 1) * 8, 0:128]; mi = Mt3[0:NT, qd * 8:(qd + 1) * 8, 128:256]
        br = bu3[0:NT, :, 0:128]; bi = bu3[0:NT, :, 128:256]
        td = TD[slot]
        t = [td[i].ap(BF16).rearrange("p (a c) -> p a c", c=128)[0:NT] for i in range(4)]
        P.tt("dve", t[0], mr, br, MUL, [Mtab, BU[slot]], [td[0]])
        P.tt("dve", t[1], mi, bi, MUL, [Mtab, BU[slot]], [td[1]])
        P.tt("dve", t[2], mr, bi, MUL, [Mtab, BU[slot]], [td[2]])
        P.tt("dve", t[3], mi, br, MUL, [Mtab, BU[slot]], [td[3]])
        return t, td

    def colsum_quarter(qd, t, td):
        items = []
        for j in range(8):
            pr = qd * 8 + j
            items.append((ZS.ap[:, pr:pr + 1], t[0][:, j, :], ones, True, False))
            items.append((ZS.ap[:, pr:pr + 1], t[1][:, j, :], nones, False, True))
            items.append((ZS.ap[:, 32 + pr:33 + pr], t[2][:, j, :], ones, True, False))
            items.append((ZS.ap[:, 32 + pr:33 + pr], t[3][:, j, :], ones, False, True))
        P.mm(items, list(td) + [onesb], [ZS.r])

    def carry_update():
        P.tt("dve", zc2, ZS.ap[:, 0:64], c2, ADD, [ZS.r, cbuf], [zcb])
        P.tt("dve", s_("k1"), zc2[:, 0:32], s_("A128r"), MUL, [zcb, kp], [kp])
        P.tt("dve", s_("k2"), zc2[:, 32:64], s_("A128i"), MUL, [zcb, kp], [kp])
        P.tt("dve", s_("k3"), zc2[:, 0:32], s_("A128i"), MUL, [zcb, kp], [kp])
        P.tt("dve", s_("k4"), zc2[:, 32:64], s_("A128r"), MUL, [zcb, kp], [kp])

    def carry_commit():
        P.tt("dve", c2[:, 0:32], s_("k1"), s_("k2"), SUB, [kp], [cbuf])
        P.tt("dve", c2[:, 32:64], s_("k3"), s_("k4"), ADD, [kp], [cbuf])

    mpre = M.mark()
    Wu = M.alloc(16 * 1024 * 2); Wu3 = Wu.ap(BF16).rearrange("p (k f) -> p k f", k=16)
    P.dma("pool", Wu3, w_in[:, 4096:5120].rearrange("(k p) f -> p k f", p=128), [rd["w_in"]], [Wu])
    xst = [M.alloc(D * 4), M.alloc(D * 4)]
    XTt = [M.alloc(16 * 128 * 2), M.alloc(16 * 128 * 2)]
    uTb = [M.alloc(8 * 128 * 2), M.alloc(8 * 128 * 2)]
    npre_tiles = E.get("npre_tiles")
    if npre_tiles is None:
        npre_tiles = NPRE * 16
    def stage_a(tt_, part):
        xb = xst[tt_ % 2]; xa = xb.ap(F32)
        xtb = XTt[tt_ % 2]; xt3 = xtb.ap(BF16).rearrange("p (k c) -> p k c", c=128)
        ub = uTb[tt_ % 2]; u3 = ub.ap(BF16).rearrange("p (o t) -> p o t", t=128)
        if part == 0:
            P.dma("sp", xa, xp[tt_ * 128:(tt_ + 1) * 128, :], [rd["xp"]], [xb])
        if part in (0, 1):
            for qq in (2 * part, 2 * part + 1):
                ps = nextps()
                P.tr([(ps.ap[:, j * 128:(j + 1) * 128], xa[:, (4 * qq + j) * 128:(4 * qq + j + 1) * 128], identF) for j in range(4)],
                     [xb, identb], [ps.r])
                P.copy("act" if qq % 2 == 0 else "dve", xt3[:, 4 * qq:4 * qq + 4, :], ps.ap.rearrange("p (a b) -> p a b", a=4), [ps.r],
                       [xtb.sub(qq * 1024, (qq + 1) * 1024)])
        else:
            half = part - 2
            ps = nextps()
            for f4 in range(4):
                ft = half * 4 + f4
                P.mm([(ps.ap[:, f4 * 128:(f4 + 1) * 128], Wu3[:, kt, ft * 128:(ft + 1) * 128], xt3[:, kt, :], kt == 0, kt == 15)
                      for kt in range(16)], [xtb, Wu], [ps.r])
            P.copy("act" if half == 0 else "dve", u3[:, half * 4:half * 4 + 4, :], ps.ap.rearrange("p (a b) -> p a b", a=4), [ps.r],
                   [ub.sub(half * 1024, (half + 1) * 1024)])
        return u3, ub

    if npre_tiles > 0:
        for part in range(4):
            stage_a(0, part)
    for tt_ in range(npre_tiles):
        ub = uTb[tt_ % 2]; u3 = ub.ap(BF16).rearrange("p (o t) -> p o t", t=128)
        slots = {0: bu_issue(u3, ub, 128, 0)}
        for qd in range(4):
            if qd + 1 < 4:
                slots[qd + 1] = bu_issue(u3, ub, 128, qd + 1)
            tq, tdq = demod(128, qd, slots[qd])
            if tt_ + 1 < npre_tiles:
                stage_a(tt_ + 1, qd)
            colsum_quarter(qd, tq, tdq)
        carry_update()
        carry_commit()
    M.release(mpre)

    dskb = E["dskb"]; dsk = dskb.ap(F32)
    bglb = E["bglb"]; bgl = bglb.ap(F32)

    ZC = [[M.alloc(4 * 128 * 2), M.alloc(4 * 128 * 2)] for _ in range(2)]
    RT = [[M.alloc(4 * 128 * 2) for _ in range(4)] for _ in range(2)]
    XG = [[M.alloc(4 * 128 * 2), M.alloc(4 * 128 * 2)] for _ in range(2)]
    Yb = M.alloc(8 * 128 * 4); Zb = M.alloc(8 * 128 * 2); Gb = M.alloc(8 * 128 * 2); SGb = M.alloc(8 * 128 * 2)
    W1 = M.alloc(8 * 128 * 4); W2 = M.alloc(8 * 128 * 2); BRt = [M.alloc(8 * 128 * 2)]
    YPS = [YP0, YP1]

    def y_octet(o, NT, xr3, xi3, xbufs):
        items = []
        for j in range(4):
            pr = 4 * o + j
            dst = YPS[o // 4].ap[:, (o % 4) * NT:(o % 4 + 1) * NT]
            items.append((dst, Clr3[:, pr, :], xr3[:, j, 0:NT], j == 0, False))
            items.append((dst, Cli3[:, pr, :], xi3[:, j, 0:NT], False, j == 3))
        P.mm(items, xbufs + [Clr, Cli], [YPS[o // 4].r])

    def glu_tail(NT, uT3, ubuf, gsrc3, gsbuf, br_out3, br_buf):
        y3 = Yb.ap(F32)[:, 0:8 * NT].rearrange("p (o t) -> p o t", t=NT)
        for o in range(8):
            P.stt(y3[:, o, :], uT3[:, o, 0:NT], dsk[:, o:o + 1], YPS[o // 4].ap[:, (o % 4) * NT:(o % 4 + 1) * NT], MUL, ADD,
                  [ubuf, dskb, YPS[o // 4].r], [Yb])
        yf = Yb.ap(F32)[:, 0:8 * NT]; w1 = W1.ap(F32)[:, 0:8 * NT]; w2 = W2.ap(BF16)[:, 0:8 * NT]
        zf = Zb.ap(BF16)[:, 0:8 * NT]; z3 = zf.rearrange("p (o t) -> p o t", t=NT)
        P.act(w1, yf, AF.Square, [Yb], [W1])
        P.ts("dve", w1, w1, 0.044715, 1.0, MUL, ADD, [W1], [W1])
        P.tt("dve", w1, w1, yf, MUL, [W1, Yb], [W1])
        P.act(w2, w1, AF.Sigmoid, [W1], [W2], scale=1.5957691216057308)
        P.tt("dve", zf, yf, w2, MUL, [Yb, W2], [Zb])
        gps = [nextps(), nextps()]
        for j in range(8):
            P.mm([(gps[j // 4].ap[:, (j % 4) * NT:(j % 4 + 1) * NT], wgl3[:, i, j * 128:(j + 1) * 128], z3[:, i, :], i == 0, i == 7)
                  for i in range(8)], [Zb, wglb], [gps[j // 4].r])
        g3 = Gb.ap(BF16)[:, 0:8 * NT].rearrange("p (o t) -> p o t", t=NT)
        for j in range(8):
            P.act(g3[:, j, :], gps[j // 4].ap[:, (j % 4) * NT:(j % 4 + 1) * NT], AF.Sigmoid, [gps[j // 4].r, bglb], [Gb],
                  bias=bgl[:, j:j + 1])
        sg = SGb.ap(BF16)[:, 0:8 * NT]
        P.act(sg.rearrange("p (o t) -> p o t", t=NT), gsrc3, AF.Silu, [gsbuf], [SGb])
        w3_ = W2.ap(BF16)[:, 0:8 * NT]
        P.tt("dve", w3_, zf, Gb.ap(BF16)[:, 0:8 * NT], MUL, [Zb, Gb], [W2])
        P.tt("dve", br_out3, w3_.rearrange("p (o t) -> p o t", t=NT), sg.rearrange("p (o t) -> p o t", t=NT), MUL, [W2, SGb], [br_buf])

    ms = M.mark()
    us = M.alloc(8 * 16 * 2); us3 = us.ap(BF16).rearrange("p (o t) -> p o t", t=16)
    P.copy("dve", us3, HS3[:, 32:40, :], [HS], [us])
    bur, bui = nextps(), nextps()
    P.mm([(bur.ap[:, pr * 16:(pr + 1) * 16], Brhs3[:, pr, 0:128], us3[:, pr // 4, :], True, True) for pr in range(32)], [us, Brhs], [bur.r])
    P.mm([(bui.ap[:, pr * 16:(pr + 1) * 16], Brhs3[:, pr, 128:256], us3[:, pr // 4, :], True, True) for pr in range(32)], [us, Brhs], [bui.r])
    XS = [M.alloc(32 * 16 * 4), M.alloc(32 * 16 * 4)]
    XS4 = [b.ap(F32).rearrange("p (a b t) -> p a b t", b=4, t=4) for b in XS]
    x0 = list(E["x0in"]) + [M.alloc(32 * 4 * 4) for _ in range(2)]
    x03 = [b.ap(F32).rearrange("p (a b) -> p a b", b=4) for b in x0]
    abr_b = s_("abr").unsqueeze(2).to_broadcast([128, 32, 4]); abi_b = s_("abi").unsqueeze(2).to_broadcast([128, 32, 4])
    bur4 = bur.ap.rearrange("p (a b t) -> p a b t", b=4, t=4); bui4 = bui.ap.rearrange("p (a b t) -> p a b t", b=4, t=4)
    for tau in range(4):
        pr_, pi_ = (x03[0], x03[1]) if tau == 0 else (XS4[0][:, :, :, tau - 1], XS4[1][:, :, :, tau - 1])
        srcb = [x0[0], x0[1]] if tau == 0 else [XS[0], XS[1]]
        P.tt("dve", x03[2], pr_, abr_b, MUL, srcb + [kp], [x0[2]]); P.tt("dve", x03[3], pi_, abi_b, MUL, srcb + [kp], [x0[3]])
        P.tt("dve", x03[2], x03[2], x03[3], SUB, [x0[2], x0[3]], [x0[2]])
        P.tt("dve", XS4[0][:, :, :, tau], x03[2], bur4[:, :, :, tau], ADD, [x0[2], bur.r], [XS[0]])
        P.tt("dve", x03[2], pr_, abi_b, MUL, srcb + [kp], [x0[2]]); P.tt("dve", x03[3], pi_, abr_b, MUL, srcb + [kp], [x0[3]])
        P.tt("dve", x03[2], x03[2], x03[3], ADD, [x0[2], x0[3]], [x0[2]])
        P.tt("dve", XS4[1][:, :, :, tau], x03[2], bui4[:, :, :, tau], ADD, [x0[2], bui.r], [XS[1]])
    for ri, nm in enumerate(("sre_s", "sim_s")):
        for b in range(4):
            P.dma("pool", sl_ap(E[nm], b * 4096), XS4[ri][:, :, b, 3], [XS[ri]], [rd[nm]], slow=True)
    P.dbg("hs", HS3, [128, 48, 16], F32, [HS])
    P.dbg("xs0", XS[0].ap(F32), [128, 512], F32, [XS[0]])
    P.dbg("x0r", x0[0].ap(F32), [128, 128], F32, [x0[0]])
    XSb = [M.alloc(32 * 16 * 2), M.alloc(32 * 16 * 2)]
    XSb3 = [b.ap(BF16).rearrange("p (a t) -> p a t", t=16) for b in XSb]
    for ri in range(2):
        P.copy("dve", XSb3[ri], XS[ri].ap(F32).rearrange("p (a t) -> p a t", t=16), [XS[ri]], [XSb[ri]])
    for o in range(8):
        y_octet(o, 16, XSb3[0][:, 4 * o:4 * o + 4, :], XSb3[1][:, 4 * o:4 * o + 4, :], [XSb[0], XSb[1]])
    brs3 = BRS.ap(BF16).rearrange("p (f t) -> p f t", t=16)
    glu_tail(16, us3, us, HS3[:, 40:48, :], HS, brs3[:, 8:16, :], BRS)
    M.release(ms)

    uTo = [M.alloc(8 * 128 * 2) for _ in range(3)]
    gso = [M.alloc(8 * 128 * 2) for _ in range(3)]
    gi = [0]

    def own_load(tt_):
        ub = uTo[tt_ % 3]; u3 = ub.ap(BF16).rearrange("p (o t) -> p o t", t=128)
        P.dma("sp", u3, UT[:, tt_ * 128:(tt_ + 1) * 128].rearrange("(o p) t -> p o t", p=128), [rd["UT"]], [ub])
        gb = gso[tt_ % 3]; g3 = gb.ap(BF16).rearrange("p (o t) -> p o t", t=128)
        P.dma("sp", g3, GST[:, tt_ * 128:(tt_ + 1) * 128].rearrange("(o p) t -> p o t", p=128), [rd["GST"]], [gb])

    ZC4 = {(a, g): [M.alloc(4 * 128 * 2), M.alloc(4 * 128 * 2)] for a in range(2) for g in range(2)}
    XG4 = {(a, g): [M.alloc(4 * 128 * 2), M.alloc(4 * 128 * 2)] for a in range(2) for g in range(2)}
    st = {}

    def tile_bufs(t):
        ub = uTo[t % 3]; u3 = ub.ap(BF16).rearrange("p (o t) -> p o t", t=128)
        gb = gso[t % 3]; g3 = gb.ap(BF16).rearrange("p (o t) -> p o t", t=128)
        return u3, ub, g3, gb

    def S1(Q):
        t, q = divmod(Q, 4)
        u3, ub, g3, gb = tile_bufs(t)
        if q == 0 and t + 1 < 16:
            own_load(t + 1)
        st[("slot", Q)] = bu_issue(u3, ub, 128, q)

    def S2(Q):
        t, q = divmod(Q, 4)
        st[("vq", Q)] = demod(128, q, st.pop(("slot", Q)))

    def S3(Q):
        t, q = divmod(Q, 4)
        tq, tdq = st.pop(("vq", Q))
        colsum_quarter(q, tq, tdq)
        for gg in range(2):
            o = q * 2 + gg
            zr, zi = nextps(), nextps()
            itr, iti = [], []
            for j in range(4):
                itr.append((zr.ap[:, j * 128:(j + 1) * 128], tq[0][:, gg * 4 + j, :], tri, True, False))
                itr.append((zr.ap[:, j * 128:(j + 1) * 128], tq[1][:, gg * 4 + j, :], ntri, False, True))
                iti.append((zi.ap[:, j * 128:(j + 1) * 128], tq[2][:, gg * 4 + j, :], tri, True, False))
                iti.append((zi.ap[:, j * 128:(j + 1) * 128], tq[3][:, gg * 4 + j, :], tri, False, True))
            P.mm(itr, list(tdq) + [trib, ntrib], [zr.r])
            P.mm(iti, list(tdq) + [trib], [zi.r])
            zcs = ZC4[(Q % 2, gg)]
            zcr = zcs[0].ap(BF16).rearrange("p (a t) -> p a t", t=128); zci = zcs[1].ap(BF16).rearrange("p (a t) -> p a t", t=128)
            for j in range(4):
                P.act(zcr[:, j, :], zr.ap[:, j * 128:(j + 1) * 128], AF.Identity, [zr.r, cbuf], [zcs[0]],
                      bias=c2[:, 4 * o + j:4 * o + j + 1])
                P.act(zci[:, j, :], zi.ap[:, j * 128:(j + 1) * 128], AF.Identity, [zi.r, cbuf], [zcs[1]],
                      bias=c2[:, 32 + 4 * o + j:32 + 4 * o + j + 1])
        if q == 3:
            carry_update()
            if t == 15:
                p127r = s_("p127r"); p127i = s_("p127i")
                P.tt("dve", s_("f1"), zc2[:, 0:32], p127r, MUL, [zcb, kp], [kp]); P.tt("dve", s_("f2"), zc2[:, 32:64], p127i, MUL, [zcb, kp], [kp])
                P.tt("dve", s_("fr"), s_("f1"), s_("f2"), SUB, [kp], [kp])
                P.tt("dve", s_("f1"), zc2[:, 0:32], p127i, MUL, [zcb, kp], [kp]); P.tt("dve", s_("f2"), zc2[:, 32:64], p127r, MUL, [zcb, kp], [kp])
                P.tt("dve", s_("fi"), s_("f1"), s_("f2"), ADD, [kp], [kp])
                P.dma("pool", sl_ap(E["sre_o"]), s_("fr"), [kp], [rd["sre_o"]], slow=True)
                P.dma("pool", sl_ap(E["sim_o"]), s_("fi"), [kp], [rd["sim_o"]], slow=True)
            carry_commit()

    def S4(Q):
        t, q = divmod(Q, 4)
        for gg in range(2):
            o = q * 2 + gg
            zcs = ZC4[(Q % 2, gg)]; xs = XG4[(Q % 2, gg)]; rts = RT[gg]
            zcr = zcs[0].ap(BF16).rearrange("p (a t) -> p a t", t=128); zci = zcs[1].ap(BF16).rearrange("p (a t) -> p a t", t=128)
            pr_ = PTr3[:, 4 * o:4 * o + 4, :]; pi_ = PTi3[:, 4 * o:4 * o + 4, :]
            r = [rts[i].ap(BF16).rearrange("p (a t) -> p a t", t=128) for i in range(4)]
            xr3 = xs[0].ap(BF16).rearrange("p (a t) -> p a t", t=128); xi3 = xs[1].ap(BF16).rearrange("p (a t) -> p a t", t=128)
            P.tt("dve", r[0], zcr, pr_, MUL, [zcs[0], PTr], [rts[0]])
            P.tt("dve", r[1], zci, pi_, MUL, [zcs[1], PTi], [rts[1]])
            P.tt("dve", r[2], zcr, pi_, MUL, [zcs[0], PTi], [rts[2]])
            P.tt("dve", r[3], zci, pr_, MUL, [zcs[1], PTr], [rts[3]])
            P.tt("dve", xr3, r[0], r[1], SUB, [rts[0], rts[1]], [xs[0]])
            P.tt("dve", xi3, r[2], r[3], ADD, [rts[2], rts[3]], [xs[1]])

    def S5(Q):
        t, q = divmod(Q, 4)
        for gg in range(2):
            o = q * 2 + gg
            xs = XG4[(Q % 2, gg)]
            xr3 = xs[0].ap(BF16).rearrange("p (a t) -> p a t", t=128); xi3 = xs[1].ap(BF16).rearrange("p (a t) -> p a t", t=128)
            y_octet(o, 128, xr3, xi3, [xs[0], xs[1]])
        if q == 3:
            u3, ub, g3, gb = tile_bufs(t)
            bt = BRt[0]
            bt3 = bt.ap(BF16).rearrange("p (o t) -> p o t", t=128)
            glu_tail(128, u3, ub, g3, gb, bt3, bt)
            P.dma("sp", BR[1024:2048, t * 128:(t + 1) * 128].rearrange("(o p) t -> p o t", p=128), bt3, [bt], [rd["BR"]])

    own_load(0)
    NQ = 64
    for k in range(NQ + 4):
        if k < NQ:
            S1(k)
        if 0 <= k - 1 < NQ:
            S2(k - 1)
        if 0 <= k - 3 < NQ:
            S4(k - 3)
        if 0 <= k - 2 < NQ:
            S3(k - 2)
        if 0 <= k - 4 < NQ:
            S5(k - 4)


DILS = (1, 4, 16)


def attn_tables(E):
    P, M, rd, banks = E["P"], E["M"], E["rd"], E["banks"]
    nextps = E["nextps"]
    TT = M.alloc(48 * 256 * 2); TT3 = TT.ap(BF16).rearrange("p (a c) -> p a c", c=256)
    E32 = M.alloc(16 * 4)
    P.dma("sp", E32.ap(F32)[0:32, :], E["rel_bias"][:, :], [rd["rel_bias"]], [E32])
    P.act(E32.ap(F32)[0:32, :], E32.ap(F32)[0:32, :], AF.Exp, [E32], [E32])
    m0 = M.mark()
    OH = M.alloc(3 * 384 * 4); OH3 = OH.ap(F32).rearrange("p (a c) -> p a c", c=384)
    P.dma("sp", OH.ap(F32)[0:32, :], E["coh"][:, :], [rd["coh"]], [OH])
    Jb = M.alloc(128 * 4)
    P.dma("sp", Jb.ap(F32), E["cj"][:, :], [rd["cj"]], [Jb])
    gst = M.alloc(384 * 4)
    GV = E["GV"]
    for p in range(3):
        ps = nextps()
        P.mm([(ps.ap[0:16, 0:384], E32.ap(F32)[0:32, :], OH3[0:32, p, :], True, True)], [E32, OH], [ps.r])
        P.copy("dve", gst.ap(F32)[0:16, :], ps.ap[0:16, 0:384], [ps.r], [gst])
        P.dma("sp", GV[p * 16:(p + 1) * 16, :], gst.ap(F32)[0:16, :], [gst], [rd["GV"]])
    hst = [M.alloc(512 * 4), M.alloc(512 * 4)]
    for i in range(24):
        hb = hst[i % 2]
        P.dma("sp", hb.ap(F32).rearrange("p (a c) -> p a c", c=256),
              bass.AP(GV.tensor, i * 2 * 384, [[1, 128], [384, 2], [1, 256]]), [rd["GV"]], [hb])
        ps = nextps()
        P.mm([(ps.ap[:, :], Jb.ap(F32), hb.ap(F32), True, True)], [Jb, hb], [ps.r])
        P.copy("act" if i % 2 == 0 else "dve", TT3[:, 2 * i:2 * i + 2, :], ps.ap.rearrange("p (a c) -> p a c", c=256), [ps.r], [TT])
    M.release(m0)
    E["TT"], E["TT3"], E["E32"] = TT, TT3, E32


def attn_phase(E):
    P, M, rd, banks = E["P"], E["M"], E["rd"], E["banks"]
    nextps = E["nextps"]
    QT, KT, GT, VS, BR = E["QT"], E["KT"], E["GT"], E["VS"], E["BR"]
    TT, TT3 = E["TT"], E["TT3"]
    MUL, ADD = ALU.mult, ALU.add
    hvb = M.alloc(4)
    P.dma("sp", hvb.ap(F32), E["hv"][:, :], [rd["hv"]], [hvb])
    onf = M.alloc(64 * 4)
    P.memset("dve", onf.ap(F32), 1.0, [onf])
    NT_H = 21
    HOFF = (0, 1, 5)
    OOFF = (21, 37, 53)
    VA = [M.alloc(69 * 65 * 2), M.alloc(69 * 65 * 2)]
    VA3 = [v.ap(BF16).rearrange("p (t c) -> p t c", c=65) for v in VA]
    for i in range(2):
        P.memset("pool", VA[i].ap(BF16), 1.0, [VA[i]])
        P.copy("dve", VA3[i][:, 0:NT_H, 64:65], hvb.ap(F32)[:, 0:1].unsqueeze(1).to_broadcast([128, NT_H, 1]), [hvb], [VA[i]])
    KTh = [M.alloc(4096 * 2), M.alloc(4096 * 2)]
    QTh = [M.alloc(2048 * 2), M.alloc(2048 * 2)]
    GTh = [M.alloc(2048 * 2), M.alloc(2048 * 2)]
    ACC = [M.alloc(2048 * 4), M.alloc(2048 * 4)]
    NEB = 6
    EB = [M.alloc(256 * 2) for _ in range(NEB)]
    PB = [M.alloc(256 * 2) for _ in range(NEB)]
    SG = M.alloc(2048 * 2)
    BRh = [M.alloc(2048 * 2), M.alloc(2048 * 2)]
    LN_ = M.alloc(2048 * 4)
    ei = [0]
    def head_loads(h):
        k2 = h % 2
        kt = KTh[k2].ap(BF16); qt = QTh[k2].ap(BF16); gt = GTh[k2].ap(BF16)
        P.dma("sp", kt[0:64, :], KT[h * 64:(h + 1) * 64, :], [rd["KT"]], [KTh[k2]])
        P.dma("sp", qt[0:64, :], QT[h * 64:(h + 1) * 64, :], [rd["QT"]], [QTh[k2]])
        P.dma("sp", gt[0:64, :], GT[h * 64:(h + 1) * 64, :], [rd["GT"]], [GTh[k2]])
        va3 = VA3[k2]
        for p, d in enumerate(DILS):
            nsp = 16 // d
            if d == 1:
                src = bass.AP(VS.tensor, TC * 1024 + h * 64, [[1024, 128], [128 * 1024, 16], [1, 64]])
                P.dma("sp", va3[:, OOFF[p]:OOFF[p] + 16, 0:64], src, [rd["VS"]], [VA[k2]])
            else:
                for s_ in range(nsp):
                    src = bass.AP(VS.tensor, (TC + s_ * 128 * d) * 1024 + h * 64, [[d * 1024, 128], [1024, d], [1, 64]])
                    P.dma("sp", va3[:, OOFF[p] + s_ * d:OOFF[p] + (s_ + 1) * d, 0:64], src, [rd["VS"]], [VA[k2]])
            src = bass.AP(VS.tensor, (TC - 128 * d) * 1024 + h * 64, [[d * 1024, 128], [1024, d], [1, 64]])
            P.dma("sp", va3[:, HOFF[p]:HOFF[p] + d, 0:64], src, [rd["VS"]], [VA[k2]])

    head_loads(0)
    for h in range(NH):
        k2 = h % 2
        kt = KTh[k2].ap(BF16); qt = QTh[k2].ap(BF16); gt = GTh[k2].ap(BF16)
        va3 = VA3[k2]
        if h + 1 < NH:
            head_loads(h + 1)
        acc = ACC[k2].ap(F32)
        blks = []
        for p, d in enumerate(DILS):
            nsp = 16 // d
            blocks = [(s_, r_) for s_ in range(nsp) for r_ in range(d)]
            for g0 in range(0, 16, 4):
                for j in range(4):
                    blks.append((p, d, g0, j, blocks[g0 + j][0], blocks[g0 + j][1]))
        LA = 3
        pbs = {}
        pos = {}

        def stage_s(i):
            p, d, g0, j, sg_, r_ = blks[i]
            tcol = TT3[:, p * 16 + h, :]
            start = sg_ * 128 * d + r_
            qv = qt[0:64, start:start + 127 * d + 1:d] if d > 1 else qt[0:64, start:start + 128]
            kc0 = TC + start
            kp0 = TC + start - 128 * d
            kcur = kt[0:64, kc0:kc0 + 127 * d + 1:d] if d > 1 else kt[0:64, kc0:kc0 + 128]
            kprv = kt[0:64, kp0:kp0 + 127 * d + 1:d] if d > 1 else kt[0:64, kp0:kp0 + 128]
            ps = nextps()
            P.mm([(ps.ap[:, 0:128], kcur, qv, True, True), (ps.ap[:, 128:256], kprv, qv, True, True)],
                 [KTh[k2], QTh[k2]], [ps.r])
            eb = EB[ei[0] % NEB]; pb = PB[ei[0] % NEB]
            P.act(eb.ap(BF16), ps.ap[:, 0:256], AF.Exp, [ps.r], [eb], scale=SCALE)
            P.tt("pool" if ei[0] % 4 == 3 else "dve", pb.ap(BF16), eb.ap(BF16), tcol, MUL, [eb, TT], [pb])
            ei[0] += 1
            pbs[i] = pb

        def stage_v(i):
            p, d, g0, j, sg_, r_ = blks[i]
            if j == 0:
                pos[(p, g0)] = nextps()
            po = pos[(p, g0)]
            pb = pbs.pop(i)
            tcur = OOFF[p] + sg_ * d + r_
            tprv = (OOFF[p] + (sg_ - 1) * d + r_) if sg_ > 0 else (HOFF[p] + r_)
            P.mm([(po.ap[0:65, j * 128:(j + 1) * 128], va3[:, tcur, :], pb.ap(BF16)[:, 0:128], True, False),
                  (po.ap[0:65, j * 128:(j + 1) * 128], va3[:, tprv, :], pb.ap(BF16)[:, 128:256], False, True)],
                 [VA[k2], pb], [po.r])
            if j == 3:
                if d == 1:
                    outv = acc[0:65, g0 * 128:(g0 + 4) * 128]
                    P.copy("dve", outv, po.ap[0:65, :], [po.r], [ACC[k2]])
                elif d == 4:
                    outv = acc[0:65, (g0 // 4) * 512:(g0 // 4 + 1) * 512].rearrange("p (i j) -> p j i", j=4)
                    P.tt("dve", outv, outv, po.ap[0:65, :].rearrange("p (j i) -> p j i", j=4), ADD, [po.r, ACC[k2]], [ACC[k2]])
                else:
                    outv = acc[0:65, :].rearrange("p (i r) -> p r i", r=16)[:, g0:g0 + 4, :]
                    P.tt("dve", outv, outv, po.ap[0:65, :].rearrange("p (j i) -> p j i", j=4), ADD, [po.r, ACC[k2]], [ACC[k2]])

        for i in range(len(blks) + LA):
            if i < len(blks):
                stage_s(i)
            if i - LA >= 0:
                stage_v(i - LA)
        P.act(acc[64:65, :], acc[64:65, :], AF.Ln, [ACC[k2]], [ACC[k2]])
        P.act(acc[64:65, :], acc[64:65, :], AF.Exp, [ACC[k2]], [ACC[k2]], scale=-1.0)
        P.act(SG.ap(BF16)[0:64, :], gt[0:64, :], AF.Silu, [GTh[k2]], [SG])
        brh = BRh[k2]
        for n in range(4):
            ps = nextps()
            P.mm([(ps.ap[0:64, :], onf.ap(F32)[64:65, 0:64], acc[64:65, n * 512:(n + 1) * 512], True, True)], [onf, ACC[k2]], [ps.r])
            P.tt("dve", LN_.ap(F32)[0:64, n * 512:(n + 1) * 512], acc[0:64, n * 512:(n + 1) * 512], ps.ap[0:64, :], MUL,
                 [ps.r, ACC[k2]], [LN_.sub(n * 2048, (n + 1) * 2048)])
            P.tt("pool", brh.ap(BF16)[0:64, n * 512:(n + 1) * 512], LN_.ap(F32)[0:64, n * 512:(n + 1) * 512],
                 SG.ap(BF16)[0:64, n * 512:(n + 1) * 512], MUL, [LN_.sub(n * 2048, (n + 1) * 2048), SG], [brh.sub(n * 1024, (n + 1) * 1024)])
        P.dma("pool", BR[h * 64:(h + 1) * 64, :], brh.ap(BF16)[0:64, :], [brh], [rd["BR"]])


def out_phase(E):
    P, M, rd, banks = E["P"], E["M"], E["rd"], E["banks"]
    nextps = E["nextps"]
    BR, xo, y_o, xs, y_s = E["BR"], E["xo"], E["y_o"], E["xs"], E["y_s"]
    BRS = E["BRS"]
    MUL, ADD, SUB = ALU.mult, ALU.add, ALU.subtract
    Wo = M.alloc(16 * 2048 * 2); wo3 = Wo.ap(BF16).rearrange("p (k d) -> p k d", k=16)
    for half in range(2):
        P.dma("pool", wo3[:, half * 8:(half + 1) * 8, :], E["w_out"][half * 1024:(half + 1) * 1024, :].rearrange("(k p) d -> p k d", p=128),
              [rd["w_out"]], [Wo.sub(half * 32768, (half + 1) * 32768)])
    bob = M.alloc(2048 * 2)
    P.dma("pool", bob.ap(BF16)[0:1, :], E["b_out"][:, :], [rd["b_out"]], [bob])
    onb = M.alloc(128 * 2)
    P.memset("dve", onb.ap(BF16), 1.0, [onb])
    gB = M.alloc(2048 * 4); bB = M.alloc(2048 * 4)
    P.dma("sp", gB.ap(F32), bass.AP(E["ln_g"].tensor, 0, [[0, 128], [1, 2048]]), [rd["ln_g"]], [gB])
    P.dma("sp", bB.ap(F32), bass.AP(E["ln_b"].tensor, 0, [[0, 128], [1, 2048]]), [rd["ln_b"]], [bB])
    xst = [M.alloc(D * 4), M.alloc(D * 4)]
    brt = [M.alloc(16 * 128 * 2), M.alloc(16 * 128 * 2)]
    Vb = M.alloc(D * 4)
    Ob = [M.alloc(D * 4), M.alloc(D * 4)]
    stb = M.alloc(64 * 4)

    def do_tile(NT, br3, brbuf, xsrc, xsrc_r, ydst, ydst_r, ti):
        xb = xst[ti % 2]; xa = xb.ap(F32)
        if xsrc is not None:
            P.dma("sp", xa[0:NT, :], xsrc, [xsrc_r], [xb])
        va = Vb.ap(F32)
        for n in range(4):
            ps = nextps()
            items = [(ps.ap[0:NT, :], br3[:, ft, 0:NT], wo3[:, ft, n * 512:(n + 1) * 512], ft == 0, False) for ft in range(16)]
            items.append((ps.ap[0:NT, :], onb.ap(BF16)[0:1, 0:NT], bob.ap(BF16)[0:1, n * 512:(n + 1) * 512], False, True))
            P.mm(items, [brbuf, Wo, onb, bob], [ps.r])
            P.stt(va[0:NT, n * 512:(n + 1) * 512], xa[0:NT, n * 512:(n + 1) * 512], DN_ALPHA, ps.ap[0:NT, :], MUL, ADD,
                  [xb, ps.r], [Vb.sub(n * 2048, (n + 1) * 2048)])
        st = stb.ap(F32)
        for n in range(4):
            def fn(h, n=n):
                return h.bn_stats(out=st[0:NT, n * 6:(n + 1) * 6], in_=va[0:NT, n * 512:(n + 1) * 512])
            P.S.op("dve", fn, res_of([Vb]), res_of([stb]))

        def fn2(h):
            return h.bn_aggr(out=st[0:NT, 32:34], in_=st[0:NT, 0:24])
        P.S.op("dve", fn2, res_of([stb]), res_of([stb]))
        P.ts("dve", st[0:NT, 34:35], st[0:NT, 33:34], LN_EPS, None, ADD, None, [stb], [stb])
        P.act(st[0:NT, 35:36], st[0:NT, 34:35], AF.Sqrt, [stb], [stb])
        P.recip(st[0:NT, 36:37], st[0:NT, 35:36], [stb], [stb])
        P.stt(st[0:NT, 37:38], st[0:NT, 32:33], -1.0, st[0:NT, 36:37], MUL, MUL, [stb], [stb])
        ob = Ob[ti % 2]; oa = ob.ap(F32)
        P.act(oa[0:NT, :], va[0:NT, :], AF.Identity, [Vb, stb], [ob], scale=st[0:NT, 36:37], bias=st[0:NT, 37:38])
        P.tt("dve", oa[0:NT, :], oa[0:NT, :], gB.ap(F32)[0:NT, :], MUL, [ob, gB], [ob])
        P.tt("pool", oa[0:NT, 0:1024], oa[0:NT, 0:1024], bB.ap(F32)[0:NT, 0:1024], ADD, [ob.sub(0, 4096), bB], [ob.sub(0, 4096)])
        P.tt("dve", oa[0:NT, 1024:2048], oa[0:NT, 1024:2048], bB.ap(F32)[0:NT, 1024:2048], ADD, [ob.sub(4096, 8192), bB], [ob.sub(4096, 8192)])
        P.dma("pool", ydst, oa[0:NT, :], [ob], [ydst_r])

    def out_loads(tt_):
        bt = brt[tt_ % 2]
        bt3 = bt.ap(BF16).rearrange("p (f t) -> p f t", t=128)
        P.dma("sp", bt3, BR[:, tt_ * 128:(tt_ + 1) * 128].rearrange("(f p) t -> p f t", p=128), [rd["BR"]], [bt])
        xb = xst[tt_ % 2]
        P.dma("sp", xb.ap(F32), xo[tt_ * 128:(tt_ + 1) * 128, :], [rd["xo"]], [xb])

    out_loads(0)
    for tt_ in range(16):
        bt = brt[tt_ % 2]
        bt3 = bt.ap(BF16).rearrange("p (f t) -> p f t", t=128)
        if tt_ + 1 < 16:
            out_loads(tt_ + 1)
        do_tile(128, bt3, bt, None, rd["xo"], y_o[tt_ * 128:(tt_ + 1) * 128, :], rd["y_o"], tt_)
    if E.get("sample_attn_done"):
        brs3 = BRS.ap(BF16).rearrange("p (f t) -> p f t", t=16)
        do_tile(16, brs3, BRS, xs[:, :], rd["xs"], y_s[:, :], rd["y_s"], 16)


def sample_attn_phase(E):
    P, M, rd, banks = E["P"], E["M"], E["rd"], E["banks"]
    rot = [0]

    def nextps():
        b_ = banks[rot[0] % 6]
        rot[0] += 1
        return b_
    HS3, HS, VSs, BRS = E["HS3"], E["HS"], E["VSs"], E["BRS"]
    E32 = E["E32"]
    ck, cv, BRSD = E["ck"], E["cv"], E["BRSD"]
    MUL = ALU.mult
    cnt = M.alloc(32 * 128 * 4); cnt3 = cnt.ap(F32).rearrange("p (a k) -> p a k", k=128)
    P.dma("sp", cnt.ap(F32)[0:32, :], E["ccnt"][:, :], [rd["ccnt"]], [cnt])
    WS = M.alloc(32 * 16 * 4); ws3 = WS.ap(F32).rearrange("p (a h) -> p a h", h=16)
    ps = nextps()
    P.mm([(ps.ap[:, a * 16:(a + 1) * 16], cnt3[0:32, a, :], E32.ap(F32)[0:32, :], True, True) for a in range(32)], [cnt, E32], [ps.r])
    P.copy("dve", WS.ap(F32), ps.ap[:, :], [ps.r], [WS])
    wsv = ws3.rearrange("p (t s) h -> p t h s", s=4)
    HSb = M.alloc(16 * 16 * 2); hsb3 = HSb.ap(BF16).rearrange("p (f t) -> p f t", t=16)
    P.copy("dve", hsb3, HS3[:, 0:16, :], [HS], [HSb])
    onf = M.alloc(64 * 4)
    P.memset("dve", onf.ap(F32), 1.0, [onf])
    QP = M.alloc(16 * 16 * 2); qp3 = QP.ap(BF16).rearrange("p (h t) -> p h t", t=16)
    P.memset("dve", QP.ap(BF16), 0.0, [QP])
    P.copy("dve", qp3[0:64, 0:16:2, :], hsb3[0:64, 0:8, :], [HSb], [QP])
    P.copy("dve", qp3[64:128, 1:16:2, :], hsb3[64:128, 0:8, :], [HSb], [QP])
    STOP = int(os.environ.get("KSA_STOP", "99"))
    if STOP <= 1:
        return
    KC = [M.alloc(1024 * 4), M.alloc(1024 * 4)]
    VC = [M.alloc(1024 * 4), M.alloc(1024 * 4)]
    VAs = M.alloc(8 * 16 * 65 * 2); vas4 = VAs.ap(BF16).rearrange("p (t h c) -> p t h c", h=16, c=65)
    P.memset("pool", VAs.ap(BF16), 1.0, [VAs])
    KTs = [M.alloc(8 * 128 * 2), M.alloc(8 * 128 * 2)]
    EBs = M.alloc(512 * 2); PMs = M.alloc(512 * 2)
    eb4 = EBs.ap(BF16).rearrange("p (t h s) -> p t h s", h=16, s=4); pm4 = PMs.ap(BF16).rearrange("p (t h s) -> p t h s", h=16, s=4)
    NUM = M.alloc(64 * 4); AT = M.alloc(64 * 4)
    identF, identb = E["identF"], E["identb"]
    li = [0]
    for b in range(4):
        sp = banks[6]
        sp4 = sp.ap.rearrange("p (t h s) -> p t h s", h=16, s=4)
        for tile in range(7):
            kc = KC[li[0] % 2]; vc = VC[li[0] % 2]; kts = KTs[li[0] % 2]
            li[0] += 1
            if tile < 4:
                r0 = 1536 + 128 * tile
                P.dma("sp", kc.ap(F32), ck[b, r0:r0 + 128, :], [rd["ck"]], [kc])
                P.dma("act", vc.ap(F32), cv[b, r0:r0 + 128, :], [rd["cv"]], [vc])
            else:
                u = tile - 4
                for sr in range(4):
                    off = b * CL * 1024 + (16 * 32 * u + sr) * 1024
                    P.dma("sp", kc.ap(F32)[sr * 32:(sr + 1) * 32, :], bass.AP(ck.tensor, off, [[16 * 1024, 32], [1, 1024]]), [rd["ck"]], [kc])
                    P.dma("act", vc.ap(F32)[sr * 32:(sr + 1) * 32, :], bass.AP(cv.tensor, off, [[16 * 1024, 32], [1, 1024]]), [rd["cv"]], [vc])
            P.copy("pool", vas4[:, tile, :, 0:64], vc.ap(F32).rearrange("p (h c) -> p h c", c=64), [vc], [VAs])
            kt3 = kts.ap(BF16).rearrange("p (f k) -> p f k", k=128)
            for half in range(2):
                pt = nextps()
                P.tr([(pt.ap[:, j * 128:(j + 1) * 128], kc.ap(F32)[:, (half * 4 + j) * 128:(half * 4 + j + 1) * 128], identF) for j in range(4)],
                     [kc, identb], [pt.r])
                P.copy("act" if half == 0 else "dve", kt3[:, half * 4:half * 4 + 4, :], pt.ap.rearrange("p (a k) -> p a k", a=4), [pt.r],
                       [kts.sub(half * 1024, (half + 1) * 1024)])
            if os.environ.get("KSA_VAR") == "a":
                continue
            P.mm([(sp4[:, tile, h, :], kt3[:, h // 2, :], qp3[:, h, b * 4:(b + 1) * 4], True, True)
                  for h in range(NH)], [kts, QP], [sp.r])
        if os.environ.get("KSA_VAR") not in ("a", "b"):
            P.mm([(sp4[0:4, 7, h, :], hsb3[:, 8 + h // 2, b * 4:(b + 1) * 4], qp3[:, h, b * 4:(b + 1) * 4], True, True)
                  for h in range(NH)], [HSb, QP], [sp.r])
        if STOP <= 2:
            continue
        P.dma("sp", vas4[0:4, 7, :, 0:64], VSs.ap(BF16)[b * 4:(b + 1) * 4, :].rearrange("p (h c) -> p h c", c=64), [VSs], [VAs])
        if STOP <= 3:
            continue
        P.act(eb4[:, 0:7, :, :], sp4[:, 0:7, :, :], AF.Exp, [sp.r], [EBs], scale=SCALE)
        P.act(eb4[0:4, 7, :, :], sp4[0:4, 7, :, :], AF.Exp, [sp.r], [EBs], scale=SCALE)
        P.tt("dve", pm4[:, 0:7, :, :], eb4[:, 0:7, :, :], wsv[:, 0:7, :, :], MUL, [EBs, WS], [PMs])
        P.tt("dve", pm4[0:4, 7, :, :], eb4[0:4, 7, :, :], wsv[0:4, 7, :, :], MUL, [EBs, WS], [PMs])
        if STOP <= 4:
            continue
        po = banks[7]
        items = []
        for h in range(NH):
            for tile in range(7):
                items.append((po.ap[0:65, h * 4:(h + 1) * 4], vas4[:, tile, h, :], pm4[:, tile, h, :], tile == 0, False))
            items.append((po.ap[0:65, h * 4:(h + 1) * 4], vas4[0:4, 7, h, :], pm4[0:4, 7, h, :], False, True))
        P.mm(items, [VAs, PMs], [po.r])
        if STOP <= 5:
            continue
        num = NUM.ap(F32)
        P.copy("dve", num[0:65, :], po.ap[0:65, 0:64], [po.r], [NUM])
        P.act(num[64:65, :], num[64:65, :], AF.Ln, [NUM], [NUM])
        P.act(num[64:65, :], num[64:65, :], AF.Exp, [NUM], [NUM], scale=-1.0)
        pb_ = nextps()
        P.mm([(pb_.ap[0:64, 0:64], onf.ap(F32)[64:65, 0:64], num[64:65, :], True, True)], [onf, NUM], [pb_.r])
        P.tt("dve", AT.ap(F32)[0:64, :], num[0:64, :], pb_.ap[0:64, 0:64], MUL, [pb_.r, NUM], [AT])
        P.dma("sp", bass.AP(BRSD.tensor, b * 4, [[16, 64], [64 * 16, 16], [1, 4]]),
              AT.ap(F32)[0:64, :].rearrange("p (h s) -> p h s", s=4), [AT], [rd["BRSD"]], slow=True)
    if STOP <= 6:
        return
    atb = M.alloc(8 * 16 * 4); sgb = M.alloc(8 * 16 * 4)
    at3 = atb.ap(F32).rearrange("p (f t) -> p f t", t=16); sg3 = sgb.ap(F32).rearrange("p (f t) -> p f t", t=16)
    P.dma("sp", at3, BRSD.rearrange("(f p) t -> p f t", p=128), [rd["BRSD"]], [atb], slow=True)
    P.act(sg3, HS3[:, 24:32, :], AF.Silu, [HS], [sgb])
    brs3 = BRS.ap(BF16).rearrange("p (f t) -> p f t", t=16)
    P.tt("dve", brs3[:, 0:8, :], at3, sg3, MUL, [atb, sgb], [BRS])
    E["sample_attn_done"] = True
```
